# Optimizing a Trainium2 kernel written in Bass

```python
import math
import jax
import jax.numpy as jnp
from jax import lax
import numpy as np

D_MODEL = 1024
BATCH = 8
SEQ = 2048
DEPTH = 4
DEC_BATCH = 16
DEC_SEQ = 4096
PAST_LEN = 128

N_MIXERS = 3
N_A = (DEPTH + 2) // 3
N_B = (DEPTH + 1) // 3
N_C = DEPTH // 3

EPS = 1e-6
D_FF = 2816

A_HEADS = 8
A_DK = 128
A_DV = 128
A_QK = A_HEADS * A_DK
A_V = A_HEADS * A_DV
A_CONV = 3
A_CHUNK = 64
A_PROJ = 2 * A_QK + 2 * A_V + 4 * A_HEADS

B_HEADS = 4
B_DK = 256
B_DV = 512
B_CHUNK = 128
ROPE_BASE = 10000.0
B_PROJ = 2 * B_HEADS * B_DK + 2 * B_HEADS * B_DV

C_ORDER = 2
C_CONV = 3
C_EMB = 33
C_FILTER_WIDTH = 64
C_TARGET = 1e-2
C_SHORT_DECAY_PCT = 0.3
C_LONG_DECAY_PCT = 1.5
C_MIN_DECAY = math.log(C_TARGET) / C_LONG_DECAY_PCT
C_MAX_DECAY = math.log(C_TARGET) / C_SHORT_DECAY_PCT

kernel_name = "hybrid_bidir_deltanet_retnet_hyena_encoder"

F32 = jnp.float32


def rmsnorm(x, g):
    xf = x.astype(F32)
    y = xf * lax.rsqrt(jnp.mean(xf * xf, axis=-1, keepdims=True) + EPS)
    return (y * g.astype(F32)).astype(x.dtype)


def l2norm(x):
    return x * lax.rsqrt(jnp.sum(x * x, axis=-1, keepdims=True) + EPS)


def short_conv(x, w):
    k = w.shape[0]
    return lax.conv_general_dilated(
        x, w[:, None, :].astype(x.dtype), window_strides=(1,), padding=[(k // 2, k // 2)],
        dimension_numbers=("NWC", "WIO", "NWC"), feature_group_count=x.shape[-1])


def swiglu(x, wg, wu, wd):
    return (jax.nn.silu(x @ wg) * (x @ wu)) @ wd


def flip_seq(t):
    return jnp.flip(t, axis=2)


def gated_delta_rule(q, k, v, g, beta):
    bsz, nh, L, dk = k.shape
    dv = v.shape[-1]
    c = A_CHUNK
    n = L // c
    q, k, v = (t.reshape(bsz, nh, n, c, t.shape[-1]) for t in (q, k, v))
    g = jnp.cumsum(g.reshape(bsz, nh, n, c), axis=-1)
    beta = beta.reshape(bsz, nh, n, c)
    causal = jnp.tril(jnp.ones((c, c), dtype=bool))
    strict = jnp.tril(jnp.ones((c, c), dtype=bool), -1)
    diff = g[..., :, None] - g[..., None, :]
    decay = jnp.where(causal, jnp.exp(jnp.where(causal, diff, 0.0)), 0.0)
    k_beta = k * beta[..., None]
    a = jnp.where(strict, jnp.einsum("bhncd,bhnmd->bhncm", k_beta, k) * decay, 0.0)
    eye = jnp.eye(c, dtype=F32)
    t_inv = lax.linalg.triangular_solve(a + eye, jnp.broadcast_to(eye, a.shape), left_side=True,
                                        lower=True, unit_diagonal=True)
    u = jnp.einsum("bhncm,bhnme->bhnce", t_inv, v * beta[..., None])
    w = jnp.einsum("bhncm,bhnmd->bhncd", t_inv, k_beta * jnp.exp(g)[..., None])
    attn = jnp.where(causal, jnp.einsum("bhncd,bhnmd->bhncm", q, k) * decay, 0.0)
    q_dec = q * jnp.exp(g)[..., None]
    k_dec = k * jnp.exp(g[..., -1:] - g)[..., None]
    g_end = jnp.exp(g[..., -1])
    xs = tuple(jnp.moveaxis(t, 2, 0) for t in (u, w, attn, q_dec, k_dec, g_end))

    def step(s, inp):
        u_c, w_c, attn_c, qd_c, kd_c, ge_c = inp
        v_new = u_c - jnp.einsum("bhcd,bhde->bhce", w_c, s)
        o_c = jnp.einsum("bhcd,bhde->bhce", qd_c, s) + jnp.einsum("bhcm,bhme->bhce", attn_c, v_new)
        s = s * ge_c[..., None, None] + jnp.einsum("bhcd,bhce->bhde", kd_c, v_new)
        return s, o_c

    s0 = jnp.zeros((bsz, nh, dk, dv), F32)
    _, o = lax.scan(step, s0, xs)
    return jnp.moveaxis(o, 0, 2).reshape(bsz, nh, L, dv)


def gdn_mixer(x, w_in, conv_w, a_log, dt_bias, norm_w, w_out):
    bsz, L, _ = x.shape
    p = x @ w_in
    n_qkv = 2 * A_QK + A_V
    qkv = jax.nn.silu(short_conv(p[..., :n_qkv], conv_w)).astype(F32)
    z = p[..., n_qkv:n_qkv + A_V].astype(F32).reshape(bsz, L, A_HEADS, A_DV)
    ab = p[..., n_qkv + A_V:].astype(F32).reshape(bsz, L, 2, 2, A_HEADS)

    def heads(t, d):
        return t.reshape(bsz, L, A_HEADS, d).transpose(0, 2, 1, 3)

    q = l2norm(heads(qkv[..., :A_QK], A_DK)) * (A_DK ** -0.5)
    k = l2norm(heads(qkv[..., A_QK:2 * A_QK], A_DK))
    v = heads(qkv[..., 2 * A_QK:], A_DV)
    g = -jnp.exp(a_log.astype(F32)) * jax.nn.softplus(ab[:, :, :, 0] + dt_bias.astype(F32))
    beta = jax.nn.sigmoid(ab[:, :, :, 1])
    g = g.transpose(2, 0, 3, 1)
    beta = beta.transpose(2, 0, 3, 1)
    o_fwd = gated_delta_rule(q, k, v, g[0], beta[0])
    o_bwd = flip_seq(gated_delta_rule(flip_seq(q), flip_seq(k), flip_seq(v), flip_seq(g[1]), flip_seq(beta[1])))
    o = (o_fwd + o_bwd).transpose(0, 2, 1, 3)
    o = o * lax.rsqrt(jnp.mean(o * o, axis=-1, keepdims=True) + EPS) * norm_w.astype(F32) * jax.nn.silu(z)
    return o.reshape(bsz, L, A_V).astype(x.dtype) @ w_out


def rope(x):
    L, d = x.shape[2], x.shape[3]
    inv = ROPE_BASE ** (-jnp.arange(0, d, 2, dtype=F32) / d)
    ang = jnp.arange(L, dtype=F32)[:, None] * inv[None, :]
    cos, sin = jnp.cos(ang), jnp.sin(ang)
    x1, x2 = x[..., :d // 2], x[..., d // 2:]
    return jnp.concatenate([x1 * cos - x2 * sin, x2 * cos + x1 * sin], axis=-1)


def retention_chunked(q, k, v, log_gamma):
    bsz, nh, L, dk = q.shape
    dv = v.shape[-1]
    c = B_CHUNK
    n = L // c
    pos = jnp.arange(c, dtype=F32)
    dist = pos[:, None] - pos[None, :]
    dmask = jnp.where(dist >= 0, jnp.exp(log_gamma[:, None, None] * jnp.maximum(dist, 0.0)), 0.0)
    zeta = jnp.exp(log_gamma[:, None] * (c - 1 - pos))
    xi = jnp.exp(log_gamma[:, None] * (pos + 1))
    g_chunk = jnp.exp(log_gamma * c)
    q, k, v = (t.reshape(bsz, nh, n, c, t.shape[-1]) for t in (q, k, v))
    scores = jnp.einsum("bhncd,bhnmd->bhncm", q, k) * dmask[None, :, None]
    inner = jnp.einsum("bhncm,bhnme->bhnce", scores, v)
    qx = q * xi[None, :, None, :, None]
    kz = k * zeta[None, :, None, :, None]
    xs = (jnp.moveaxis(qx, 2, 0), jnp.moveaxis(kz, 2, 0), jnp.moveaxis(v, 2, 0))

    def step(r, inp):
        qx_c, kz_c, v_c = inp
        cross = jnp.einsum("bhcd,bhde->bhce", qx_c, r)
        r = r * g_chunk[None, :, None, None] + jnp.einsum("bhcd,bhce->bhde", kz_c, v_c)
        return r, cross

    r0 = jnp.zeros((bsz, nh, dk, dv), F32)
    _, cross = lax.scan(step, r0, xs)
    return (inner + jnp.moveaxis(cross, 0, 2)).reshape(bsz, nh, L, dv)


def retnet_mixer(x, w_in, decay_logit, gn_w, w_out):
    bsz, L, _ = x.shape
    p = x @ w_in
    nq = B_HEADS * B_DK
    nv = B_HEADS * B_DV

    def heads(t, d):
        return t.astype(F32).reshape(bsz, L, B_HEADS, d).transpose(0, 2, 1, 3)

    q = rope(heads(p[..., :nq], B_DK))
    k = rope(heads(p[..., nq:2 * nq], B_DK)) * (B_DK ** -0.5)
    v = heads(p[..., 2 * nq:2 * nq + nv], B_DV)
    gate = p[..., 2 * nq + nv:].astype(F32)
    log_gamma = jnp.log1p(-jnp.exp2(decay_logit.astype(F32)))
    o = retention_chunked(q, k, v, log_gamma[0]) + flip_seq(
        retention_chunked(flip_seq(q), flip_seq(k), flip_seq(v), log_gamma[1]))
    o = o.transpose(0, 2, 1, 3)
    mu = jnp.mean(o, axis=-1, keepdims=True)
    oc = o - mu
    on = oc * lax.rsqrt(jnp.mean(oc * oc, axis=-1, keepdims=True) + EPS)
    on = on.reshape(bsz, L, nv) * gn_w.astype(F32)
    return (jax.nn.silu(gate) * on).astype(x.dtype) @ w_out


def hyena_filters(L, w1, b1, fr1, w2, b2, fr2, w3):
    bands = (C_EMB - 1) // 2
    t = jnp.linspace(0.0, 1.0, L, dtype=F32)[:, None]
    omega = 2.0 * math.pi * jnp.arange(L, dtype=F32)[:, None] / L
    f = jnp.linspace(1e-4, bands - 1, bands, dtype=F32)[None, :]
    z = jnp.concatenate([t, jnp.cos(f * omega), -jnp.sin(f * omega)], axis=-1)
    h = jnp.sin(fr1.astype(F32) * (z @ w1.astype(F32) + b1.astype(F32)))
    h = jnp.sin(fr2.astype(F32) * (h @ w2.astype(F32) + b2.astype(F32)))
    h = h @ w3.astype(F32)
    deltas = jnp.abs(jnp.linspace(C_MIN_DECAY, C_MAX_DECAY, D_MODEL, dtype=F32))
    window = jnp.exp(-t * deltas[None, :])
    return h.reshape(L, 2, C_ORDER, D_MODEL) * window[:, None, None, :]


def bidir_long_conv(u, h_fwd, h_bwd, skip):
    L, d = h_fwd.shape
    filt = jnp.concatenate([h_fwd, jnp.zeros((1, d), F32), jnp.flip(h_bwd[1:], axis=0)], axis=0)
    filt_f = jnp.fft.rfft(filt, axis=0)
    u_f = jnp.fft.rfft(u, n=2 * L, axis=1)
    y = jnp.fft.irfft(u_f * filt_f[None], n=2 * L, axis=1)[:, :L]
    return y + u * skip


def hyena_mixer(x, w_in, b_in, conv_w, conv_b, f_w1, f_b1, f_fr1, f_w2, f_b2, f_fr2, f_w3, bias_d, w_out, b_out):
    L = x.shape[1]
    p = (short_conv(x @ w_in + b_in, conv_w) + conv_b).astype(F32)
    v, g1, g2 = p[..., :D_MODEL], p[..., D_MODEL:2 * D_MODEL], p[..., 2 * D_MODEL:]
    h = hyena_filters(L, f_w1, f_b1, f_fr1, f_w2, f_b2, f_fr2, f_w3)
    z = v
    for n, gate in enumerate((g1, g2)):
        z = gate * bidir_long_conv(z, h[:, 0, n], h[:, 1, n], bias_d[n].astype(F32))
    return z.astype(x.dtype) @ w_out + b_out


def setup_inputs(seed: int = 0) -> dict:
    key = jax.random.key(seed)
    ks = iter(jax.random.split(key, 40))

    def nrm(shape, scale):
        return jax.random.normal(next(ks), shape, F32) * scale

    def gain(shape):
        return 1.0 + nrm(shape, 0.05)

    x_prompt = nrm((BATCH, SEQ, D_MODEL), 1.0)
    x_sample = nrm((DEC_BATCH, DEC_SEQ, D_MODEL), 1.0)
    norm_g = gain((DEPTH, 6, D_MODEL))
    ffn_w_gate = nrm((DEPTH, 2, D_MODEL, D_FF), D_MODEL ** -0.5)
    ffn_w_up = nrm((DEPTH, 2, D_MODEL, D_FF), D_MODEL ** -0.5)
    ffn_w_down = nrm((DEPTH, 2, D_FF, D_MODEL), D_FF ** -0.5)
    a_w_in = nrm((N_A, D_MODEL, A_PROJ), D_MODEL ** -0.5)
    a_conv_w = nrm((N_A, A_CONV, 2 * A_QK + A_V), A_CONV ** -0.5)
    a_a_log = jnp.log(jax.random.uniform(next(ks), (N_A, 2, A_HEADS), F32, 1.0, 16.0))
    dt = jnp.exp(jax.random.uniform(next(ks), (N_A, 2, A_HEADS), F32, math.log(1e-3), math.log(1e-1)))
    a_dt_bias = dt + jnp.log(-jnp.expm1(-dt))
    a_norm_w = gain((N_A, A_DV))
    a_w_out = nrm((N_A, A_V, D_MODEL), A_V ** -0.5)
    b_w_in = nrm((N_B, D_MODEL, B_PROJ), D_MODEL ** -0.5)
    b_decay_logit = -(5.0 + jnp.arange(B_HEADS, dtype=F32))[None, None, :] + nrm((N_B, 2, B_HEADS), 0.1)
    b_gn_w = gain((N_B, B_HEADS * B_DV))
    b_w_out = nrm((N_B, B_HEADS * B_DV, D_MODEL), (B_HEADS * B_DV) ** -0.5)
    c_w_in = nrm((N_C, D_MODEL, 3 * D_MODEL), D_MODEL ** -0.5)
    c_b_in = nrm((N_C, 3 * D_MODEL), 0.02)
    c_conv_w = nrm((N_C, C_CONV, 3 * D_MODEL), C_CONV ** -0.5)
    c_conv_b = nrm((N_C, 3 * D_MODEL), 0.02)
    c_f_w1 = nrm((N_C, C_EMB, C_FILTER_WIDTH), C_EMB ** -0.5)
    c_f_b1 = nrm((N_C, C_FILTER_WIDTH), 0.1)
    c_f_freq1 = gain((N_C, C_FILTER_WIDTH))
    c_f_w2 = nrm((N_C, C_FILTER_WIDTH, C_FILTER_WIDTH), C_FILTER_WIDTH ** -0.5)
    c_f_b2 = nrm((N_C, C_FILTER_WIDTH), 0.1)
    c_f_freq2 = gain((N_C, C_FILTER_WIDTH))
    c_f_w3 = nrm((N_C, C_FILTER_WIDTH, 2 * C_ORDER * D_MODEL), C_FILTER_WIDTH ** -0.5)
    c_bias_d = nrm((N_C, C_ORDER, D_MODEL), 1.0)
    c_w_out = nrm((N_C, D_MODEL, D_MODEL), D_MODEL ** -0.5)
    c_b_out = nrm((N_C, D_MODEL), 0.02)
    return {
        "x_prompt": x_prompt, "x_sample": x_sample, "norm_g": norm_g,
        "ffn_w_gate": ffn_w_gate, "ffn_w_up": ffn_w_up, "ffn_w_down": ffn_w_down,
        "a_w_in": a_w_in, "a_conv_w": a_conv_w, "a_a_log": a_a_log, "a_dt_bias": a_dt_bias,
        "a_norm_w": a_norm_w, "a_w_out": a_w_out,
        "b_w_in": b_w_in, "b_decay_logit": b_decay_logit, "b_gn_w": b_gn_w, "b_w_out": b_w_out,
        "c_w_in": c_w_in, "c_b_in": c_b_in, "c_conv_w": c_conv_w, "c_conv_b": c_conv_b,
        "c_f_w1": c_f_w1, "c_f_b1": c_f_b1, "c_f_freq1": c_f_freq1,
        "c_f_w2": c_f_w2, "c_f_b2": c_f_b2, "c_f_freq2": c_f_freq2, "c_f_w3": c_f_w3,
        "c_bias_d": c_bias_d, "c_w_out": c_w_out, "c_b_out": c_b_out,
    }


def reference(x_prompt, x_sample, norm_g, ffn_w_gate, ffn_w_up, ffn_w_down,
              a_w_in, a_conv_w, a_a_log, a_dt_bias, a_norm_w, a_w_out,
              b_w_in, b_decay_logit, b_gn_w, b_w_out,
              c_w_in, c_b_in, c_conv_w, c_conv_b, c_f_w1, c_f_b1, c_f_freq1,
              c_f_w2, c_f_b2, c_f_freq2, c_f_w3, c_bias_d, c_w_out, c_b_out):

    def token_mixer(h, i):
        kind, j = i % N_MIXERS, i // N_MIXERS
        if kind == 0:
            return gdn_mixer(h, a_w_in[j], a_conv_w[j], a_a_log[j], a_dt_bias[j], a_norm_w[j], a_w_out[j])
        if kind == 1:
            return retnet_mixer(h, b_w_in[j], b_decay_logit[j], b_gn_w[j], b_w_out[j])
        return hyena_mixer(h, c_w_in[j], c_b_in[j], c_conv_w[j], c_conv_b[j], c_f_w1[j], c_f_b1[j],
                           c_f_freq1[j], c_f_w2[j], c_f_b2[j], c_f_freq2[j], c_f_w3[j], c_bias_d[j],
                           c_w_out[j], c_b_out[j])

    def trunk(x):
        for i in range(DEPTH):
            g = norm_g[i]
            h = swiglu(rmsnorm(x, g[0]), ffn_w_gate[i, 0], ffn_w_up[i, 0], ffn_w_down[i, 0])
            x = x + 0.5 * rmsnorm(h, g[1])
            h = token_mixer(rmsnorm(x, g[2]), i)
            x = x + rmsnorm(h, g[3])
            h = swiglu(rmsnorm(x, g[4]), ffn_w_gate[i, 1], ffn_w_up[i, 1], ffn_w_down[i, 1])
            x = x + 0.5 * rmsnorm(h, g[5])
        return x

    y_prompt = trunk(x_prompt)
    y_sample = trunk(x_sample)
    return (y_prompt, y_sample)
```

```python
import contextlib
import math
import numpy as np
import concourse.bass as bass
import concourse.mybir as mybir
from concourse.bass_utils import run_bass_kernel_spmd

F32 = mybir.dt.float32
BF16 = mybir.dt.bfloat16
AF = mybir.ActivationFunctionType
ALU = mybir.AluOpType

D = 1024
DFF = 2816
NFF = DFF // 128
EPS = 1e-6
N_CORES = 8
DBG = 99
OWN_RAW = True
GDN_SKIP_B = False
ADD_ENG = "dve"


class Res:
    __slots__ = ("w", "r", "psum")

    def __init__(self):
        self.w = None
        self.r = {}
        self.psum = False


class Tile:
    def __init__(self, t, nres=1):
        self.t = t
        self.rs = [Res() for _ in range(nres)]

    @property
    def r(self):
        return self.rs[0]


class KB:
    def __init__(self, nc):
        self.nc = nc
        self.st = contextlib.ExitStack()
        self.engs = {"pe": nc.tensor, "act": nc.scalar, "dve": nc.vector, "pool": nc.gpsimd, "sp": nc.sync}
        self.semh = {}
        self.cnt = {}
        self.seen = {e: {} for e in self.engs}
        for e in self.engs:
            self.semh[e] = self.st.enter_context(nc.semaphore("s_" + e))
            self.cnt[e] = 0
        self.dslots = {"sp": 24, "pool": 2, "act": 8}
        self.dnext = {q: 0 for q in self.dslots}
        self.dtot = {}
        for q, n in self.dslots.items():
            for i in range(n):
                sk = ("d", q, i)
                self.semh[sk] = self.st.enter_context(nc.semaphore("d_%s_%d" % (q, i)))
                self.dtot[sk] = 0
        self.dram_res = {}
        self.ninst = 0
        self.uid = 0

    def dres(self, name, a, b, blk=128):
        out = []
        for i in range(a // blk, (b + blk - 1) // blk):
            key = (name, i)
            if key not in self.dram_res:
                self.dram_res[key] = Res()
            out.append(self.dram_res[key])
        return out

    def _need(self, e, reads, writes, own_raw=True):
        need = {}

        def add(sk, v):
            if need.get(sk, 0) < v:
                need[sk] = v

        for r in reads:
            if r.w is not None:
                add(*r.w)
        for r in writes:
            if r.w is not None:
                add(*r.w)
            for sk, v in r.r.items():
                if sk == e:
                    continue
                add(sk, v)
        if not own_raw:
            need.pop(e, None)
        return need

    def _wait(self, e, need):
        eng = self.engs[e]
        seen = self.seen[e]
        for sk, v in need.items():
            if seen.get(sk, 0) < v:
                eng.wait_ge(self.semh[sk], v)
                seen[sk] = v
                self.ninst += 1

    def op(self, e, fn, reads=(), writes=()):
        need = self._need(e, reads, writes, own_raw=(e != "pe") and (OWN_RAW or e == "pool"))
        self._wait(e, need)
        ins = fn(self.engs[e])
        self.cnt[e] += 1
        c = self.cnt[e]
        ins.then_inc(self.semh[e], 1)
        self.ninst += 1
        for r in reads:
            if r.psum:
                assert all(k == e for k in r.r), ("two engines read one PSUM tile in the same epoch", e, list(r.r))
            r.r[e] = c
        for r in writes:
            r.w = (e, c)
            r.r = {}
        return c

    def dma(self, q, out, in_, reads=(), writes=()):
        i = self.dnext[q]
        self.dnext[q] = (i + 1) % self.dslots[q]
        sk = ("d", q, i)
        tot = self.dtot[sk]
        need = self._need(q, reads, writes)
        if need.get(sk, 0) < tot:
            need[sk] = tot
        self._wait(q, need)
        self.engs[q].dma_start(out=out, in_=in_).then_inc(self.semh[sk], 16)
        self.ninst += 1
        tot += 16
        self.dtot[sk] = tot
        for r in reads:
            r.r[sk] = tot
        for r in writes:
            r.w = (sk, tot)
            r.r = {}

    def barrier(self):
        cur = {}
        for e in self.engs:
            cur[e] = self.cnt[e]
        for sk, v in self.dtot.items():
            cur[sk] = v
        for e in self.engs:
            need = {sk: v for sk, v in cur.items() if sk != e and v > 0}
            self._wait(e, need)

    def sb(self, st, name, shape, dt, nres=1):
        self.uid += 1
        return Tile(st.enter_context(self.nc.sbuf_tensor("%s_%d" % (name, self.uid), list(shape), dt)), nres)

    def ps(self, st, name, shape, dt=F32, nres=1):
        self.uid += 1
        t = Tile(st.enter_context(self.nc.psum_tensor("%s_%d" % (name, self.uid), list(shape), dt)), nres)
        for r in t.rs:
            r.psum = True
        return t


def load_consts(kb, st, P):
    c = {}
    c["ones_bf"] = kb.sb(st, "ones_bf", [128, 128], BF16)
    kb.op("dve", lambda e: e.memset(c["ones_bf"].t[:], 1.0), writes=[c["ones_bf"].r])
    c["ng"] = kb.sb(st, "ng", [128, 4 * 6 * 8], F32)
    kb.dma("sp", c["ng"].t[:], P["norm_g"][:, :], writes=[c["ng"].r])
    c["ngh"] = kb.sb(st, "ngh", [128, 4 * 6 * 8], F32)
    kb.op("dve", lambda e: e.tensor_scalar(out=c["ngh"].t[:], in0=c["ng"].t[:], scalar1=0.5, scalar2=None,
                                           op0=ALU.mult), reads=[c["ng"].r], writes=[c["ngh"].r])
    c["eps"] = kb.sb(st, "eps", [128, 1], F32)
    kb.op("dve", lambda e: e.memset(c["eps"].t[:], EPS), writes=[c["eps"].r])
    c["one"] = kb.sb(st, "one", [128, 1], F32)
    kb.op("dve", lambda e: e.memset(c["one"].t[:], 1.0), writes=[c["one"].r])
    c["ident_f"] = kb.sb(st, "ident_f", [128, 128], F32)
    kb.dma("sp", c["ident_f"].t[:], P["ident"][:, :], writes=[c["ident_f"].r])
    c["ident_bf"] = kb.sb(st, "ident_bf", [128, 128], BF16)
    kb.op("dve", lambda e: e.tensor_copy(out=c["ident_bf"].t[:], in_=c["ident_f"].t[:]), reads=[c["ident_f"].r], writes=[c["ident_bf"].r])
    return c


def gcol(c, which, l, j, ch):
    idx = (l * 6 + j) * 8 + ch
    return c[which].t[:, idx:idx + 1]


def rstd_from_sumsq(kb, ps_ss, rt, scratch, n, cst, scale):
    kb.op("act", lambda e: e.activation(out=scratch.t[:, :n], in_=ps_ss.t[:, :n], func=AF.Sqrt,
                                        bias=cst["eps"].t[:, 0:1], scale=scale),
          reads=[ps_ss.r, cst["eps"].r], writes=[scratch.r])
    kb.op("dve", lambda e: e.reciprocal(out=rt.t[:, :n], in_=scratch.t[:, :n]), reads=[scratch.r], writes=[rt.r])


class WLoader:
    def __init__(self, kb, st, width, nbuf=3):
        self.kb = kb
        self.stg = [kb.sb(st, "wstg%d" % i, [128, width], F32) for i in range(nbuf)]
        self.i = 0
        self.engs = ["act", "dve", "pool"]

    def load(self, dst, src, res, width, rows=128):
        kb = self.kb
        k = self.i
        self.i += 1
        stg = self.stg[k % len(self.stg)]
        eng = self.engs[k % len(self.engs)]
        kb.dma("sp", stg.t[:rows, :width], src, writes=[stg.r])
        if eng == "act":
            kb.op("act", lambda e: e.activation(out=dst, in_=stg.t[:rows, :width], func=AF.Copy), reads=[stg.r], writes=[res])
        else:
            kb.op(eng, lambda e: e.tensor_copy(out=dst, in_=stg.t[:rows, :width]), reads=[stg.r], writes=[res])


def ffn_stage(kb, cst, P, Xin, X, seqs, layer, j):
    nc = kb.nc
    T = 256
    with contextlib.ExitStack() as st:
        wg = kb.sb(st, "wg", [128, 8, DFF], BF16, nres=8)
        wu = kb.sb(st, "wu", [128, 8, DFF], BF16, nres=8)
        wd = kb.sb(st, "wd", [128, NFF, D], BF16, nres=NFF)
        xt = kb.sb(st, "xt", [128, 8, T], F32)
        ht = kb.sb(st, "ht", [128, 8, T], BF16)
        at = kb.sb(st, "at", [128, NFF, T], BF16, nres=NFF)
        ot = kb.sb(st, "ot", [128, 8, T], F32, nres=8)
        sq = kb.sb(st, "sq", [128, 8, T], BF16, nres=8)
        sg = [kb.sb(st, "sg%d" % i, [128, T], F32) for i in range(2)]
        rt = kb.sb(st, "rt", [128, T], F32)
        rs = kb.sb(st, "rs", [128, T], F32)
        tmp = [kb.sb(st, "tmp%d" % i, [128, T], F32) for i in range(2)]
        psg = [kb.ps(st, "psg%d" % i, [128, 512]) for i in range(2)]
        psu = [kb.ps(st, "psu%d" % i, [128, 512]) for i in range(2)]
        pso = [kb.ps(st, "pso%d" % i, [128, 512]) for i in range(2)]
        pss = kb.ps(st, "pss", [128, 512])

        Wg = P["ffn_w_gate"][layer * 2 + j]
        Wu = P["ffn_w_up"][layer * 2 + j]
        Wd = P["ffn_w_down"][layer * 2 + j]
        wl = WLoader(kb, st, DFF, nbuf=2)
        for c in range(8):
            wl.load(wg.t[:, c, :], Wg[c * 128:(c + 1) * 128, :], wg.rs[c], DFF)
            wl.load(wu.t[:, c, :], Wu[c * 128:(c + 1) * 128, :], wu.rs[c], DFF)
        for f in range(NFF):
            wl.load(wd.t[:, f, :], Wd[f * 128:(f + 1) * 128, :], wd.rs[f], D)

        for s, L in enumerate(seqs):
            if DBG < 1:
                break
            Xv = X[s].rearrange("(c p) l -> p c l", p=128)
            Xiv = Xin[s].rearrange("(c p) l -> p c l", p=128)
            for t0 in range(0, L, T):
                n = min(T, L - t0)
                xr = kb.dres(("X", s), t0, t0 + n)
                kb.dma("sp", xt.t[:, :, :n], Xiv[:, :, t0:t0 + n], reads=xr, writes=[xt.r])
                if DBG < 2:
                    kb.dma("sp", Xv[:, :, t0:t0 + n], xt.t[:, :, :n], reads=[xt.r], writes=xr)
                    continue
                kb.op("act", lambda e: e.activation(out=sq.t[:, :, :n], in_=xt.t[:, :, :n], func=AF.Square),
                      reads=[xt.r], writes=sq.rs)

                def ssum(e, n=n):
                    for c in range(8):
                        ins = e.matmul(pss.t[:, :n], cst["ones_bf"].t[:], sq.t[:, c, :n], start=(c == 0), stop=(c == 7))
                    return ins
                kb.op("pe", ssum, reads=sq.rs + [cst["ones_bf"].r], writes=[pss.r])
                rstd_from_sumsq(kb, pss, rt, rs, n, cst, 1.0 / D)
                for c in range(8):
                    kb.op("dve", lambda e, c=c: e.scalar_tensor_tensor(
                        out=ht.t[:, c, :n], in0=xt.t[:, c, :n], scalar=gcol(cst, "ng", layer, 3 * j + 0 if j == 0 else 4, c),
                        op0=ALU.mult, in1=rt.t[:, :n], op1=ALU.mult), reads=[xt.r, rt.r, cst["ng"].r], writes=[ht.r])
                if DBG < 3:
                    kb.dma("sp", Xv[:, :, t0:t0 + n], xt.t[:, :, :n], reads=[xt.r, ht.r], writes=xr)
                    continue
                for f in range(NFF):
                    b = f % 2

                    def mmg(e, f=f, b=b, n=n):
                        for c in range(8):
                            ins = e.matmul(psg[b].t[:, :n], wg.t[:, c, f * 128:(f + 1) * 128], ht.t[:, c, :n],
                                           start=(c == 0), stop=(c == 7))
                        return ins

                    def mmu(e, f=f, b=b, n=n):
                        for c in range(8):
                            ins = e.matmul(psu[b].t[:, :n], wu.t[:, c, f * 128:(f + 1) * 128], ht.t[:, c, :n],
                                           start=(c == 0), stop=(c == 7))
                        return ins
                    kb.op("pe", mmg, reads=wg.rs + [ht.r], writes=[psg[b].r])
                    kb.op("pe", mmu, reads=wu.rs + [ht.r], writes=[psu[b].r])
                    kb.op("act", lambda e, b=b, n=n: e.activation(out=sg[b].t[:, :n], in_=psg[b].t[:, :n], func=AF.Silu),
                          reads=[psg[b].r], writes=[sg[b].r])
                    kb.op("dve", lambda e, b=b, f=f, n=n: e.tensor_tensor(out=at.t[:, f, :n], in0=psu[b].t[:, :n],
                                                                          in1=sg[b].t[:, :n], op=ALU.mult),
                          reads=[psu[b].r, sg[b].r], writes=[at.rs[f]])
                if DBG < 4:
                    kb.dma("sp", Xv[:, :, t0:t0 + n], xt.t[:, :, :n], reads=[xt.r] + at.rs, writes=xr)
                    continue
                for d in range(8):
                    b = d % 2

                    def mmd(e, d=d, b=b, n=n):
                        for f in range(NFF):
                            ins = e.matmul(pso[b].t[:, :n], wd.t[:, f, d * 128:(d + 1) * 128], at.t[:, f, :n],
                                           start=(f == 0), stop=(f == NFF - 1))
                        return ins
                    kb.op("pe", mmd, reads=wd.rs + at.rs, writes=[pso[b].r])
                    kb.op("dve", lambda e, d=d, b=b, n=n: e.tensor_copy(out=ot.t[:, d, :n], in_=pso[b].t[:, :n]),
                          reads=[pso[b].r], writes=[ot.rs[d]])
                    kb.op("act", lambda e, d=d, b=b, n=n: e.activation(out=sq.t[:, d, :n], in_=ot.t[:, d, :n], func=AF.Square),
                          reads=[ot.rs[d]], writes=[sq.rs[d]])
                if DBG < 5:
                    kb.dma("sp", Xv[:, :, t0:t0 + n], xt.t[:, :, :n], reads=[xt.r] + ot.rs + sq.rs, writes=xr)
                    continue
                kb.op("pe", ssum, reads=sq.rs + [cst["ones_bf"].r], writes=[pss.r])
                rstd_from_sumsq(kb, pss, rt, rs, n, cst, 1.0 / D)
                if DBG < 6:
                    kb.dma("sp", Xv[:, :, t0:t0 + n], xt.t[:, :, :n], reads=[xt.r, rt.r], writes=xr)
                    continue
                jn = 1 if j == 0 else 5
                for c in range(8):
                    b = c % 2
                    kb.op("dve", lambda e, c=c, b=b, n=n: e.scalar_tensor_tensor(
                        out=tmp[b].t[:, :n], in0=ot.t[:, c, :n], scalar=gcol(cst, "ngh", layer, jn, c),
                        op0=ALU.mult, in1=rt.t[:, :n], op1=ALU.mult), reads=[ot.rs[c], rt.r, cst["ngh"].r], writes=[tmp[b].r])
                    kb.op(ADD_ENG, lambda e, c=c, b=b, n=n: e.tensor_tensor(out=xt.t[:, c, :n], in0=xt.t[:, c, :n],
                                                                           in1=tmp[b].t[:, :n], op=ALU.add),
                          reads=[tmp[b].r, xt.r], writes=[xt.r])
                kb.dma("sp", Xv[:, :, t0:t0 + n], xt.t[:, :, :n], reads=[xt.r], writes=xr)
        kb.barrier()


class NormCtx:
    def __init__(self, kb, st, cst, T):
        self.kb, self.cst, self.T = kb, cst, T
        self.sq = kb.sb(st, "nsq", [128, 8, T], BF16)
        self.rt = kb.sb(st, "nrt", [128, T], F32)
        self.rs = kb.sb(st, "nrs", [128, T], F32)
        self.tmp = [kb.sb(st, "ntmp%d" % i, [128, T], F32) for i in range(2)]
        self.pss = kb.ps(st, "npss", [128, 512])

    def _rstd(self, src, n):
        kb, cst = self.kb, self.cst
        kb.op("act", lambda e: e.activation(out=self.sq.t[:, :, :n], in_=src.t[:, :, :n], func=AF.Square),
              reads=src.rs, writes=[self.sq.r])

        def ssum(e):
            for c in range(8):
                ins = e.matmul(self.pss.t[:, :n], cst["ones_bf"].t[:], self.sq.t[:, c, :n], start=(c == 0), stop=(c == 7))
            return ins
        kb.op("pe", ssum, reads=[self.sq.r, cst["ones_bf"].r], writes=[self.pss.r])
        rstd_from_sumsq(kb, self.pss, self.rt, self.rs, n, cst, 1.0 / D)

    def prenorm(self, xt, ht, n, layer, jn):
        kb, cst = self.kb, self.cst
        self._rstd(xt, n)
        for c in range(8):
            kb.op("dve", lambda e, c=c: e.scalar_tensor_tensor(
                out=ht.t[:, c, :n], in0=xt.t[:, c, :n], scalar=gcol(cst, "ng", layer, jn, c),
                op0=ALU.mult, in1=self.rt.t[:, :n], op1=ALU.mult), reads=xt.rs + [self.rt.r, cst["ng"].r], writes=ht.rs)

    def postnorm_res(self, ot, xt, n, layer, jn, half):
        kb, cst = self.kb, self.cst
        self._rstd(ot, n)
        which = "ngh" if half else "ng"
        for c in range(8):
            b = c % 2
            kb.op("dve", lambda e, c=c, b=b: e.scalar_tensor_tensor(
                out=self.tmp[b].t[:, :n], in0=ot.t[:, c, :n], scalar=gcol(cst, which, layer, jn, c),
                op0=ALU.mult, in1=self.rt.t[:, :n], op1=ALU.mult), reads=ot.rs + [self.rt.r, cst[which].r], writes=[self.tmp[b].r])
            kb.op("pool", lambda e, c=c, b=b: e.tensor_tensor(out=xt.t[:, c, :n], in0=xt.t[:, c, :n],
                                                             in1=self.tmp[b].t[:, :n], op=ALU.add),
                  reads=[self.tmp[b].r] + xt.rs, writes=xt.rs)


def outproj_phase(kb, cst, Wap, bias_tile, U, uname, Kdim, X, seqs, layer):
    KC = Kdim // 128
    T = 512
    with contextlib.ExitStack() as st:
        wo = kb.sb(st, "wo", [128, KC, D], BF16, nres=KC)
        wl = WLoader(kb, st, D)
        for k in range(KC):
            wl.load(wo.t[:, k, :], Wap[k * 128:(k + 1) * 128, :], wo.rs[k], D)
        nrm = NormCtx(kb, st, cst, T)
        xt = kb.sb(st, "oxt", [128, 8, T], F32)
        ut = kb.sb(st, "out", [128, KC, T], BF16)
        ot = kb.sb(st, "oot", [128, 8, T], F32)
        pso = [kb.ps(st, "opso%d" % i, [128, 512]) for i in range(2)]
        for s, L in enumerate(seqs):
            Xv = X[s].rearrange("(c p) l -> p c l", p=128)
            Uv = U[s].rearrange("(k p) l -> p k l", p=128)
            for t0 in range(0, L, T):
                n = min(T, L - t0)
                xr = kb.dres(("X", s), t0, t0 + n)
                ur = kb.dres((uname, s), t0, t0 + n)
                kb.dma("sp", xt.t[:, :, :n], Xv[:, :, t0:t0 + n], reads=xr, writes=[xt.r])
                kb.dma("sp", ut.t[:, :, :n], Uv[:, 0:KC, t0:t0 + n], reads=ur, writes=[ut.r])
                for d in range(8):
                    b = d % 2

                    def mmo(e, d=d, b=b):
                        for k in range(KC):
                            ins = e.matmul(pso[b].t[:, :n], wo.t[:, k, d * 128:(d + 1) * 128], ut.t[:, k, :n],
                                           start=(k == 0), stop=(k == KC - 1))
                        return ins
                    kb.op("pe", mmo, reads=wo.rs + [ut.r], writes=[pso[b].r])
                    if bias_tile is None:
                        kb.op("act", lambda e, d=d, b=b: e.activation(out=ot.t[:, d, :n], in_=pso[b].t[:, :n], func=AF.Copy),
                              reads=[pso[b].r], writes=[ot.r])
                    else:
                        kb.op("act", lambda e, d=d, b=b: e.activation(out=ot.t[:, d, :n], in_=pso[b].t[:, :n], func=AF.Identity,
                                                                       bias=bias_tile.t[:, d:d + 1], scale=1.0),
                              reads=[pso[b].r, bias_tile.r], writes=[ot.r])
                nrm.postnorm_res(ot, xt, n, layer, 3, False)
                kb.dma("sp", Xv[:, :, t0:t0 + n], xt.t[:, :, :n], reads=[xt.r], writes=xr)
        kb.barrier()


R_H, R_DK, R_DV = 4, 256, 512


def ret_stage(kb, cst, P, S, X, seqs, layer):
    nc = kb.nc
    T = 256
    W = P["b_w_in"]
    with contextlib.ExitStack() as st:
        wi = kb.sb(st, "rwi", [128, 8, 6144], BF16, nres=8)
        wl = WLoader(kb, st, 3072, nbuf=2)
        for c in range(8):
            for hh in range(2):
                wl.load(wi.t[:, c, hh * 3072:(hh + 1) * 3072], W[c * 128:(c + 1) * 128, hh * 3072:(hh + 1) * 3072], wi.rs[c], 3072)
        nrm = NormCtx(kb, st, cst, T)
        xt = kb.sb(st, "rxt", [128, 8, T], F32)
        ht = kb.sb(st, "rht", [128, 8, T], BF16)
        rope = kb.sb(st, "rrope", [128, 4, T], F32)
        qk = kb.sb(st, "rqk", [128, 16, T], BF16)
        x1 = kb.sb(st, "rx1", [128, T], F32)
        x2 = kb.sb(st, "rx2", [128, T], F32)
        ta = [kb.sb(st, "rta%d" % i, [128, T], F32) for i in range(2)]
        tb = [kb.sb(st, "rtb%d" % i, [128, T], F32) for i in range(2)]
        vt = [kb.sb(st, "rvt%d" % i, [128, 2048], BF16) for i in range(2)]
        gt = [kb.sb(st, "rgt%d" % i, [128, 2048], BF16) for i in range(2)]
        ps1 = kb.ps(st, "rps1", [128, 512])
        ps2 = kb.ps(st, "rps2", [128, 512])
        psv = [kb.ps(st, "rpsv%d" % i, [128, 512]) for i in range(2)]
        for s, L in enumerate(seqs):
            Xv = X[s].rearrange("(c p) l -> p c l", p=128)
            QKv = S["QK"][s].rearrange("(k p) l -> p k l", p=128)
            for t0 in range(0, L, T):
                n = min(T, L - t0)
                xr = kb.dres(("X", s), t0, t0 + n)
                kb.dma("sp", xt.t[:, :, :n], Xv[:, :, t0:t0 + n], reads=xr, writes=[xt.r])
                kb.dma("sp", rope.t[:, :, :n], P["rope"][:, :, t0:t0 + n], writes=[rope.r])
                nrm.prenorm(xt, ht, n, layer, 2)
                for pr in range(8):
                    isk = pr >= 4
                    for half, pst in ((0, ps1), (1, ps2)):
                        cc = 2 * pr + half

                        def mm(e, cc=cc, pst=pst):
                            for c in range(8):
                                ins = e.matmul(pst.t[:, :n], wi.t[:, c, cc * 128:(cc + 1) * 128], ht.t[:, c, :n],
                                               start=(c == 0), stop=(c == 7))
                            return ins
                        kb.op("pe", mm, reads=wi.rs + [ht.r], writes=[pst.r])
                    kb.op("act", lambda e: e.activation(out=x1.t[:, :n], in_=ps1.t[:, :n], func=AF.Copy), reads=[ps1.r], writes=[x1.r])
                    kb.op("act", lambda e: e.activation(out=x2.t[:, :n], in_=ps2.t[:, :n], func=AF.Copy), reads=[ps2.r], writes=[x2.r])
                    ci, si = (2, 3) if isk else (0, 1)
                    kb.op("dve", lambda e: e.tensor_tensor(out=ta[0].t[:, :n], in0=x1.t[:, :n], in1=rope.t[:, ci, :n], op=ALU.mult),
                          reads=[x1.r, rope.r], writes=[ta[0].r])
                    kb.op("dve", lambda e: e.tensor_tensor(out=ta[1].t[:, :n], in0=x2.t[:, :n], in1=rope.t[:, si, :n], op=ALU.mult),
                          reads=[x2.r, rope.r], writes=[ta[1].r])
                    kb.op("pool", lambda e: e.tensor_tensor(out=tb[0].t[:, :n], in0=x2.t[:, :n], in1=rope.t[:, ci, :n], op=ALU.mult),
                          reads=[x2.r, rope.r], writes=[tb[0].r])
                    kb.op("pool", lambda e: e.tensor_tensor(out=tb[1].t[:, :n], in0=x1.t[:, :n], in1=rope.t[:, si, :n], op=ALU.mult),
                          reads=[x1.r, rope.r], writes=[tb[1].r])
                    kb.op("dve", lambda e: e.tensor_tensor(out=qk.t[:, 2 * pr, :n], in0=ta[0].t[:, :n], in1=ta[1].t[:, :n], op=ALU.subtract),
                          reads=[ta[0].r, ta[1].r], writes=[qk.r])
                    kb.op("pool", lambda e: e.tensor_tensor(out=qk.t[:, 2 * pr + 1, :n], in0=tb[0].t[:, :n], in1=tb[1].t[:, :n], op=ALU.add),
                          reads=[tb[0].r, tb[1].r], writes=[qk.r])
                kb.dma("sp", QKv[:, :, t0:t0 + n], qk.t[:, :, :n], reads=[qk.r], writes=kb.dres(("QK", s), t0, t0 + n))
                for tb_i in range(n // 128):
                    bb = tb_i % 2
                    for cb in range(8):
                        pb = cb % 2

                        def mmv(e, cb=cb, pb=pb, tb_i=tb_i):
                            for c in range(8):
                                ins = e.matmul(psv[pb].t[:, :], ht.t[:, c, tb_i * 128:(tb_i + 1) * 128],
                                               wi.t[:, c, 2048 + cb * 512:2048 + (cb + 1) * 512], start=(c == 0), stop=(c == 7))
                            return ins
                        kb.op("pe", mmv, reads=wi.rs + [ht.r], writes=[psv[pb].r])
                        if cb < 4:
                            kb.op("act", lambda e, cb=cb, pb=pb, bb=bb: e.activation(out=vt[bb].t[:, cb * 512:(cb + 1) * 512], in_=psv[pb].t[:, :], func=AF.Copy),
                                  reads=[psv[pb].r], writes=[vt[bb].r])
                        else:
                            kb.op("act", lambda e, cb=cb, pb=pb, bb=bb: e.activation(out=gt[bb].t[:, (cb - 4) * 512:(cb - 3) * 512], in_=psv[pb].t[:, :], func=AF.Silu),
                                  reads=[psv[pb].r], writes=[gt[bb].r])
                    r0 = t0 + tb_i * 128
                    kb.dma("sp", S["V"][s][r0:r0 + 128, :], vt[bb].t[:, :], reads=[vt[bb].r], writes=kb.dres(("V", s), r0, r0 + 128))
                    kb.dma("sp", S["G"][s][r0:r0 + 128, :], gt[bb].t[:, :], reads=[gt[bb].r], writes=kb.dres(("G", s), r0, r0 + 128))
        kb.barrier()
    Lmax = max(seqs)
    NBmax = Lmax // 128
    with contextlib.ExitStack() as st:
        dl = kb.sb(st, "rdl", [128, 8], F32)
        lg = kb.sb(st, "rlg", [128, 8], F32)
        nlg = kb.sb(st, "rnlg", [128, 8], F32)
        dmat = kb.sb(st, "rdmat", [128, 128], F32)
        tri = kb.sb(st, "rtri", [128, 2, 128], F32)
        dvals = kb.sb(st, "rdvals", [128, 32], F32)
        sc = kb.sb(st, "rsc", [128, 8, 32], F32)
        Ff = kb.sb(st, "rFf", [128, 4, 128], F32)
        Fb = kb.sb(st, "rFb", [128, 4, 128], F32)
        Dg = kb.sb(st, "rDg", [128, 4, 128], F32)
        tmpm = kb.sb(st, "rtmpm", [128, 128], F32)
        gnw = kb.sb(st, "rgnw", [128, 2048], F32)
        kb.dma("sp", dl.t[:], P["b_decay_logit"][:, :], writes=[dl.r])
        kb.dma("sp", dmat.t[:], P["dmat"][:, :], writes=[dmat.r])
        kb.dma("sp", tri.t[:], P["tri"][:, :, :], writes=[tri.r])
        kb.dma("sp", dvals.t[:], P["dvals"][:, :], writes=[dvals.r])
        kb.dma("sp", gnw.t[:], P["b_gn_w"][:, :], writes=[gnw.r])
        kb.op("act", lambda e: e.activation(out=lg.t[:], in_=dl.t[:], func=AF.Exp, scale=math.log(2.0)), reads=[dl.r], writes=[lg.r])
        kb.op("act", lambda e: e.activation(out=lg.t[:], in_=lg.t[:], func=AF.Ln, scale=-1.0, bias=cst["one"].t[:, 0:1]),
              reads=[lg.r, cst["one"].r], writes=[lg.r])
        kb.op("dve", lambda e: e.tensor_scalar(out=nlg.t[:], in0=lg.t[:], scalar1=-1.0, scalar2=None, op0=ALU.mult), reads=[lg.r], writes=[nlg.r])
        for hd in range(4):
            kb.op("act", lambda e, hd=hd: e.activation(out=Ff.t[:, hd, :], in_=dmat.t[:], func=AF.Exp, scale=lg.t[:, hd:hd + 1]),
                  reads=[dmat.r, lg.r], writes=[Ff.r])
            kb.op("act", lambda e, hd=hd: e.activation(out=Fb.t[:, hd, :], in_=dmat.t[:], func=AF.Exp, scale=nlg.t[:, 4 + hd:5 + hd]),
                  reads=[dmat.r, nlg.r], writes=[Fb.r])
            kb.op("dve", lambda e, hd=hd: e.tensor_tensor(out=tmpm.t[:], in0=Ff.t[:, hd, :], in1=tri.t[:, 0, :], op=ALU.mult),
                  reads=[Ff.r, tri.r], writes=[tmpm.r])
            kb.op("dve", lambda e, hd=hd: e.tensor_tensor(out=Dg.t[:, hd, :], in0=Fb.t[:, hd, :], in1=tri.t[:, 1, :], op=ALU.mult),
                  reads=[Fb.r, tri.r], writes=[Dg.r])
            kb.op("dve", lambda e, hd=hd: e.tensor_tensor(out=Dg.t[:, hd, :], in0=Dg.t[:, hd, :], in1=tmpm.t[:], op=ALU.add),
                  reads=[Dg.r, tmpm.r], writes=[Dg.r])
        for k in range(8):
            kb.op("act", lambda e, k=k: e.activation(out=sc.t[:, k, :], in_=dvals.t[:], func=AF.Exp, scale=lg.t[:, k:k + 1]),
                  reads=[dvals.r, lg.r], writes=[sc.r])
        qT = kb.sb(st, "rqT", [128, 2, Lmax], BF16)
        kT = kb.sb(st, "rkT", [128, 2, Lmax], BF16)
        vh = kb.sb(st, "rvh", [128, NBmax, 512], BF16)
        pT = [kb.sb(st, "rpT%d" % i, [128, 128], BF16) for i in range(3)]
        sgt = [kb.sb(st, "rsg%d" % i, [128, 512], BF16) for i in range(2)]
        on = kb.sb(st, "ron", [128, 512], F32)
        on2 = kb.sb(st, "ron2", [128, 512], F32)
        utok = kb.sb(st, "rutok", [128, 512], BF16)
        uT = [kb.sb(st, "ruT%d" % i, [128, 4, 128], BF16) for i in range(2)]
        stats = kb.sb(st, "rstats", [128, 6], F32)
        mv = kb.sb(st, "rmv", [128, 2], F32)
        sd = kb.sb(st, "rsd", [128, 1], F32)
        rstd = kb.sb(st, "rrstd", [128, 1], F32)
        pss = [kb.ps(st, "rpss%d" % i, [128, 512]) for i in range(3)]
        pso = [kb.ps(st, "rpso%d" % i, [128, 512]) for i in range(2)]
        pst = kb.ps(st, "rpst", [128, 1024], BF16)
        for s, L in enumerate(seqs):
            NB = L // 128
            QKv = S["QK"][s].rearrange("(k p) l -> p k l", p=128)
            Vv = S["V"][s].rearrange("(b p) e -> p b e", p=128)
            Uv = S["U"][s].rearrange("(k p) l -> p k l", p=128)
            for hd in range(4):
                allr = kb.dres(("QK", s), 0, L)
                for l0 in range(0, L, 1024):
                    l1 = min(L, l0 + 1024)
                    kb.dma("sp", qT.t[:, :, l0:l1], QKv[:, 2 * hd:2 * hd + 2, l0:l1], reads=allr, writes=[qT.r])
                    kb.dma("sp", kT.t[:, :, l0:l1], QKv[:, 8 + 2 * hd:8 + 2 * hd + 2, l0:l1], reads=allr, writes=[kT.r])
                for b0 in range(0, NB, 4):
                    b1 = min(NB, b0 + 4)
                    kb.dma("sp", vh.t[:, b0:b1, :], Vv[:, b0:b1, hd * 512:(hd + 1) * 512], reads=kb.dres(("V", s), 0, L), writes=[vh.r])
                it = 0
                for i in range(NB):
                    ob = i % 2
                    kb.dma("sp", sgt[ob].t[:], S["G"][s][i * 128:(i + 1) * 128, hd * 512:(hd + 1) * 512],
                           reads=kb.dres(("G", s), i * 128, (i + 1) * 128), writes=[sgt[ob].r])
                    pend = []

                    def emit_mmo(j, b3, ob):
                        kb.op("pe", lambda e: e.matmul(pso[ob].t[:, :], pT[b3].t[:], vh.t[:, j, :], start=(j == 0), stop=(j == NB - 1)),
                              reads=[pT[b3].r, vh.r], writes=[pso[ob].r])
                    for j in range(NB):
                        b3 = it % 3
                        it += 1

                        def mms(e, j=j, i=i, b3=b3):
                            e.matmul(pss[b3].t[:, 0:128], kT.t[:, 0, j * 128:(j + 1) * 128], qT.t[:, 0, i * 128:(i + 1) * 128], start=True, stop=False)
                            return e.matmul(pss[b3].t[:, 0:128], kT.t[:, 1, j * 128:(j + 1) * 128], qT.t[:, 1, i * 128:(i + 1) * 128], start=False, stop=True)
                        kb.op("pe", mms, reads=[kT.r, qT.r], writes=[pss[b3].r])
                        if i == j:
                            kb.op("dve", lambda e, b3=b3: e.tensor_tensor(out=pT[b3].t[:], in0=pss[b3].t[:, 0:128], in1=Dg.t[:, hd, :], op=ALU.mult),
                                  reads=[pss[b3].r, Dg.r], writes=[pT[b3].r])
                        elif j < i:
                            kb.op("dve", lambda e, b3=b3, i=i, j=j: e.scalar_tensor_tensor(
                                out=pT[b3].t[:], in0=pss[b3].t[:, 0:128], scalar=sc.t[:, hd, (i - j):(i - j) + 1], op0=ALU.mult,
                                in1=Ff.t[:, hd, :], op1=ALU.mult), reads=[pss[b3].r, Ff.r, sc.r], writes=[pT[b3].r])
                        else:
                            kb.op("dve", lambda e, b3=b3, i=i, j=j: e.scalar_tensor_tensor(
                                out=pT[b3].t[:], in0=pss[b3].t[:, 0:128], scalar=sc.t[:, 4 + hd, (j - i):(j - i) + 1], op0=ALU.mult,
                                in1=Fb.t[:, hd, :], op1=ALU.mult), reads=[pss[b3].r, Fb.r, sc.r], writes=[pT[b3].r])
                        pend.append((j, b3))
                        if len(pend) > 2:
                            emit_mmo(*pend.pop(0), ob)
                    while pend:
                        emit_mmo(*pend.pop(0), ob)
                    kb.op("dve", lambda e, ob=ob: e.bn_stats(out=stats.t[:], in_=pso[ob].t[:, :]), reads=[pso[ob].r], writes=[stats.r])
                    kb.op("dve", lambda e: e.bn_aggr(out=mv.t[:], in_=stats.t[:]), reads=[stats.r], writes=[mv.r])
                    kb.op("act", lambda e: e.activation(out=sd.t[:], in_=mv.t[:, 1:2], func=AF.Sqrt, bias=cst["eps"].t[:, 0:1], scale=1.0),
                          reads=[mv.r, cst["eps"].r], writes=[sd.r])
                    kb.op("dve", lambda e: e.reciprocal(out=rstd.t[:], in_=sd.t[:]), reads=[sd.r], writes=[rstd.r])
                    kb.op("dve", lambda e, ob=ob: e.tensor_scalar(out=on.t[:], in0=pso[ob].t[:, :], scalar1=mv.t[:, 0:1], scalar2=rstd.t[:, 0:1],
                                                                   op0=ALU.subtract, op1=ALU.mult), reads=[pso[ob].r, mv.r, rstd.r], writes=[on.r])
                    kb.op("pool", lambda e: e.tensor_tensor(out=on2.t[:], in0=on.t[:], in1=gnw.t[:, hd * 512:(hd + 1) * 512], op=ALU.mult),
                          reads=[on.r, gnw.r], writes=[on2.r])
                    kb.op("pool", lambda e, ob=ob: e.tensor_tensor(out=utok.t[:], in0=on2.t[:], in1=sgt[ob].t[:], op=ALU.mult),
                          reads=[on2.r, sgt[ob].r], writes=[utok.r])

                    def tr(e):
                        for q in range(4):
                            ins = e.transpose(out=pst.t[:, q * 128:(q + 1) * 128], in_=utok.t[:, q * 128:(q + 1) * 128], identity=cst["ident_bf"].t[:])
                        return ins
                    kb.op("pe", tr, reads=[utok.r, cst["ident_bf"].r], writes=[pst.r])
                    kb.op("act", lambda e, ob=ob: e.activation(out=uT[ob].t[:].rearrange("p a b -> p (a b)"), in_=pst.t[:, 0:512], func=AF.Copy),
                          reads=[pst.r], writes=[uT[ob].r])
                    kb.dma("sp", Uv[:, hd * 4:(hd + 1) * 4, i * 128:(i + 1) * 128], uT[ob].t[:], reads=[uT[ob].r],
                           writes=kb.dres(("U", s), i * 128, (i + 1) * 128))
        kb.barrier()
    outproj_phase(kb, cst, P["b_w_out"], None, S["U"], "U", 2048, X, seqs, layer)


G_H = 8


def gdn_stage(kb, cst, P, S, X, seqs, layer):
    nc = kb.nc
    ja = layer // 3
    T = 256
    W = P["a_w_in"][ja]
    with contextlib.ExitStack() as st:
        wi = kb.sb(st, "gwi", [128, 8, 4128], BF16, nres=8)
        wl = WLoader(kb, st, 2064, nbuf=2)
        for c in range(8):
            for hh in range(2):
                wl.load(wi.t[:, c, hh * 2064:(hh + 1) * 2064], W[c * 128:(c + 1) * 128, hh * 2064:(hh + 1) * 2064], wi.rs[c], 2064)
        N2 = T + 2
        nrm = NormCtx(kb, st, cst, N2)
        xt = kb.sb(st, "gxt", [128, 8, N2], F32)
        ht = kb.sb(st, "ght", [128, 8, N2], BF16)
        cw = kb.sb(st, "gcw", [128, 72], F32)
        kb.dma("sp", cw.t[:], P["a_conv_w"][ja], writes=[cw.r])
        nA = kb.sb(st, "gnA", [128, 16], F32)
        dtb = kb.sb(st, "gdtb", [128, 16], F32)
        kb.dma("sp", nA.t[:], P["a_a_log"][ja], writes=[nA.r])
        kb.dma("sp", dtb.t[:], P["a_dt_bias"][ja], writes=[dtb.r])
        kb.op("act", lambda e: e.activation(out=nA.t[:], in_=nA.t[:], func=AF.Exp), reads=[nA.r], writes=[nA.r])
        kb.op("dve", lambda e: e.tensor_scalar(out=nA.t[:], in0=nA.t[:], scalar1=-1.0, scalar2=None, op0=ALU.mult), reads=[nA.r], writes=[nA.r])
        NSL = 3
        pc = [kb.sb(st, "gpc%d" % i, [128, N2], F32) for i in range(NSL)]
        acc = [kb.sb(st, "gacc%d" % i, [128, T], F32) for i in range(NSL)]
        sl = [kb.sb(st, "gsl%d" % i, [128, T], F32) for i in range(NSL)]
        sqn = [kb.sb(st, "gsqn%d" % i, [128, T], BF16) for i in range(NSL)]
        rn = [kb.sb(st, "grn%d" % i, [128, T], F32) for i in range(NSL)]
        rn2 = [kb.sb(st, "grn2%d" % i, [128, T], F32) for i in range(NSL)]
        qkv = kb.sb(st, "gqkv", [128, 24, T], BF16, nres=24)
        zt = [kb.sb(st, "gzt%d" % i, [128, 1024], BF16) for i in range(2)]
        gbt = [kb.sb(st, "ggb%d" % i, [128, 32], F32) for i in range(2)]
        t16 = kb.sb(st, "gt16", [128, 2, 8], F32)
        psq = [kb.ps(st, "gpsq%d" % i, [128, 512]) for i in range(NSL)]
        psn = [kb.ps(st, "gpsn%d" % i, [128, 512]) for i in range(NSL)]
        psz = [psq[0], psq[1]]
        psab = psn[0]

        def lockstep(gens):
            gens = list(gens)
            while gens:
                for g in list(gens):
                    try:
                        next(g)
                    except StopIteration:
                        gens.remove(g)
        for s, L in enumerate(seqs):
            Xv = X[s].rearrange("(c p) l -> p c l", p=128)
            QKVv = S["QKV"][s].rearrange("(k p) l -> p k l", p=128)
            for t0 in range(0, L, T):
                n = min(T, L - t0)
                n2 = n + 2
                lo, hi = max(t0 - 1, 0), min(t0 + n + 1, L)
                xr = kb.dres(("X", s), lo, hi)
                if t0 == 0:
                    kb.op("pool", lambda e: e.memset(xt.t[:, :, 0:1], 0.0), writes=[xt.r])
                if t0 + n == L:
                    kb.op("pool", lambda e: e.memset(xt.t[:, :, n + 1:n + 2], 0.0), writes=[xt.r])
                kb.dma("sp", xt.t[:, :, lo - (t0 - 1):hi - (t0 - 1)], Xv[:, :, lo:hi], reads=xr, writes=[xt.r])
                nrm.prenorm(xt, ht, n2, layer, 2)
                def chain(cc, b):
                    def mm(e):
                        for c in range(8):
                            ins = e.matmul(psq[b].t[:, :n2], wi.t[:, c, cc * 128:(cc + 1) * 128], ht.t[:, c, :n2],
                                           start=(c == 0), stop=(c == 7))
                        return ins
                    kb.op("pe", mm, reads=wi.rs + [ht.r], writes=[psq[b].r])
                    yield
                    kb.op("act", lambda e: e.activation(out=pc[b].t[:, :n2], in_=psq[b].t[:, :n2], func=AF.Copy), reads=[psq[b].r], writes=[pc[b].r])
                    yield
                    kb.op("pool", lambda e: e.tensor_scalar(out=acc[b].t[:, :n], in0=pc[b].t[:, 0:n], scalar1=cw.t[:, cc:cc + 1], scalar2=None, op0=ALU.mult),
                          reads=[pc[b].r, cw.r], writes=[acc[b].r])
                    yield
                    for tap in (1, 2):
                        kb.op("dve", lambda e: e.scalar_tensor_tensor(
                            out=acc[b].t[:, :n], in0=pc[b].t[:, tap:tap + n], scalar=cw.t[:, tap * 24 + cc:tap * 24 + cc + 1], op0=ALU.mult,
                            in1=acc[b].t[:, :n], op1=ALU.add), reads=[pc[b].r, cw.r, acc[b].r], writes=[acc[b].r])
                        yield
                    if cc >= 16:
                        kb.op("act", lambda e: e.activation(out=qkv.t[:, cc, :n], in_=acc[b].t[:, :n], func=AF.Silu), reads=[acc[b].r], writes=[qkv.rs[cc]])
                        yield
                    else:
                        kb.op("act", lambda e: e.activation(out=sl[b].t[:, :n], in_=acc[b].t[:, :n], func=AF.Silu), reads=[acc[b].r], writes=[sl[b].r])
                        yield
                        kb.op("act", lambda e: e.activation(out=sqn[b].t[:, :n], in_=sl[b].t[:, :n], func=AF.Square), reads=[sl[b].r], writes=[sqn[b].r])
                        yield
                        kb.op("pe", lambda e: e.matmul(psn[b].t[:, :n], cst["ones_bf"].t[:], sqn[b].t[:, :n], start=True, stop=True),
                              reads=[sqn[b].r, cst["ones_bf"].r], writes=[psn[b].r])
                        yield
                        kb.op("act", lambda e: e.activation(out=rn2[b].t[:, :n], in_=psn[b].t[:, :n], func=AF.Sqrt, bias=cst["eps"].t[:, 0:1], scale=1.0),
                              reads=[psn[b].r, cst["eps"].r], writes=[rn2[b].r])
                        yield
                        kb.op("dve", lambda e: e.reciprocal(out=rn[b].t[:, :n], in_=rn2[b].t[:, :n]), reads=[rn2[b].r], writes=[rn[b].r])
                        yield
                        scl = (128.0 ** -0.5) if cc < 8 else 1.0
                        kb.op("dve", lambda e: e.scalar_tensor_tensor(
                            out=qkv.t[:, cc, :n], in0=sl[b].t[:, :n], scalar=scl, op0=ALU.mult, in1=rn[b].t[:, :n], op1=ALU.mult),
                            reads=[sl[b].r, rn[b].r], writes=[qkv.rs[cc]])
                        yield
                for g0 in range(0, 24, NSL):
                    lockstep([chain(cc, cc - g0) for cc in range(g0, g0 + NSL)])
                kb.dma("sp", QKVv[:, :, t0:t0 + n], qkv.t[:, :, :n], reads=qkv.rs, writes=kb.dres(("QKV", s), t0, t0 + n))
                for tb_i in range(n // 128):
                    bb = tb_i % 2
                    c0 = 1 + tb_i * 128
                    for zb in range(2):
                        def mmz(e, zb=zb, c0=c0):
                            for c in range(8):
                                ins = e.matmul(psz[zb].t[:, :], ht.t[:, c, c0:c0 + 128], wi.t[:, c, 3072 + zb * 512:3072 + (zb + 1) * 512],
                                               start=(c == 0), stop=(c == 7))
                            return ins
                        kb.op("pe", mmz, reads=wi.rs + [ht.r], writes=[psz[zb].r])
                        kb.op("act", lambda e, zb=zb, bb=bb: e.activation(out=zt[bb].t[:, zb * 512:(zb + 1) * 512], in_=psz[zb].t[:, :], func=AF.Silu),
                              reads=[psz[zb].r], writes=[zt[bb].r])

                    def mmab(e, c0=c0):
                        for c in range(8):
                            ins = e.matmul(psab.t[:, 0:32], ht.t[:, c, c0:c0 + 128], wi.t[:, c, 4096:4128], start=(c == 0), stop=(c == 7))
                        return ins
                    kb.op("pe", mmab, reads=wi.rs + [ht.r], writes=[psab.r])
                    abv = psab.t[:, 0:32].rearrange("p (d k) -> p d k", d=2)
                    kb.op("dve", lambda e: e.tensor_tensor(out=t16.t[:], in0=abv[:, :, 0:8], in1=dtb.t[:].rearrange("p (d k) -> p d k", d=2), op=ALU.add),
                          reads=[psab.r, dtb.r], writes=[t16.r])
                    kb.op("dve", lambda e, bb=bb: e.tensor_copy(out=gbt[bb].t[:, 16:32].rearrange("p (d k) -> p d k", d=2), in_=abv[:, :, 8:16]),
                          reads=[psab.r], writes=[gbt[bb].r])
                    kb.op("act", lambda e: e.activation(out=t16.t[:], in_=t16.t[:], func=AF.Exp), reads=[t16.r], writes=[t16.r])
                    kb.op("act", lambda e: e.activation(out=t16.t[:], in_=t16.t[:], func=AF.Ln, bias=cst["one"].t[:, 0:1], scale=1.0),
                          reads=[t16.r, cst["one"].r], writes=[t16.r])
                    kb.op("act", lambda e, bb=bb: e.activation(out=gbt[bb].t[:, 16:32], in_=gbt[bb].t[:, 16:32], func=AF.Sigmoid), reads=[gbt[bb].r], writes=[gbt[bb].r])
                    kb.op("dve", lambda e, bb=bb: e.tensor_tensor(out=gbt[bb].t[:, 0:16], in0=t16.t[:].rearrange("p d k -> p (d k)"), in1=nA.t[:], op=ALU.mult),
                          reads=[t16.r, nA.r, gbt[bb].r], writes=[gbt[bb].r])
                    r0 = t0 + tb_i * 128
                    kb.dma("sp", S["Z"][s][r0:r0 + 128, :], zt[bb].t[:, :], reads=[zt[bb].r], writes=kb.dres(("Z", s), r0, r0 + 128))
                    kb.dma("sp", S["GB"][s][r0:r0 + 128, :], gbt[bb].t[:, :], reads=[gbt[bb].r], writes=kb.dres(("GB", s), r0, r0 + 128))
        kb.barrier()
    AX = mybir.AxisListType
    HS = 4
    W4 = HS * 128
    with contextlib.ExitStack() as st:
        gm = kb.sb(st, "ggm", [128, 4, 128], F32)
        negm = kb.sb(st, "gnegm", [128, 4, W4], F32)
        ones_f = kb.sb(st, "gones", [128, 128], F32)
        nw = kb.sb(st, "gnw", [128, 1024], F32)
        bmk = kb.sb(st, "gbmk", [128, 4, W4], BF16)
        bmf = kb.sb(st, "gbmf", [128, W4], F32)
        kb.dma("sp", gm.t[:], P["gmask"][:, :, :], writes=[gm.r])
        for i in range(4):
            kb.dma("sp", negm.t[:, i, :], P["negm"][:, i, 0:W4], writes=[negm.r])
            kb.dma("sp", bmf.t[:], P["bmask"][:, i, 0:W4], writes=[bmf.r])
            kb.op("dve", lambda e, i=i: e.tensor_copy(out=bmk.t[:, i, :], in_=bmf.t[:]), reads=[bmf.r], writes=[bmk.r])
        kb.dma("sp", nw.t[:], P["a_norm_w"][ja], writes=[nw.r])
        kb.op("pool", lambda e: e.memset(ones_f.t[:], 1.0), writes=[ones_f.r])

        def alloc_half(hh):
            t = {}
            n3 = lambda nm, dt: kb.sb(st, "g%s%d" % (nm, hh), [128, HS, 128], dt)
            t["Q"] = [kb.sb(st, "gQ%d_%d" % (hh, i), [128, 3, HS, 128], BF16) for i in range(2)]
            t["G"] = [kb.sb(st, "gG%d_%d" % (hh, i), [128, 32], F32) for i in range(2)]
            for nm in ("GmA", "GmT", "DAs", "DTc", "Yf", "Tf", "tav", "ot", "Sf"):
                t[nm] = n3(nm, F32)
            for nm in ("Yb", "Tb", "NGb", "vb", "kbg", "kdec", "nwT", "vn", "attnT", "Sb"):
                t[nm] = n3(nm, BF16)
            t["Pb"] = [n3("Pb%d_" % i, BF16) for i in range(2)]
            t["Ppb"] = [n3("Ppb%d_" % i, BF16) for i in range(2)]
            t["Bm"] = [n3("Bm%d_" % i, BF16) for i in range(3)]
            t["gc16"] = kb.sb(st, "ggc%d" % hh, [128, 2 * HS], F32)
            t["sm"] = kb.sb(st, "gsm%d" % hh, [128, 6, HS], F32)
            t["PX"] = kb.ps(st, "gPX%d" % hh, [128, 512])
            t["PY"] = kb.ps(st, "gPY%d" % hh, [128, 512])
            return t
        TL = [[alloc_half(0), alloc_half(1)], [alloc_half(2), alloc_half(3)]]

        def flat(t):
            return t.t[:].rearrange("p h m -> p (h m)")

        def hs(h):
            return slice(h * 128, (h + 1) * 128)

        def mm4(ps, lhs, rhs, lsel=None, rsel=None):
            def f(e):
                for h in range(HS):
                    l = lhs.t[:, h, :] if lsel is None else lhs.t[:, lsel, h, :]
                    r = rhs.t[:, h, :] if rsel is None else rhs.t[:, rsel, h, :]
                    ins = e.matmul(ps.t[:, hs(h)], l, r, start=True, stop=True)
                return ins
            return f

        def bfv(ps):
            return ps.t[:].bitcast(BF16)

        def tr4(ps, src, sel=None, off=0):
            def f(e):
                pv = bfv(ps)
                for h in range(HS):
                    i_ = src.t[:, h, :] if sel is None else src.t[:, sel, h, :]
                    ins = e.transpose(out=pv[:, off + h * 128:off + (h + 1) * 128], in_=i_, identity=cst["ident_bf"].t[:])
                return ins
            return f

        def half_iter(t, hh, s, dr, it, cb, QKVv4, Uv):
            h0 = hh * HS
            x0, x1 = (0, 1) if dr == 0 else (2, 3)
            c0 = cb * 128
            bq = it % 2
            Q, G = t["Q"][bq], t["G"][bq]
            PX, PY = t["PX"], t["PY"]
            sm, gc16 = t["sm"], t["gc16"]
            GmA, GmT, DAs, DTc = t["GmA"], t["GmT"], t["DAs"], t["DTc"]
            Pb, Ppb, Bm, Yf, Yb, Tf, Tb, NGb = t["Pb"], t["Ppb"], t["Bm"], t["Yf"], t["Yb"], t["Tf"], t["Tb"], t["NGb"]
            vb, kbg, kdec, nwT, vn, attnT, tav, ot, Sf, Sb = t["vb"], t["kbg"], t["kdec"], t["nwT"], t["vn"], t["attnT"], t["tav"], t["ot"], t["Sf"], t["Sb"]
            for tq in range(3):
                kb.dma("sp", Q.t[:, tq, :, :], QKVv4[:, tq, h0:h0 + HS, c0:c0 + 128], reads=kb.dres(("QKV", s), c0, c0 + 128), writes=[Q.r])
            kb.dma("sp", G.t[:], S["GB"][s][c0:c0 + 128, :], reads=kb.dres(("GB", s), c0, c0 + 128), writes=[G.r])
            yield
            gofs = dr * 8 + h0
            g8 = G.t[:, gofs:gofs + HS]
            be8 = G.t[:, 16 + gofs:16 + gofs + HS]
            for h in range(HS):
                gcolh = G.t[:, gofs + h:gofs + h + 1]
                kb.op("pool", lambda e: e.tensor_scalar(out=GmA.t[:, h, :], in0=gm.t[:, x1, :], scalar1=gcolh, scalar2=None, op0=ALU.mult),
                      reads=[gm.r, G.r], writes=[GmA.r])
                kb.op("pool", lambda e: e.tensor_scalar(out=GmT.t[:, h, :], in0=gm.t[:, x0, :], scalar1=gcolh, scalar2=None, op0=ALU.mult),
                      reads=[gm.r, G.r], writes=[GmT.r])
                yield

            def mmA(e):
                e.matmul(PX.t[:, :], gm.t[:, x0, :], flat(GmA), start=True, stop=False)
                return e.matmul(PX.t[:, :], cst["ident_f"].t[:], negm.t[:, x0, :], start=False, stop=True)
            kb.op("pe", mmA, reads=[gm.r, GmA.r, negm.r, cst["ident_f"].r], writes=[PX.r])
            kb.op("act", lambda e: e.activation(out=flat(DAs), in_=PX.t[:], func=AF.Exp), reads=[PX.r], writes=[DAs.r])
            yield

            def mmT(e):
                e.matmul(PY.t[:, :], gm.t[:, x1, :], flat(GmT), start=True, stop=False)
                return e.matmul(PY.t[:, :], cst["ident_f"].t[:], negm.t[:, x1, :], start=False, stop=True)
            kb.op("pe", mmT, reads=[gm.r, GmT.r, negm.r, cst["ident_f"].r], writes=[PY.r])
            kb.op("act", lambda e: e.activation(out=flat(DTc), in_=PY.t[:], func=AF.Exp), reads=[PY.r], writes=[DTc.r])
            yield

            def mmg(e):
                e.matmul(PX.t[:, 0:HS], gm.t[:, x0, :], g8, start=True, stop=True)
                return e.matmul(PX.t[:, HS:2 * HS], ones_f.t[:], g8, start=True, stop=True)
            kb.op("pe", mmg, reads=[gm.r, G.r, ones_f.r], writes=[PX.r])
            kb.op("dve", lambda e: e.tensor_copy(out=gc16.t[:], in_=PX.t[:, 0:2 * HS]), reads=[PX.r], writes=[gc16.r])
            yield
            kb.op("act", lambda e: e.activation(out=sm.t[:, 0:2, :].rearrange("p a k -> p (a k)"), in_=gc16.t[:], func=AF.Exp), reads=[gc16.r], writes=[sm.r])
            kb.op("dve", lambda e: e.tensor_tensor(out=sm.t[:, 5, :], in0=gc16.t[:, HS:2 * HS], in1=gc16.t[:, 0:HS], op=ALU.subtract), reads=[gc16.r, sm.r], writes=[sm.r])
            yield
            kb.op("act", lambda e: e.activation(out=sm.t[:, 2, :], in_=sm.t[:, 5, :], func=AF.Exp), reads=[sm.r], writes=[sm.r])
            kb.op("dve", lambda e: e.tensor_tensor(out=sm.t[:, 3, :], in0=sm.t[:, 0, :], in1=be8, op=ALU.mult), reads=[sm.r, G.r], writes=[sm.r])
            kb.op("dve", lambda e: e.tensor_scalar(out=sm.t[:, 4, :], in0=be8, scalar1=-1.0, scalar2=None, op0=ALU.mult), reads=[G.r, sm.r], writes=[sm.r])
            yield
            kb.op("pe", tr4(PY, Q, sel=1, off=0), reads=[Q.r, cst["ident_bf"].r], writes=[PY.r])
            kb.op("pe", tr4(PY, Q, sel=2, off=W4), reads=[Q.r, cst["ident_bf"].r], writes=[PY.r])
            yield
            for h in range(HS):
                kb.op("dve", lambda e: e.tensor_scalar(out=kbg.t[:, h, :], in0=bfv(PY)[:, hs(h)], scalar1=sm.t[:, 3, h:h + 1], scalar2=None, op0=ALU.mult),
                      reads=[PY.r, sm.r], writes=[kbg.r])
                kb.op("dve", lambda e: e.tensor_scalar(out=kdec.t[:, h, :], in0=bfv(PY)[:, hs(h)], scalar1=sm.t[:, 2, h:h + 1], scalar2=None, op0=ALU.mult),
                      reads=[PY.r, sm.r], writes=[kdec.r])
                kb.op("dve", lambda e: e.tensor_scalar(out=vb.t[:, h, :], in0=bfv(PY)[:, W4 + h * 128:W4 + (h + 1) * 128], scalar1=G.t[:, 16 + gofs + h:16 + gofs + h + 1], scalar2=None, op0=ALU.mult),
                      reads=[PY.r, G.r], writes=[vb.r])
                yield
            kb.op("pe", mm4(PX, Q, Q, lsel=1, rsel=1), reads=[Q.r], writes=[PX.r])
            yield
            for h in range(HS):
                kb.op("dve", lambda e: e.scalar_tensor_tensor(out=Pb[0].t[:, h, :], in0=PX.t[:, hs(h)], scalar=sm.t[:, 4, h:h + 1], op0=ALU.mult,
                                                               in1=DAs.t[:, h, :], op1=ALU.mult), reads=[PX.r, sm.r, DAs.r], writes=[Pb[0].r])
                yield
            kb.op("pe", tr4(PY, Pb[0]), reads=[Pb[0].r, cst["ident_bf"].r], writes=[PY.r])
            kb.op("dve", lambda e: e.tensor_copy(out=flat(Ppb[0]), in_=bfv(PY)[:, 0:W4]), reads=[PY.r], writes=[Ppb[0].r])
            yield
            for lv in range(3):
                kb.op("pool", lambda e: e.tensor_tensor(out=flat(Bm[lv]), in0=flat(Ppb[0]), in1=bmk.t[:, 1 + lv, :], op=ALU.mult),
                      reads=[Ppb[0].r, bmk.r], writes=[Bm[lv].r])
                yield
            kb.op("pool", lambda e: e.tensor_tensor(out=flat(Pb[1]), in0=flat(Pb[0]), in1=bmk.t[:, 0, :], op=ALU.mult), reads=[Pb[0].r, bmk.r], writes=[Pb[1].r])
            kb.op("pool", lambda e: e.tensor_tensor(out=flat(Ppb[1]), in0=flat(Ppb[0]), in1=bmk.t[:, 0, :], op=ALU.mult), reads=[Ppb[0].r, bmk.r], writes=[Ppb[1].r])
            yield
            for h in range(HS):
                kb.op("pool", lambda e: e.tensor_tensor(out=Yf.t[:, h, :], in0=Ppb[1].t[:, h, :], in1=cst["ident_f"].t[:], op=ALU.add),
                      reads=[Ppb[1].r, cst["ident_f"].r], writes=[Yf.r])
            kb.op("act", lambda e: e.activation(out=flat(Yb), in_=flat(Yf), func=AF.Copy), reads=[Yf.r], writes=[Yb.r])
            yield
            for k in range(3):
                cur, nxt = 1 - (k % 2), k % 2
                kb.op("pe", mm4(PX, Ppb[cur], Pb[cur]), reads=[Ppb[cur].r, Pb[cur].r], writes=[PX.r])
                if k < 2:
                    kb.op("pe", mm4(PY, Pb[cur], Ppb[cur]), reads=[Ppb[cur].r, Pb[cur].r], writes=[PY.r])
                yield
                kb.op("act", lambda e: e.activation(out=flat(Pb[nxt]), in_=PX.t[:], func=AF.Copy), reads=[PX.r], writes=[Pb[nxt].r])
                if k < 2:
                    kb.op("dve", lambda e: e.tensor_copy(out=flat(Ppb[nxt]), in_=PY.t[:]), reads=[PY.r], writes=[Ppb[nxt].r])
                yield
                kb.op("pe", mm4(PX, Pb[nxt], Yb), reads=[Pb[nxt].r, Yb.r], writes=[PX.r])
                yield
                kb.op("dve", lambda e: e.tensor_tensor(out=flat(Yf), in0=PX.t[:], in1=flat(Yf), op=ALU.add), reads=[PX.r, Yf.r], writes=[Yf.r])
                yield
                kb.op("act", lambda e: e.activation(out=flat(Yb), in_=flat(Yf), func=AF.Copy), reads=[Yf.r], writes=[Yb.r])
                yield
            kb.op("pe", tr4(PY, Yb), reads=[Yb.r, cst["ident_bf"].r], writes=[PY.r])
            yield
            kb.op("dve", lambda e: e.tensor_copy(out=flat(Tf), in_=bfv(PY)[:, 0:W4]), reads=[PY.r], writes=[Tf.r])
            yield
            kb.op("act", lambda e: e.activation(out=flat(Tb), in_=flat(Tf), func=AF.Copy), reads=[Tf.r], writes=[Tb.r])
            yield
            for lv in range(3):
                kb.op("pe", mm4(PX, Bm[lv], Tb), reads=[Bm[lv].r, Tb.r], writes=[PX.r])
                yield
                kb.op("act", lambda e: e.activation(out=flat(NGb), in_=PX.t[:], func=AF.Copy), reads=[PX.r], writes=[NGb.r])
                yield
                kb.op("pe", mm4(PY, Yb, NGb), reads=[Yb.r, NGb.r], writes=[PY.r])
                yield
                kb.op("dve", lambda e: e.tensor_tensor(out=flat(Tf), in0=PY.t[:], in1=flat(Tf), op=ALU.add), reads=[PY.r, Tf.r], writes=[Tf.r])
                yield
                kb.op("act", lambda e: e.activation(out=flat(Tb), in_=flat(Tf), func=AF.Copy), reads=[Tf.r], writes=[Tb.r])
                yield
                kb.op("pe", tr4(PX, Tb), reads=[Tb.r, cst["ident_bf"].r], writes=[PX.r])
                yield
                kb.op("dve", lambda e: e.tensor_copy(out=flat(Yb), in_=bfv(PX)[:, 0:W4]), reads=[PX.r], writes=[Yb.r])
                yield
            kb.op("pe", mm4(PX, kbg, Yb), reads=[kbg.r, Yb.r], writes=[PX.r])
            yield
            kb.op("act", lambda e: e.activation(out=flat(nwT), in_=PX.t[:], func=AF.Copy, scale=-1.0), reads=[PX.r], writes=[nwT.r])
            yield

            def mmvn(e):
                for h in range(HS):
                    e.matmul(PY.t[:, hs(h)], Yb.t[:, h, :], vb.t[:, h, :], start=True, stop=False)
                    ins = e.matmul(PY.t[:, hs(h)], nwT.t[:, h, :], Sb.t[:, h, :], start=False, stop=True)
                return ins
            kb.op("pe", mmvn, reads=[Yb.r, vb.r, nwT.r, Sb.r], writes=[PY.r])
            yield
            kb.op("act", lambda e: e.activation(out=flat(vn), in_=PY.t[:], func=AF.Copy), reads=[PY.r], writes=[vn.r])
            kb.op("pe", mm4(PX, Q, Q, lsel=1, rsel=0), reads=[Q.r], writes=[PX.r])
            yield
            kb.op("dve", lambda e: e.tensor_tensor(out=flat(attnT), in0=PX.t[:], in1=flat(DTc), op=ALU.mult), reads=[PX.r, DTc.r], writes=[attnT.r])
            kb.op("pe", mm4(PY, Q, Sb, lsel=0), reads=[Q.r, Sb.r], writes=[PY.r])
            yield
            kb.op("pe", mm4(PX, attnT, vn), reads=[attnT.r, vn.r], writes=[PX.r])
            yield
            kb.op("act", lambda e: e.activation(out=flat(tav), in_=PX.t[:], func=AF.Copy), reads=[PX.r], writes=[tav.r])
            yield
            for h in range(HS):
                kb.op("dve", lambda e: e.scalar_tensor_tensor(out=ot.t[:, h, :], in0=PY.t[:, hs(h)], scalar=sm.t[:, 0, h:h + 1], op0=ALU.mult,
                                                               in1=tav.t[:, h, :], op1=ALU.add), reads=[PY.r, sm.r, tav.r], writes=[ot.r])
                yield
            kb.op("pe", mm4(PX, kdec, vn), reads=[kdec.r, vn.r], writes=[PX.r])
            yield
            for h in range(HS):
                kb.op("dve", lambda e: e.scalar_tensor_tensor(out=Sf.t[:, h, :], in0=Sf.t[:, h, :], scalar=sm.t[:, 1, h:h + 1], op0=ALU.mult,
                                                               in1=PX.t[:, hs(h)], op1=ALU.add), reads=[PX.r, sm.r, Sf.r], writes=[Sf.r])
                yield
            kb.op("act", lambda e: e.activation(out=flat(Sb), in_=flat(Sf), func=AF.Copy), reads=[Sf.r], writes=[Sb.r])
            yield
            onm = "OF" if dr == 0 else "OB"
            kb.dma("sp", S[onm][s][c0:c0 + 128, h0 * 128:(h0 + HS) * 128], flat(ot), reads=[ot.r], writes=kb.dres((onm + str(hh), s), c0, c0 + 128))
            yield

        def lockstep(gens):
            gens = list(gens)
            while gens:
                for g in list(gens):
                    try:
                        next(g)
                    except StopIteration:
                        gens.remove(g)

        for s, L in enumerate(seqs):
            NB = L // 128
            QKVv4 = S["QKV"][s].rearrange("(t h p) l -> p t h l", t=3, h=8, p=128)
            for dr in range(2):
                for hh in range(2):
                    kb.op("pool", lambda e: e.memset(flat(TL[dr][hh]["Sf"]), 0.0), writes=[TL[dr][hh]["Sf"].r])
                    kb.op("pool", lambda e: e.memset(flat(TL[dr][hh]["Sb"]), 0.0), writes=[TL[dr][hh]["Sb"].r])
            for it in range(NB):
                if GDN_SKIP_B:
                    break
                lockstep([half_iter(TL[dr][hh], hh, s, dr, it, (it if dr == 0 else NB - 1 - it), QKVv4, None)
                          for dr in range(2) for hh in range(2)])
        kb.barrier()
    with contextlib.ExitStack() as st:
        nw = kb.sb(st, "gnw2", [128, 1024], F32)
        kb.dma("sp", nw.t[:], P["a_norm_w"][ja], writes=[nw.r])
        NCH = 3

        def talloc(i):
            t = {}
            t["of"] = kb.sb(st, "gtof%d" % i, [128, 8, 128], F32)
            t["ob"] = kb.sb(st, "gtob%d" % i, [128, 8, 128], F32)
            t["zl"] = kb.sb(st, "gtzl%d" % i, [128, 1024], BF16)
            t["sq"] = kb.sb(st, "gtsq%d" % i, [128, 8, 128], F32)
            t["utok"] = kb.sb(st, "gtut%d" % i, [128, 8, 128], BF16)
            t["uT"] = kb.sb(st, "gtuT%d" % i, [128, 8, 128], BF16)
            t["ss8"] = kb.sb(st, "gtss%d" % i, [128, 8], F32)
            t["rs8"] = kb.sb(st, "gtrs%d" % i, [128, 8], F32)
            t["PK"] = kb.ps(st, "gtPK%d" % i, [128, 1024], BF16)
            return t
        TT = [talloc(i) for i in range(NCH)]

        def flat(t):
            return t.t[:].rearrange("p h m -> p (h m)")

        def tail(t, s, c0, Uv):
            of, ob, zl, sq, utok, uT, ss8, rs8, PK = t["of"], t["ob"], t["zl"], t["sq"], t["utok"], t["uT"], t["ss8"], t["rs8"], t["PK"]
            kb.dma("sp", flat(of), S["OF"][s][c0:c0 + 128, :], reads=kb.dres(("OF0", s), c0, c0 + 128) + kb.dres(("OF1", s), c0, c0 + 128), writes=[of.r])
            kb.dma("sp", flat(ob), S["OB"][s][c0:c0 + 128, :], reads=kb.dres(("OB0", s), c0, c0 + 128) + kb.dres(("OB1", s), c0, c0 + 128), writes=[ob.r])
            kb.dma("sp", zl.t[:], S["Z"][s][c0:c0 + 128, :], reads=kb.dres(("Z", s), c0, c0 + 128), writes=[zl.r])
            yield
            kb.op("pool", lambda e: e.tensor_tensor(out=flat(of), in0=flat(of), in1=flat(ob), op=ALU.add), reads=[of.r, ob.r], writes=[of.r])
            yield
            kb.op("act", lambda e: e.activation(out=flat(sq), in_=flat(of), func=AF.Square), reads=[of.r], writes=[sq.r])
            yield
            kb.op("dve", lambda e: e.tensor_reduce(out=ss8.t[:], in_=sq.t[:], axis=AX.X, op=ALU.add), reads=[sq.r], writes=[ss8.r])
            yield
            kb.op("act", lambda e: e.activation(out=rs8.t[:], in_=ss8.t[:], func=AF.Sqrt, bias=cst["eps"].t[:, 0:1], scale=1.0 / 128.0),
                  reads=[ss8.r, cst["eps"].r], writes=[rs8.r])
            yield
            kb.op("dve", lambda e: e.reciprocal(out=rs8.t[:], in_=rs8.t[:]), reads=[rs8.r], writes=[rs8.r])
            yield
            for h in range(8):
                kb.op("dve", lambda e: e.scalar_tensor_tensor(out=sq.t[:, h, :], in0=of.t[:, h, :], scalar=rs8.t[:, h:h + 1], op0=ALU.mult,
                                                               in1=nw.t[:, h * 128:(h + 1) * 128], op1=ALU.mult), reads=[of.r, rs8.r, nw.r, sq.r], writes=[sq.r])
                yield
            kb.op("pool", lambda e: e.tensor_tensor(out=flat(utok), in0=flat(sq), in1=zl.t[:], op=ALU.mult), reads=[sq.r, zl.r], writes=[utok.r])
            yield

            def tr(e):
                for h in range(8):
                    ins = e.transpose(out=PK.t[:, h * 128:(h + 1) * 128], in_=utok.t[:, h, :], identity=cst["ident_bf"].t[:])
                return ins
            kb.op("pe", tr, reads=[utok.r, cst["ident_bf"].r], writes=[PK.r])
            yield
            kb.op("act", lambda e: e.activation(out=flat(uT), in_=PK.t[:], func=AF.Copy), reads=[PK.r], writes=[uT.r])
            yield
            kb.dma("sp", Uv[:, 0:8, c0:c0 + 128], uT.t[:], reads=[uT.r], writes=kb.dres(("U", s), c0, c0 + 128))
            yield

        def lockstep2(gens):
            gens = list(gens)
            while gens:
                for g in list(gens):
                    try:
                        next(g)
                    except StopIteration:
                        gens.remove(g)
        for s, L in enumerate(seqs):
            Uv = S["U"][s].rearrange("(k p) l -> p k l", p=128)
            blocks = list(range(0, L, 128))
            for g0 in range(0, len(blocks), NCH):
                lockstep2([tail(TT[i], s, c0, Uv) for i, c0 in enumerate(blocks[g0:g0 + NCH])])
        kb.barrier()
    outproj_phase(kb, cst, P["a_w_out"][ja], None, S["U"], "U", 1024, X, seqs, layer)


TWO_PI = 2.0 * math.pi


def cast_load(kb, wl, dst2d, src2d, res, rows, width, chunk=1024):
    for c0 in range(0, width, chunk):
        c1 = min(width, c0 + chunk)
        wl.load(dst2d[:, c0:c1], src2d[:, c0:c1], res, c1 - c0, rows=rows)


def hy_tables(kb, st, P, L):
    N1 = 2 * L // 64
    wl = WLoader(kb, st, 1024, nbuf=2)
    tb = {}
    tb["f1t"] = kb.sb(st, "hf1t", [N1, 2 * N1], BF16)
    cast_load(kb, wl, tb["f1t"].t[:, :], P["hy_f1t_%d" % L][:, :], tb["f1t"].r, N1, 2 * N1)
    tb["tw"] = kb.sb(st, "htw", [64, N1 * 192], BF16)
    cast_load(kb, wl, tb["tw"].t[:, :], P["hy_tw_%d" % L][:, :], tb["tw"].r, 64, N1 * 192)
    tb["twt"] = kb.sb(st, "htwt", [64, N1 * 192], BF16)
    cast_load(kb, wl, tb["twt"].t[:, :], P["hy_twt_%d" % L][:, :], tb["twt"].r, 64, N1 * 192)
    tb["g2t"] = kb.sb(st, "hg2t", [N1, N1], BF16)
    cast_load(kb, wl, tb["g2t"].t[:, :], P["hy_g2t_%d" % L][:, :], tb["g2t"].r, N1, N1)
    return tb


def hy_F1(kb, tb, Q, ab, srcv, K, N1, C, D1, d1name):
    xt = tb["xt"]
    k = 0
    for n2 in range(64):
        for cg in range(C // 1024):
            b = k % 2
            k += 1
            kb.dma("sp", xt[b].t[:K, :], srcv[0:K, n2, cg * 1024:(cg + 1) * 1024], reads=tb["src_res"], writes=[xt[b].r])
            for ri in range(2):
                def mm(e, ri=ri, b=b):
                    for hf in range(2):
                        ins = e.matmul(Q[ri].t[:N1, hf * 512:(hf + 1) * 512], tb["f1t"].t[:K, ri * N1:(ri + 1) * N1], xt[b].t[:K, hf * 512:(hf + 1) * 512],
                                       start=True, stop=True)
                    return ins
                kb.op("pe", mm, reads=[tb["f1t"].r, xt[b].r], writes=[Q[ri].r])
            kb.op("act", lambda e, b=b: e.activation(out=ab[b].t[:N1, 0, :], in_=Q[0].t[:N1, :], func=AF.Copy), reads=[Q[0].r], writes=[ab[b].r])
            kb.op("dve", lambda e, b=b: e.tensor_copy(out=ab[b].t[:N1, 1, :], in_=Q[1].t[:N1, :]), reads=[Q[1].r], writes=[ab[b].r])
            kb.dma("sp", D1[0:N1, n2, :, cg * 1024:(cg + 1) * 1024], ab[b].t[:N1, :, :], reads=[ab[b].r], writes=kb.dres(d1name, 0, 128))


def hy_F2(kb, tb, Q, N1, C, D1, d1name, mode, HS=None, hsname=None, order=0, D2=None, d2name=None):
    a2, hb, xr, xi, tt, yb, zb = tb["a2"], tb["hb"], tb["xr"], tb["xi"], tb["tt"], tb["yb"], tb["zb"]
    tw, twt = tb["tw"], tb["twt"]
    k = 0
    for k1 in range(N1):
        for cg in range(C // 1024):
            b = k % 2
            k += 1
            kb.dma("sp", a2[b].t[:, :, :], D1[k1, :, :, cg * 1024:(cg + 1) * 1024], reads=kb.dres(d1name, 0, 128), writes=[a2[b].r])

            def tws(j):
                return tw.t[:, (k1 * 3 + j) * 64:(k1 * 3 + j + 1) * 64]

            def twts(j):
                return twt.t[:, (k1 * 3 + j) * 64:(k1 * 3 + j + 1) * 64]

            def mmre(e, b=b):
                for hf in range(2):
                    cs = slice(hf * 512, (hf + 1) * 512)
                    e.matmul(Q[0].t[:64, cs], tws(0), a2[b].t[:, 0, cs], start=True, stop=False)
                    ins = e.matmul(Q[0].t[:64, cs], tws(2), a2[b].t[:, 1, cs], start=False, stop=True)
                return ins

            def mmim(e, b=b):
                for hf in range(2):
                    cs = slice(hf * 512, (hf + 1) * 512)
                    e.matmul(Q[1].t[:64, cs], tws(1), a2[b].t[:, 0, cs], start=True, stop=False)
                    ins = e.matmul(Q[1].t[:64, cs], tws(0), a2[b].t[:, 1, cs], start=False, stop=True)
                return ins
            kb.op("pe", mmre, reads=[tw.r, a2[b].r], writes=[Q[0].r])
            kb.op("pe", mmim, reads=[tw.r, a2[b].r], writes=[Q[1].r])
            if mode == "H":
                kb.op("act", lambda e, b=b: e.activation(out=hb[b].t[:, 0, :], in_=Q[0].t[:64, :], func=AF.Copy), reads=[Q[0].r], writes=[hb[b].r])
                kb.op("dve", lambda e, b=b: e.tensor_copy(out=hb[b].t[:, 1, :], in_=Q[1].t[:64, :]), reads=[Q[1].r], writes=[hb[b].r])
                kb.dma("sp", HS[k1, :, :, cg * 1024:(cg + 1) * 1024], hb[b].t[:, :, :], reads=[hb[b].r], writes=kb.dres(hsname, 0, 128))
                continue
            kb.dma("sp", hb[b].t[:, :, :], HS[k1, :, :, order * 1024:(order + 1) * 1024], reads=kb.dres(hsname, 0, 128), writes=[hb[b].r])
            kb.op("act", lambda e: e.activation(out=xr.t[:, :], in_=Q[0].t[:64, :], func=AF.Copy), reads=[Q[0].r], writes=[xr.r])
            kb.op("dve", lambda e: e.tensor_copy(out=xi.t[:, :], in_=Q[1].t[:64, :]), reads=[Q[1].r], writes=[xi.r])
            kb.op("dve", lambda e, b=b: e.tensor_tensor(out=tt[0].t[:, :], in0=xr.t[:, :], in1=hb[b].t[:, 0, :], op=ALU.mult), reads=[xr.r, hb[b].r], writes=[tt[0].r])
            kb.op("pool", lambda e, b=b: e.tensor_tensor(out=tt[1].t[:, :], in0=xi.t[:, :], in1=hb[b].t[:, 1, :], op=ALU.mult), reads=[xi.r, hb[b].r], writes=[tt[1].r])
            kb.op("dve", lambda e, b=b: e.tensor_tensor(out=yb[b].t[:, 0, :], in0=tt[0].t[:, :], in1=tt[1].t[:, :], op=ALU.subtract), reads=[tt[0].r, tt[1].r], writes=[yb[b].r])
            kb.op("pool", lambda e, b=b: e.tensor_tensor(out=tt[1].t[:, :], in0=xr.t[:, :], in1=hb[b].t[:, 1, :], op=ALU.mult), reads=[xr.r, hb[b].r], writes=[tt[1].r])
            kb.op("dve", lambda e, b=b: e.tensor_tensor(out=tt[0].t[:, :], in0=xi.t[:, :], in1=hb[b].t[:, 0, :], op=ALU.mult), reads=[xi.r, hb[b].r], writes=[tt[0].r])
            kb.op("pool", lambda e, b=b: e.tensor_tensor(out=yb[b].t[:, 1, :], in0=tt[0].t[:, :], in1=tt[1].t[:, :], op=ALU.add), reads=[tt[0].r, tt[1].r], writes=[yb[b].r])

            def mzre(e, b=b):
                for hf in range(2):
                    cs = slice(hf * 512, (hf + 1) * 512)
                    e.matmul(Q[2].t[:64, cs], twts(0), yb[b].t[:, 0, cs], start=True, stop=False)
                    ins = e.matmul(Q[2].t[:64, cs], twts(1), yb[b].t[:, 1, cs], start=False, stop=True)
                return ins

            def mzim(e, b=b):
                for hf in range(2):
                    cs = slice(hf * 512, (hf + 1) * 512)
                    e.matmul(Q[3].t[:64, cs], twts(0), yb[b].t[:, 1, cs], start=True, stop=False)
                    ins = e.matmul(Q[3].t[:64, cs], twts(2), yb[b].t[:, 0, cs], start=False, stop=True)
                return ins
            kb.op("pe", mzre, reads=[twt.r, yb[b].r], writes=[Q[2].r])
            kb.op("pe", mzim, reads=[twt.r, yb[b].r], writes=[Q[3].r])
            kb.op("act", lambda e, b=b: e.activation(out=zb[b].t[:, 0, :], in_=Q[2].t[:64, :], func=AF.Copy), reads=[Q[2].r], writes=[zb[b].r])
            kb.op("act", lambda e, b=b: e.activation(out=zb[b].t[:, 1, :], in_=Q[3].t[:64, :], func=AF.Copy), reads=[Q[3].r], writes=[zb[b].r])
            kb.dma("sp", D2[k1, :, :, :], zb[b].t[:, :, :], reads=[zb[b].r], writes=kb.dres(d2name, 0, 128))


def hyena_stage(kb, cst, P, S, X, seqs, layer):
    nc = kb.nc
    for L in sorted(set(seqs)):
        N = 2 * L
        N1 = N // 64
        NR = N // 128
        with contextlib.ExitStack() as st:
            w1 = kb.sb(st, "hw1", [33, 64], F32)
            w2 = kb.sb(st, "hw2", [64, 64], F32)
            w3 = kb.sb(st, "hw3", [64, 4096], F32)
            prm = kb.sb(st, "hprm", [64, 4], F32)
            bf = kb.sb(st, "hbf", [64, 2], F32)
            kb.dma("sp", w1.t[:], P["c_f_w1"][:, :], writes=[w1.r])
            kb.dma("sp", w2.t[:], P["c_f_w2"][:, :], writes=[w2.r])
            kb.dma("sp", w3.t[:], P["c_f_w3"][:, :], writes=[w3.r])
            kb.dma("sp", prm.t[:], P["c_f_prm"][:, :], writes=[prm.r])
            kb.op("dve", lambda e: e.tensor_tensor(out=bf.t[:, 0:1], in0=prm.t[:, 0:1], in1=prm.t[:, 1:2], op=ALU.mult), reads=[prm.r], writes=[bf.r])
            kb.op("dve", lambda e: e.tensor_tensor(out=bf.t[:, 1:2], in0=prm.t[:, 2:3], in1=prm.t[:, 3:4], op=ALU.mult), reads=[prm.r, bf.r], writes=[bf.r])
            zt = kb.sb(st, "hzt", [33, 512], F32)
            aa = kb.sb(st, "haa", [64, 512], F32)
            m1 = kb.sb(st, "hm1", [64, 512], F32)
            m2 = kb.sb(st, "hm2", [64, 512], F32)
            h1 = kb.sb(st, "hh1", [64, 512], F32)
            h2 = kb.sb(st, "hh2", [64, N], F32)
            ntp = kb.sb(st, "hntp", [128, NR], F32)
            dlt = kb.sb(st, "hdlt", [128, 1024], F32)
            win = kb.sb(st, "hwin", [128, 1024], F32)
            fl = [kb.sb(st, "hfl%d" % i, [128, 2048], BF16) for i in range(2)]
            kb.dma("sp", ntp.t[:], P["hy_ntp_%d" % L][:, :], writes=[ntp.r])
            kb.dma("sp", dlt.t[:], P["hy_delta"][:, :], writes=[dlt.r])
            Q = [kb.ps(st, "hQ%d" % i, [128, 1024]) for i in range(4)]

            def sin_layer(ps, col_f, col_b, dst):
                kb.op("act", lambda e: e.activation(out=aa.t[:, :], in_=ps.t[:64, 0:512], func=AF.Identity, bias=bf.t[:, col_b:col_b + 1], scale=prm.t[:, col_f:col_f + 1]),
                      reads=[ps.r, bf.r, prm.r], writes=[aa.r])
                kb.op("dve", lambda e: e.tensor_scalar(out=m1.t[:, :], in0=aa.t[:, :], scalar1=math.pi, scalar2=-TWO_PI, op0=ALU.is_gt, op1=ALU.mult), reads=[aa.r], writes=[m1.r])
                kb.op("dve", lambda e: e.tensor_scalar(out=m2.t[:, :], in0=aa.t[:, :], scalar1=-math.pi, scalar2=TWO_PI, op0=ALU.is_lt, op1=ALU.mult), reads=[aa.r], writes=[m2.r])
                kb.op("dve", lambda e: e.tensor_tensor(out=m1.t[:, :], in0=m1.t[:, :], in1=m2.t[:, :], op=ALU.add), reads=[m1.r, m2.r], writes=[m1.r])
                kb.op("dve", lambda e: e.tensor_tensor(out=aa.t[:, :], in0=aa.t[:, :], in1=m1.t[:, :], op=ALU.add), reads=[aa.r, m1.r], writes=[aa.r])
                kb.op("act", lambda e: e.activation(out=dst, in_=aa.t[:, :], func=AF.Sin), reads=[aa.r], writes=[h1.r, h2.r])

            for c0 in range(0, N, 512):
                kb.dma("sp", zt.t[:, :], P["hy_z_%d" % L][:, c0:c0 + 512], writes=[zt.r])
                kb.op("pe", lambda e: e.matmul(Q[0].t[:64, 0:512], w1.t[:, :], zt.t[:, :], start=True, stop=True), reads=[w1.r, zt.r], writes=[Q[0].r])
                sin_layer(Q[0], 1, 0, h1.t[:, :])
                kb.op("pe", lambda e: e.matmul(Q[1].t[:64, 0:512], w2.t[:, :], h1.t[:, :], start=True, stop=True), reads=[w2.r, h1.r], writes=[Q[1].r])
                sin_layer(Q[1], 3, 1, h2.t[:, c0:c0 + 512])
            FILT = S["FILT"][L]
            for r in range(NR):
                b = r % 2
                dsel = 0 if r * 128 < L else 1
                kb.op("act", lambda e, r=r: e.activation(out=win.t[:, :], in_=dlt.t[:, :], func=AF.Exp, scale=ntp.t[:, r:r + 1]), reads=[dlt.r, ntp.r], writes=[win.r])
                for q in range(4):
                    pq = Q[q % 2 + 2]
                    kb.op("pe", lambda e, q=q, pq=pq, r=r: e.matmul(pq.t[:, 0:512], h2.t[:, r * 128:(r + 1) * 128], w3.t[:, dsel * 2048 + q * 512:dsel * 2048 + (q + 1) * 512],
                                                                     start=True, stop=True), reads=[h2.r, w3.r], writes=[pq.r])
                    kb.op("dve", lambda e, q=q, pq=pq, b=b: e.tensor_tensor(out=fl[b].t[:, q * 512:(q + 1) * 512], in0=pq.t[:, 0:512], in1=win.t[:, (q % 2) * 512:(q % 2 + 1) * 512], op=ALU.mult),
                          reads=[pq.r, win.r], writes=[fl[b].r])
                kb.dma("sp", FILT[r * 128:(r + 1) * 128, :], fl[b].t[:, :], reads=[fl[b].r], writes=kb.dres(("FILT", L), 0, 128))
            kb.barrier()
        with contextlib.ExitStack() as st:
            tb = hy_tables(kb, st, P, L)
            hy_work_tiles(kb, st, tb)
            Q = [kb.ps(st, "hQ%d" % i, [128, 1024]) for i in range(4)]
            tb["src_res"] = kb.dres(("FILT", L), 0, 128)
            hy_F1(kb, tb, Q, tb["ab"], S["FILT"][L].rearrange("(a b) c -> a b c", b=64), N1, N1, 2048, S["D1F"][L], ("D1F", L))
            hy_F2(kb, tb, Q, N1, 2048, S["D1F"][L], ("D1F", L), "H", HS=S["HS"][L], hsname=("HS", L))
            kb.barrier()
    with contextlib.ExitStack() as st:
        wi = kb.sb(st, "hwi", [128, 8, 3072], BF16, nres=8)
        wl = WLoader(kb, st, 3072, nbuf=2)
        for c in range(8):
            wl.load(wi.t[:, c, :], P["c_w_in"][c * 128:(c + 1) * 128, :], wi.rs[c], 3072)
        T = 256
        nrm = NormCtx(kb, st, cst, T)
        xt = kb.sb(st, "hxt", [128, 8, T], F32)
        ht = kb.sb(st, "hht", [128, 8, T], BF16)
        binb = kb.sb(st, "hbinb", [128, 3072], F32)
        kb.dma("sp", binb.t[:], P["c_b_in"][:, :], writes=[binb.r])
        pt = [kb.sb(st, "hpt%d" % i, [128, 3072], F32) for i in range(2)]
        zr = kb.sb(st, "hzr", [1, 3072], F32)
        kb.op("pool", lambda e: e.memset(zr.t[:], 0.0), writes=[zr.r])
        psp = [kb.ps(st, "hpsp%d" % i, [128, 512]) for i in range(2)]
        for s, L in enumerate(seqs):
            Xv = X[s].rearrange("(c p) l -> p c l", p=128)
            PP = S["PP"][s]
            kb.dma("sp", PP[0:1, :], zr.t[:], reads=[zr.r], writes=kb.dres(("PP", s), 0, 1))
            kb.dma("sp", PP[L + 1:L + 2, :], zr.t[:], reads=[zr.r], writes=kb.dres(("PP", s), L + 1, L + 2))
            for t0 in range(0, L, T):
                n = min(T, L - t0)
                kb.dma("sp", xt.t[:, :, :n], Xv[:, :, t0:t0 + n], reads=kb.dres(("X", s), t0, t0 + n), writes=[xt.r])
                nrm.prenorm(xt, ht, n, layer, 2)
                for tb_i in range(n // 128):
                    bb = tb_i % 2
                    for cb in range(6):
                        pb = cb % 2

                        def mmp(e, cb=cb, pb=pb, tb_i=tb_i):
                            for c in range(8):
                                ins = e.matmul(psp[pb].t[:, :], ht.t[:, c, tb_i * 128:(tb_i + 1) * 128], wi.t[:, c, cb * 512:(cb + 1) * 512],
                                               start=(c == 0), stop=(c == 7))
                            return ins
                        kb.op("pe", mmp, reads=wi.rs + [ht.r], writes=[psp[pb].r])
                        kb.op("dve", lambda e, cb=cb, pb=pb, bb=bb: e.tensor_tensor(out=pt[bb].t[:, cb * 512:(cb + 1) * 512], in0=psp[pb].t[:, :],
                                                                                     in1=binb.t[:, cb * 512:(cb + 1) * 512], op=ALU.add),
                              reads=[psp[pb].r, binb.r], writes=[pt[bb].r])
                    r0 = t0 + tb_i * 128
                    kb.dma("sp", PP[1 + r0:1 + r0 + 128, :], pt[bb].t[:, :], reads=[pt[bb].r], writes=kb.dres(("PP", s), 1 + r0, 1 + r0 + 128))
        kb.barrier()
    with contextlib.ExitStack() as st:
        cwb = kb.sb(st, "hcwb", [128, 4, 3072], F32)
        for i in range(4):
            kb.dma("sp", cwb.t[:, i, :], P["c_conv"][:, i, :], writes=[cwb.r])
        pa = [kb.sb(st, "hpa%d" % i, [128, 3072], F32) for i in range(3)]
        acc = kb.sb(st, "hacc", [128, 3072], F32)
        acc2 = kb.sb(st, "hacc2", [128, 3072], F32)
        ob = kb.sb(st, "hob", [128, 3072], BF16)
        for s, L in enumerate(seqs):
            PP = S["PP"][s]
            for r0 in range(0, L, 128):
                for i in range(3):
                    kb.dma("sp", pa[i].t[:, :], PP[r0 + i:r0 + i + 128, :], reads=kb.dres(("PP", s), r0 + i, r0 + i + 128), writes=[pa[i].r])
                kb.op("dve", lambda e: e.tensor_tensor(out=acc.t[:, :], in0=pa[0].t[:, :], in1=cwb.t[:, 0, :], op=ALU.mult), reads=[pa[0].r, cwb.r], writes=[acc.r])
                kb.op("pool", lambda e: e.tensor_tensor(out=acc2.t[:, :], in0=pa[1].t[:, :], in1=cwb.t[:, 1, :], op=ALU.mult), reads=[pa[1].r, cwb.r], writes=[acc2.r])
                kb.op("dve", lambda e: e.tensor_tensor(out=acc.t[:, :], in0=acc.t[:, :], in1=cwb.t[:, 3, :], op=ALU.add), reads=[acc.r, cwb.r], writes=[acc.r])
                kb.op("pool", lambda e: e.tensor_tensor(out=pa[2].t[:, :], in0=pa[2].t[:, :], in1=cwb.t[:, 2, :], op=ALU.mult), reads=[pa[2].r, cwb.r], writes=[pa[2].r])
                kb.op("dve", lambda e: e.tensor_tensor(out=acc.t[:, :], in0=acc.t[:, :], in1=acc2.t[:, :], op=ALU.add), reads=[acc.r, acc2.r], writes=[acc.r])
                kb.op("pool", lambda e: e.tensor_tensor(out=ob.t[:, :], in0=acc.t[:, :], in1=pa[2].t[:, :], op=ALU.add), reads=[acc.r, pa[2].r], writes=[ob.r])
                kb.dma("sp", S["HV"][s][0][r0:r0 + 128, :], ob.t[:, 0:1024], reads=[ob.r], writes=kb.dres(("HV0", s), r0, r0 + 128))
                kb.dma("sp", S["HG"][s][r0:r0 + 128, :], ob.t[:, 1024:3072], reads=[ob.r], writes=kb.dres(("HG", s), r0, r0 + 128))
        kb.barrier()
    for L in sorted(set(seqs)):
        N1 = 2 * L // 64
        K = N1 // 2
        with contextlib.ExitStack() as st:
            tb = hy_tables(kb, st, P, L)
            hy_work_tiles(kb, st, tb)
            Q = [kb.ps(st, "hQ%d" % i, [128, 1024]) for i in range(4)]
            skb = kb.sb(st, "hskb", [128, 2, 1024], F32)
            kb.dma("sp", skb.t[:], P["c_bias_d"][:, :, :], writes=[skb.r])
            zt = [kb.sb(st, "hzt%d" % i, [128, 2, 1024], BF16) for i in range(2)]
            ut = [kb.sb(st, "hut%d" % i, [64, 1024], BF16) for i in range(2)]
            gt = [kb.sb(st, "hgt%d" % i, [64, 1024], BF16) for i in range(2)]
            t1 = kb.sb(st, "ht1", [64, 1024], F32)
            t2 = kb.sb(st, "ht2", [64, 1024], F32)
            zo = [kb.sb(st, "hzo%d" % i, [64, 1024], BF16) for i in range(2)]
            for s, Ls in enumerate(seqs):
                if Ls != L:
                    continue
                for order in range(2):
                    Uin = S["HV"][s][order]
                    Uout = S["HV"][s][order + 1]
                    uin_name, uout_name = ("HV%d" % order, s), ("HV%d" % (order + 1), s)
                    Uiv = Uin.rearrange("(a b) c -> a b c", b=64)
                    Uov = Uout.rearrange("(a b) c -> a b c", b=64)
                    Gv = S["HG"][s].rearrange("(a b) c -> a b c", b=64)
                    tb["src_res"] = kb.dres(uin_name, 0, L)
                    hy_F1(kb, tb, Q, tb["ab"], Uiv, K, N1, 1024, S["D1"], "D1")
                    hy_F2(kb, tb, Q, N1, 1024, S["D1"], "D1", "conv", HS=S["HS"][L], hsname=("HS", L), order=order, D2=S["D2"], d2name="D2")
                    for n2 in range(64):
                        b = n2 % 2
                        kb.dma("sp", zt[b].t[:N1, :, :], S["D2"][0:N1, n2, :, :], reads=kb.dres("D2", 0, 128), writes=[zt[b].r])
                        kb.dma("sp", ut[b].t[:K, :], Uiv[0:K, n2, :], reads=kb.dres(uin_name, 0, L), writes=[ut[b].r])
                        kb.dma("sp", gt[b].t[:K, :], Gv[0:K, n2, order * 1024:(order + 1) * 1024], reads=kb.dres(("HG", s), 0, L), writes=[gt[b].r])

                        def mmy(e, b=b):
                            for hf in range(2):
                                cs = slice(hf * 512, (hf + 1) * 512)
                                e.matmul(Q[0].t[:K, cs], tb["g2t"].t[:N1, 0:K], zt[b].t[:N1, 0, cs], start=True, stop=False)
                                ins = e.matmul(Q[0].t[:K, cs], tb["g2t"].t[:N1, K:2 * K], zt[b].t[:N1, 1, cs], start=False, stop=True)
                            return ins
                        kb.op("pe", mmy, reads=[tb["g2t"].r, zt[b].r], writes=[Q[0].r])
                        kb.op("pool", lambda e, b=b: e.tensor_tensor(out=t1.t[:K, :], in0=ut[b].t[:K, :], in1=skb.t[:K, order, :], op=ALU.mult), reads=[ut[b].r, skb.r], writes=[t1.r])
                        kb.op("dve", lambda e: e.tensor_tensor(out=t2.t[:K, :], in0=Q[0].t[:K, :], in1=t1.t[:K, :], op=ALU.add), reads=[Q[0].r, t1.r], writes=[t2.r])
                        kb.op("pool", lambda e, b=b: e.tensor_tensor(out=zo[b].t[:K, :], in0=t2.t[:K, :], in1=gt[b].t[:K, :], op=ALU.mult), reads=[t2.r, gt[b].r], writes=[zo[b].r])
                        kb.dma("sp", Uov[0:K, n2, :], zo[b].t[:K, :], reads=[zo[b].r], writes=kb.dres(uout_name, 0, L))
            kb.barrier()
    with contextlib.ExitStack() as st:
        zl = [kb.sb(st, "hzl%d" % i, [128, 8, 128], BF16) for i in range(2)]
        uT = [kb.sb(st, "huT%d" % i, [128, 8, 128], BF16) for i in range(2)]
        PK = kb.ps(st, "hPK", [128, 1024], BF16)
        for s, L in enumerate(seqs):
            Uv = S["U"][s].rearrange("(k p) l -> p k l", p=128)
            for i, r0 in enumerate(range(0, L, 128)):
                b = i % 2
                kb.dma("sp", zl[b].t[:].rearrange("p h m -> p (h m)"), S["HV"][s][2][r0:r0 + 128, :], reads=kb.dres(("HV2", s), 0, L), writes=[zl[b].r])

                def tr(e, b=b):
                    for h in range(8):
                        ins = e.transpose(out=PK.t[:, h * 128:(h + 1) * 128], in_=zl[b].t[:, h, :], identity=cst["ident_bf"].t[:])
                    return ins
                kb.op("pe", tr, reads=[zl[b].r, cst["ident_bf"].r], writes=[PK.r])
                kb.op("act", lambda e, b=b: e.activation(out=uT[b].t[:].rearrange("p h m -> p (h m)"), in_=PK.t[:], func=AF.Copy), reads=[PK.r], writes=[uT[b].r])
                kb.dma("sp", Uv[:, 0:8, r0:r0 + 128], uT[b].t[:], reads=[uT[b].r], writes=kb.dres(("U", s), r0, r0 + 128))
        kb.barrier()
    with contextlib.ExitStack() as st:
        bo = kb.sb(st, "hbo", [128, 8], F32)
        kb.dma("sp", bo.t[:], P["c_b_out"][:, :], writes=[bo.r])
        outproj_phase(kb, cst, P["c_w_out"], bo, S["U"], "U", 1024, X, seqs, layer)


def hy_work_tiles(kb, st, tb):
    tb["xt"] = [kb.sb(st, "hwxt%d" % i, [128, 1024], BF16) for i in range(2)]
    tb["ab"] = [kb.sb(st, "hwab%d" % i, [128, 2, 1024], BF16) for i in range(2)]
    tb["a2"] = [kb.sb(st, "hwa2%d" % i, [64, 2, 1024], BF16) for i in range(2)]
    tb["hb"] = [kb.sb(st, "hwhb%d" % i, [64, 2, 1024], BF16) for i in range(2)]
    tb["xr"] = kb.sb(st, "hwxr", [64, 1024], F32)
    tb["xi"] = kb.sb(st, "hwxi", [64, 1024], F32)
    tb["tt"] = [kb.sb(st, "hwtt%d" % i, [64, 1024], F32) for i in range(2)]
    tb["yb"] = [kb.sb(st, "hwyb%d" % i, [64, 2, 1024], BF16) for i in range(2)]
    tb["zb"] = [kb.sb(st, "hwzb%d" % i, [64, 2, 1024], BF16) for i in range(2)]


def copy_stage(kb, Xin, X, seqs):
    with contextlib.ExitStack() as st:
        xt = [kb.sb(st, "cxt%d" % i, [128, 8, 512], F32) for i in range(2)]
        k = 0
        for s, L in enumerate(seqs):
            Xv = X[s].rearrange("(c p) l -> p c l", p=128)
            Xiv = Xin[s].rearrange("(c p) l -> p c l", p=128)
            for t0 in range(0, L, 512):
                n = min(512, L - t0)
                b = k % 2
                k += 1
                kb.dma("sp", xt[b].t[:, :, :n], Xiv[:, :, t0:t0 + n], writes=[xt[b].r])
                kb.dma("sp", Xv[:, :, t0:t0 + n], xt[b].t[:, :, :n], reads=[xt[b].r], writes=kb.dres(("X", s), t0, t0 + n))
        kb.barrier()


def build(seqs, stages):
    nc = bass.Bass("TRN2", target_bir_lowering=False)
    P = {}

    def din(name, shape):
        P[name] = nc.dram_tensor(name, list(shape), F32, kind="ExternalInput").ap()

    X = []
    for s, L in enumerate(seqs):
        X.append(nc.dram_tensor("xin%d" % s, [D, L], F32, kind="ExternalInput").ap())
    Y = []
    for s, L in enumerate(seqs):
        Y.append(nc.dram_tensor("yout%d" % s, [D, L], F32, kind="ExternalOutput").ap())
    din("norm_g", [128, 192])
    din("ffn_w_gate", [8, D, DFF])
    din("ffn_w_up", [8, D, DFF])
    din("ffn_w_down", [8, DFF, D])
    din("ident", [128, 128])
    din("rope", [128, 4, 4096])
    din("dmat", [128, 128])
    din("tri", [128, 2, 128])
    din("dvals", [128, 32])
    din("b_w_in", [D, 6144])
    din("b_decay_logit", [128, 8])
    din("b_gn_w", [128, 2048])
    din("b_w_out", [2048, D])
    din("a_w_in", [2, D, 4128])
    din("a_conv_w", [2, 128, 72])
    din("a_a_log", [2, 128, 16])
    din("a_dt_bias", [2, 128, 16])
    din("a_norm_w", [2, 128, 1024])
    din("a_w_out", [2, D, D])
    din("gmask", [128, 4, 128])
    din("negm", [128, 4, 1024])
    din("bmask", [128, 4, 1024])
    din("c_w_in", [D, 3072])
    din("c_b_in", [128, 3072])
    din("c_conv", [128, 4, 3072])
    din("c_f_w1", [33, 64])
    din("c_f_w2", [64, 64])
    din("c_f_w3", [64, 4096])
    din("c_f_prm", [64, 4])
    din("c_bias_d", [128, 2, 1024])
    din("c_w_out", [D, D])
    din("c_b_out", [128, 8])
    din("hy_delta", [128, 1024])
    S = {k: [] for k in ("QK", "V", "G", "U", "QKV", "Z", "GB", "OF", "OB", "PP", "HV", "HG")}
    S["FILT"], S["D1F"], S["HS"] = {}, {}, {}
    for L in sorted(set(seqs)):
        N1 = 2 * L // 64
        din("hy_f1t_%d" % L, [N1, 2 * N1])
        din("hy_tw_%d" % L, [64, N1 * 192])
        din("hy_twt_%d" % L, [64, N1 * 192])
        din("hy_g2t_%d" % L, [N1, N1])
        din("hy_z_%d" % L, [33, 2 * L])
        din("hy_ntp_%d" % L, [128, 2 * L // 128])
        S["FILT"][L] = _scr(nc, "s_filt%d" % L, [2 * L, 2048], BF16).ap()
        S["D1F"][L] = _scr(nc, "s_d1f%d" % L, [N1, 64, 2, 2048], BF16).ap()
        S["HS"][L] = _scr(nc, "s_hs%d" % L, [N1, 64, 2, 2048], BF16).ap()
    S["D1"] = _scr(nc, "s_d1", [128, 64, 2, 1024], BF16).ap()
    S["D2"] = _scr(nc, "s_d2", [128, 64, 2, 1024], BF16).ap()
    for s, L in enumerate(seqs):
        S["QK"].append(_scr(nc, "s_qk%d" % s, [2048, L], BF16).ap())
        S["V"].append(_scr(nc, "s_v%d" % s, [L, 2048], BF16).ap())
        S["G"].append(_scr(nc, "s_g%d" % s, [L, 2048], BF16).ap())
        S["U"].append(_scr(nc, "s_u%d" % s, [2048, L], BF16).ap())
        S["PP"].append(_scr(nc, "s_pp%d" % s, [L + 2, 3072], F32).ap())
        S["HV"].append([_scr(nc, "s_hv%d_%d" % (s, i), [L, 1024], BF16).ap() for i in range(3)])
        S["HG"].append(_scr(nc, "s_hg%d" % s, [L, 2048], BF16).ap())
        S["QKV"].append(_scr(nc, "s_qkv%d" % s, [3072, L], BF16).ap())
        S["Z"].append(_scr(nc, "s_z%d" % s, [L, 1024], BF16).ap())
        S["GB"].append(_scr(nc, "s_gb%d" % s, [L, 32], F32).ap())
        S["OF"].append(_scr(nc, "s_of%d" % s, [L, 1024], F32).ap())
        S["OB"].append(_scr(nc, "s_ob%d" % s, [L, 1024], F32).ap())
    kb = KB(nc)
    with kb.st:
        cst = load_consts(kb, kb.st, P)
        first = True
        for stg in stages:
            Xin = X if first else Y
            first = False
            if stg[0] == "ffn":
                ffn_stage(kb, cst, P, Xin, Y, seqs, stg[1], stg[2])
            elif stg[0] == "copy":
                copy_stage(kb, Xin, Y, seqs)
            elif stg[0] == "mix":
                kind = stg[1] % 3
                if kind == 1:
                    ret_stage(kb, cst, P, S, Y, seqs, stg[1])
                elif kind == 0:
                    gdn_stage(kb, cst, P, S, Y, seqs, stg[1])
                else:
                    hyena_stage(kb, cst, P, S, Y, seqs, stg[1])
        kb.barrier()
    return nc, kb


def hyena_consts(L):
    N = 2 * L
    N1 = N // 64
    K = N1 // 2
    o = {}
    n1 = np.arange(N1, dtype=np.float64)
    ph = 2 * np.pi * np.outer(n1, n1) / N1
    o["hy_f1t_%d" % L] = np.concatenate([np.cos(ph), -np.sin(ph)], 1).astype(np.float32)
    n2 = np.arange(64, dtype=np.float64)
    k1 = np.arange(N1, dtype=np.float64)
    k2 = np.arange(64, dtype=np.float64)
    th = 2 * np.pi * n2[:, None, None] * (k1[None, :, None] + N1 * k2[None, None, :]) / N
    C, Sm = np.cos(th), -np.sin(th)
    tw = np.stack([C, Sm, -Sm], 2)
    o["hy_tw_%d" % L] = np.ascontiguousarray(tw.reshape(64, N1 * 192).astype(np.float32))
    twt = np.stack([C, Sm, -Sm], 0).transpose(3, 2, 0, 1)
    o["hy_twt_%d" % L] = np.ascontiguousarray(twt.reshape(64, N1 * 192).astype(np.float32))
    ph2 = 2 * np.pi * np.outer(k1, np.arange(K, dtype=np.float64)) / N1
    o["hy_g2t_%d" % L] = np.concatenate([np.cos(ph2) / N, -np.sin(ph2) / N], 1).astype(np.float32)
    n = np.arange(N)
    pos = np.where(n < L, n, N - n).astype(np.float64)
    pos[L] = 0
    t = pos / (L - 1)
    om = 2 * np.pi * pos / L
    f = np.linspace(1e-4, 15.0, 16)
    z = np.concatenate([t[None, :], np.cos(f[:, None] * om[None, :]), -np.sin(f[:, None] * om[None, :])], 0)
    o["hy_z_%d" % L] = np.ascontiguousarray(z.astype(np.float32))
    nt = -t
    nt[L] = -1e4
    o["hy_ntp_%d" % L] = np.ascontiguousarray(nt.reshape(N // 128, 128).T.astype(np.float32))
    return o


def prep_params(inp, seqs=(2048, 4096)):
    out = {}
    for L in sorted(set(seqs)):
        out.update(hyena_consts(L))
    min_decay = math.log(1e-2) / 1.5
    max_decay = math.log(1e-2) / 0.3
    deltas = np.abs(np.linspace(min_decay, max_decay, D))
    out["hy_delta"] = np.ascontiguousarray(np.broadcast_to(deltas.astype(np.float32)[None, :], (128, D)))
    f32 = lambda k: np.asarray(inp[k], np.float32)
    out["c_w_in"] = np.ascontiguousarray(f32("c_w_in")[0])
    out["c_b_in"] = np.ascontiguousarray(np.broadcast_to(f32("c_b_in")[0][None, :], (128, 3072)))
    cc = np.concatenate([f32("c_conv_w")[0], f32("c_conv_b")[0][None, :]], 0)
    out["c_conv"] = np.ascontiguousarray(np.broadcast_to(cc[None], (128, 4, 3072)))
    out["c_f_w1"] = np.ascontiguousarray(f32("c_f_w1")[0])
    out["c_f_w2"] = np.ascontiguousarray(f32("c_f_w2")[0])
    out["c_f_w3"] = np.ascontiguousarray(f32("c_f_w3")[0])
    out["c_f_prm"] = np.ascontiguousarray(np.stack([f32("c_f_b1")[0], f32("c_f_freq1")[0], f32("c_f_b2")[0], f32("c_f_freq2")[0]], 1))
    out["c_bias_d"] = np.ascontiguousarray(np.broadcast_to(f32("c_bias_d")[0][None], (128, 2, 1024)))
    out["c_w_out"] = np.ascontiguousarray(f32("c_w_out")[0])
    out["c_b_out"] = np.ascontiguousarray(f32("c_b_out")[0].reshape(8, 128).T)
    ng = np.asarray(inp["norm_g"], np.float32)
    out["norm_g"] = np.ascontiguousarray(ng.reshape(4, 6, 8, 128).transpose(3, 0, 1, 2).reshape(128, 192))
    out["ffn_w_gate"] = np.ascontiguousarray(np.asarray(inp["ffn_w_gate"], np.float32).reshape(8, D, DFF))
    out["ffn_w_up"] = np.ascontiguousarray(np.asarray(inp["ffn_w_up"], np.float32).reshape(8, D, DFF))
    out["ffn_w_down"] = np.ascontiguousarray(np.asarray(inp["ffn_w_down"], np.float32).reshape(8, DFF, D))
    out["ident"] = np.eye(128, dtype=np.float32)
    inv = (10000.0 ** (-np.arange(0, 256, 2, dtype=np.float64) / 256.0))
    ang = inv[:, None] * np.arange(4096, dtype=np.float64)[None, :]
    rope = np.stack([np.cos(ang), np.sin(ang), np.cos(ang) / 16.0, np.sin(ang) / 16.0], 1)
    out["rope"] = np.ascontiguousarray(rope.astype(np.float32))
    m = np.arange(128)
    out["dmat"] = np.ascontiguousarray((m[None, :] - m[:, None]).astype(np.float32))
    out["tri"] = np.ascontiguousarray(np.stack([(m[None, :] >= m[:, None]), (m[:, None] >= m[None, :])], 1).astype(np.float32))
    out["dvals"] = np.ascontiguousarray(np.broadcast_to(128.0 * np.arange(32, dtype=np.float32)[None, :], (128, 32)))
    le = (m[:, None] <= m[None, :]).astype(np.float32)
    gt = (m[:, None] > m[None, :]).astype(np.float32)
    ge = (m[:, None] >= m[None, :]).astype(np.float32)
    lt = (m[:, None] < m[None, :]).astype(np.float32)
    out["gmask"] = np.ascontiguousarray(np.stack([le, gt, ge, lt], 1))
    out["negm"] = np.ascontiguousarray(np.stack([np.tile(-1e4 * x, (1, 8)) for x in (le, gt, ge, lt)], 1))
    blk = lambda b: (m[:, None] // b) == (m[None, :] // b)
    bms = [blk(16), blk(32) & ~blk(16), blk(64) & ~blk(32), ~blk(64)]
    out["bmask"] = np.ascontiguousarray(np.stack([np.tile(x.astype(np.float32), (1, 8)) for x in bms], 1))
    out["a_w_in"] = np.ascontiguousarray(np.asarray(inp["a_w_in"], np.float32))
    cwa = np.asarray(inp["a_conv_w"], np.float32)
    out["a_conv_w"] = np.ascontiguousarray(cwa.reshape(2, 3, 24, 128).transpose(0, 3, 1, 2).reshape(2, 128, 72))
    out["a_a_log"] = np.ascontiguousarray(np.broadcast_to(np.asarray(inp["a_a_log"], np.float32).reshape(2, 1, 16), (2, 128, 16)))
    out["a_dt_bias"] = np.ascontiguousarray(np.broadcast_to(np.asarray(inp["a_dt_bias"], np.float32).reshape(2, 1, 16), (2, 128, 16)))
    out["a_norm_w"] = np.ascontiguousarray(np.broadcast_to(np.tile(np.asarray(inp["a_norm_w"], np.float32), (1, 8)).reshape(2, 1, 1024), (2, 128, 1024)))
    out["a_w_out"] = np.ascontiguousarray(np.asarray(inp["a_w_out"], np.float32))
    out["b_w_in"] = np.ascontiguousarray(np.asarray(inp["b_w_in"], np.float32)[0])
    out["b_decay_logit"] = np.ascontiguousarray(np.broadcast_to(np.asarray(inp["b_decay_logit"], np.float32)[0].reshape(1, 8), (128, 8)))
    out["b_gn_w"] = np.ascontiguousarray(np.broadcast_to(np.asarray(inp["b_gn_w"], np.float32)[0].reshape(1, 2048), (128, 2048)))
    out["b_w_out"] = np.ascontiguousarray(np.asarray(inp["b_w_out"], np.float32)[0])
    return out


DEBUG_OUT = set()


def _scr(nc, name, shape, dt):
    kind = "ExternalOutput" if name in DEBUG_OUT else "Internal"
    return nc.dram_tensor(name, list(shape), dt, kind=kind)


ALL_STAGES = []
for _l in range(4):
    ALL_STAGES += [("ffn", _l, 0), ("mix", _l), ("ffn", _l, 1)]


def kernel(**inp):
    xp = np.asarray(inp["x_prompt"], np.float32)
    xs = np.asarray(inp["x_sample"], np.float32)
    seqs = [xp.shape[1], xs.shape[1], xs.shape[1]]
    nc, kb = build(seqs, ALL_STAGES)
    pp = prep_params(inp, seqs)
    in_maps = []
    for c in range(N_CORES):
        m = dict(pp)
        m["xin0"] = np.ascontiguousarray(xp[c].T)
        m["xin1"] = np.ascontiguousarray(xs[2 * c].T)
        m["xin2"] = np.ascontiguousarray(xs[2 * c + 1].T)
        in_maps.append(m)
    res = run_bass_kernel_spmd(nc, in_maps, core_ids=list(range(N_CORES)))
    yp = np.stack([res.results[c]["yout0"].T for c in range(N_CORES)], 0)
    ys = np.stack([res.results[c][k].T for c in range(N_CORES) for k in ("yout1", "yout2")], 0)
    return (np.ascontiguousarray(yp, dtype=np.float32), np.ascontiguousarray(ys, dtype=np.float32))
```

```python
import contextlib
import math
import numpy as np
import concourse.bass as bass
import concourse.mybir as mybir
from concourse.bass_utils import run_bass_kernel_spmd

F32 = mybir.dt.float32
BF16 = mybir.dt.bfloat16
AF = mybir.ActivationFunctionType
ALU = mybir.AluOpType

D = 1024
DFF = 2816
NFF = DFF // 128
EPS = 1e-6
N_CORES = 8
DBG = 99
STQ = "sp"
HSTQ = "act"
HY_SKIP = set()
OWN_RAW = True
GDN_SKIP_B = False
ADD_ENG = "dve"


class Res:
    __slots__ = ("w", "r", "psum")

    def __init__(self):
        self.w = None
        self.r = {}
        self.psum = False


class Tile:
    def __init__(self, t, nres=1):
        self.t = t
        self.rs = [Res() for _ in range(nres)]

    @property
    def r(self):
        return self.rs[0]


class KB:
    def __init__(self, nc):
        self.nc = nc
        self.st = contextlib.ExitStack()
        self.engs = {"pe": nc.tensor, "act": nc.scalar, "dve": nc.vector, "pool": nc.gpsimd, "sp": nc.sync}
        self.semh = {}
        self.cnt = {}
        self.seen = {e: {} for e in self.engs}
        for e in self.engs:
            self.semh[e] = self.st.enter_context(nc.semaphore("s_" + e))
            self.cnt[e] = 0
        self.dslots = {"sp": 24, "pool": 2, "act": 16}
        self.dnext = {q: 0 for q in self.dslots}
        self.dtot = {}
        for q, n in self.dslots.items():
            for i in range(n):
                sk = ("d", q, i)
                self.semh[sk] = self.st.enter_context(nc.semaphore("d_%s_%d" % (q, i)))
                self.dtot[sk] = 0
        self.dram_res = {}
        self.ninst = 0
        self.uid = 0

    def dres(self, name, a, b, blk=128):
        out = []
        for i in range(a // blk, (b + blk - 1) // blk):
            key = (name, i)
            if key not in self.dram_res:
                self.dram_res[key] = Res()
            out.append(self.dram_res[key])
        return out

    def _need(self, e, reads, writes, own_raw=True):
        need = {}

        def add(sk, v):
            if need.get(sk, 0) < v:
                need[sk] = v

        for r in reads:
            if r.w is not None:
                add(*r.w)
        for r in writes:
            if r.w is not None:
                add(*r.w)
            for sk, v in r.r.items():
                if sk == e:
                    continue
                add(sk, v)
        if not own_raw:
            need.pop(e, None)
        return need

    def _wait(self, e, need):
        eng = self.engs[e]
        seen = self.seen[e]
        for sk, v in need.items():
            if seen.get(sk, 0) < v:
                eng.wait_ge(self.semh[sk], v)
                seen[sk] = v
                self.ninst += 1

    def op(self, e, fn, reads=(), writes=()):
        need = self._need(e, reads, writes, own_raw=(e != "pe") and (OWN_RAW or e == "pool"))
        self._wait(e, need)
        ins = fn(self.engs[e])
        self.cnt[e] += 1
        c = self.cnt[e]
        ins.then_inc(self.semh[e], 1)
        self.ninst += 1
        for r in reads:
            if r.psum:
                assert all(k == e for k in r.r), ("two engines read one PSUM tile in the same epoch", e, list(r.r))
            r.r[e] = c
        for r in writes:
            r.w = (e, c)
            r.r = {}
        return c

    def dma(self, q, out, in_, reads=(), writes=()):
        i = self.dnext[q]
        self.dnext[q] = (i + 1) % self.dslots[q]
        sk = ("d", q, i)
        tot = self.dtot[sk]
        need = self._need(q, reads, writes)
        if need.get(sk, 0) < tot:
            need[sk] = tot
        self._wait(q, need)
        self.engs[q].dma_start(out=out, in_=in_).then_inc(self.semh[sk], 16)
        self.ninst += 1
        tot += 16
        self.dtot[sk] = tot
        for r in reads:
            r.r[sk] = tot
        for r in writes:
            r.w = (sk, tot)
            r.r = {}

    def barrier(self):
        cur = {}
        for e in self.engs:
            cur[e] = self.cnt[e]
        for sk, v in self.dtot.items():
            cur[sk] = v
        for e in self.engs:
            need = {sk: v for sk, v in cur.items() if sk != e and v > 0}
            self._wait(e, need)

    def sb(self, st, name, shape, dt, nres=1):
        self.uid += 1
        return Tile(st.enter_context(self.nc.sbuf_tensor("%s_%d" % (name, self.uid), list(shape), dt)), nres)

    def ps(self, st, name, shape, dt=F32, nres=1):
        self.uid += 1
        t = Tile(st.enter_context(self.nc.psum_tensor("%s_%d" % (name, self.uid), list(shape), dt)), nres)
        for r in t.rs:
            r.psum = True
        return t


def load_consts(kb, st, P):
    c = {}
    c["ones_bf"] = kb.sb(st, "ones_bf", [128, 128], BF16)
    kb.op("dve", lambda e: e.memset(c["ones_bf"].t[:], 1.0), writes=[c["ones_bf"].r])
    c["ng"] = kb.sb(st, "ng", [128, 4 * 6 * 8], F32)
    kb.dma("sp", c["ng"].t[:], P["norm_g"][:, :], writes=[c["ng"].r])
    c["ngh"] = kb.sb(st, "ngh", [128, 4 * 6 * 8], F32)
    kb.op("dve", lambda e: e.tensor_scalar(out=c["ngh"].t[:], in0=c["ng"].t[:], scalar1=0.5, scalar2=None,
                                           op0=ALU.mult), reads=[c["ng"].r], writes=[c["ngh"].r])
    c["eps"] = kb.sb(st, "eps", [128, 1], F32)
    kb.op("dve", lambda e: e.memset(c["eps"].t[:], EPS), writes=[c["eps"].r])
    c["one"] = kb.sb(st, "one", [128, 1], F32)
    kb.op("dve", lambda e: e.memset(c["one"].t[:], 1.0), writes=[c["one"].r])
    c["ident_f"] = kb.sb(st, "ident_f", [128, 128], F32)
    kb.dma("sp", c["ident_f"].t[:], P["ident"][:, :], writes=[c["ident_f"].r])
    c["ident_bf"] = kb.sb(st, "ident_bf", [128, 128], BF16)
    kb.op("dve", lambda e: e.tensor_copy(out=c["ident_bf"].t[:], in_=c["ident_f"].t[:]), reads=[c["ident_f"].r], writes=[c["ident_bf"].r])
    return c


def gcol(c, which, l, j, ch):
    idx = (l * 6 + j) * 8 + ch
    return c[which].t[:, idx:idx + 1]


def rstd_from_sumsq(kb, ps_ss, rt, scratch, n, cst, scale):
    kb.op("act", lambda e: e.activation(out=scratch.t[:, :n], in_=ps_ss.t[:, :n], func=AF.Sqrt,
                                        bias=cst["eps"].t[:, 0:1], scale=scale),
          reads=[ps_ss.r, cst["eps"].r], writes=[scratch.r])
    kb.op("dve", lambda e: e.reciprocal(out=rt.t[:, :n], in_=scratch.t[:, :n]), reads=[scratch.r], writes=[rt.r])


class Sub:
    def __init__(self, ap, psum=False):
        self.t = ap
        self.rs = [Res()]
        self.rs[0].psum = psum

    @property
    def r(self):
        return self.rs[0]


def rolling(items, width, make):
    pending = list(items)
    active = {}
    for b in range(width):
        if pending:
            active[b] = make(pending.pop(0), b)
    while active:
        for b in list(active.keys()):
            try:
                next(active[b])
            except StopIteration:
                if pending:
                    active[b] = make(pending.pop(0), b)
                else:
                    del active[b]


class WLoader:
    def __init__(self, kb, st, width, nbuf=3):
        self.kb = kb
        self.stg = [kb.sb(st, "wstg%d" % i, [128, width], F32) for i in range(nbuf)]
        self.i = 0
        self.engs = ["act", "dve", "pool"]

    def load(self, dst, src, res, width, rows=128):
        kb = self.kb
        k = self.i
        self.i += 1
        stg = self.stg[k % len(self.stg)]
        eng = self.engs[k % len(self.engs)]
        kb.dma("sp", stg.t[:rows, :width], src, writes=[stg.r])
        if eng == "act":
            kb.op("act", lambda e: e.activation(out=dst, in_=stg.t[:rows, :width], func=AF.Copy), reads=[stg.r], writes=[res])
        else:
            kb.op(eng, lambda e: e.tensor_copy(out=dst, in_=stg.t[:rows, :width]), reads=[stg.r], writes=[res])


def ffn_stage(kb, cst, P, Xin, X, seqs, layer, j):
    nc = kb.nc
    T = 256
    with contextlib.ExitStack() as st:
        wg = kb.sb(st, "wg", [128, 8, DFF], BF16, nres=8)
        wu = kb.sb(st, "wu", [128, 8, DFF], BF16, nres=8)
        wd = kb.sb(st, "wd", [128, NFF, D], BF16, nres=NFF)
        xt = kb.sb(st, "xt", [128, 8, T], F32)
        ht = kb.sb(st, "ht", [128, 8, T], BF16)
        at = kb.sb(st, "at", [128, NFF, T], BF16, nres=NFF)
        ot = kb.sb(st, "ot", [128, 8, T], F32, nres=8)
        sq = kb.sb(st, "sq", [128, 8, T], BF16, nres=8)
        sg = [kb.sb(st, "sg%d" % i, [128, T], F32) for i in range(2)]
        rt = kb.sb(st, "rt", [128, T], F32)
        rs = kb.sb(st, "rs", [128, T], F32)
        tmp = [kb.sb(st, "tmp%d" % i, [128, T], F32) for i in range(2)]
        psg = [kb.ps(st, "psg%d" % i, [128, 512]) for i in range(2)]
        psu = [kb.ps(st, "psu%d" % i, [128, 512]) for i in range(2)]
        pso = [kb.ps(st, "pso%d" % i, [128, 512]) for i in range(2)]
        pss = kb.ps(st, "pss", [128, 512])

        Wg = P["ffn_w_gate"][layer * 2 + j]
        Wu = P["ffn_w_up"][layer * 2 + j]
        Wd = P["ffn_w_down"][layer * 2 + j]
        wl = WLoader(kb, st, DFF, nbuf=2)
        for c in range(8):
            wl.load(wg.t[:, c, :], Wg[c * 128:(c + 1) * 128, :], wg.rs[c], DFF)
            wl.load(wu.t[:, c, :], Wu[c * 128:(c + 1) * 128, :], wu.rs[c], DFF)
        for f in range(NFF):
            wl.load(wd.t[:, f, :], Wd[f * 128:(f + 1) * 128, :], wd.rs[f], D)

        for s, L in enumerate(seqs):
            if DBG < 1:
                break
            Xv = X[s].rearrange("(c p) l -> p c l", p=128)
            Xiv = Xin[s].rearrange("(c p) l -> p c l", p=128)
            for t0 in range(0, L, T):
                n = min(T, L - t0)
                xr = kb.dres(("X", s), t0, t0 + n)
                kb.dma("sp", xt.t[:, :, :n], Xiv[:, :, t0:t0 + n], reads=xr, writes=[xt.r])
                if DBG < 2:
                    kb.dma(STQ, Xv[:, :, t0:t0 + n], xt.t[:, :, :n], reads=[xt.r], writes=xr)
                    continue
                kb.op("act", lambda e: e.activation(out=sq.t[:, :, :n], in_=xt.t[:, :, :n], func=AF.Square),
                      reads=[xt.r], writes=sq.rs)

                def ssum(e, n=n):
                    for c in range(8):
                        ins = e.matmul(pss.t[:, :n], cst["ones_bf"].t[:], sq.t[:, c, :n], start=(c == 0), stop=(c == 7))
                    return ins
                kb.op("pe", ssum, reads=sq.rs + [cst["ones_bf"].r], writes=[pss.r])
                rstd_from_sumsq(kb, pss, rt, rs, n, cst, 1.0 / D)
                for c in range(8):
                    kb.op("dve", lambda e, c=c: e.scalar_tensor_tensor(
                        out=ht.t[:, c, :n], in0=xt.t[:, c, :n], scalar=gcol(cst, "ng", layer, 3 * j + 0 if j == 0 else 4, c),
                        op0=ALU.mult, in1=rt.t[:, :n], op1=ALU.mult), reads=[xt.r, rt.r, cst["ng"].r], writes=[ht.r])
                if DBG < 3:
                    kb.dma(STQ, Xv[:, :, t0:t0 + n], xt.t[:, :, :n], reads=[xt.r, ht.r], writes=xr)
                    continue
                for f in range(NFF):
                    b = f % 2

                    def mmg(e, f=f, b=b, n=n):
                        for c in range(8):
                            ins = e.matmul(psg[b].t[:, :n], wg.t[:, c, f * 128:(f + 1) * 128], ht.t[:, c, :n],
                                           start=(c == 0), stop=(c == 7))
                        return ins

                    def mmu(e, f=f, b=b, n=n):
                        for c in range(8):
                            ins = e.matmul(psu[b].t[:, :n], wu.t[:, c, f * 128:(f + 1) * 128], ht.t[:, c, :n],
                                           start=(c == 0), stop=(c == 7))
                        return ins
                    kb.op("pe", mmg, reads=wg.rs + [ht.r], writes=[psg[b].r])
                    kb.op("pe", mmu, reads=wu.rs + [ht.r], writes=[psu[b].r])
                    kb.op("act", lambda e, b=b, n=n: e.activation(out=sg[b].t[:, :n], in_=psg[b].t[:, :n], func=AF.Silu),
                          reads=[psg[b].r], writes=[sg[b].r])
                    kb.op("dve", lambda e, b=b, f=f, n=n: e.tensor_tensor(out=at.t[:, f, :n], in0=psu[b].t[:, :n],
                                                                          in1=sg[b].t[:, :n], op=ALU.mult),
                          reads=[psu[b].r, sg[b].r], writes=[at.rs[f]])
                if DBG < 4:
                    kb.dma(STQ, Xv[:, :, t0:t0 + n], xt.t[:, :, :n], reads=[xt.r] + at.rs, writes=xr)
                    continue
                for d in range(8):
                    b = d % 2

                    def mmd(e, d=d, b=b, n=n):
                        for f in range(NFF):
                            ins = e.matmul(pso[b].t[:, :n], wd.t[:, f, d * 128:(d + 1) * 128], at.t[:, f, :n],
                                           start=(f == 0), stop=(f == NFF - 1))
                        return ins
                    kb.op("pe", mmd, reads=wd.rs + at.rs, writes=[pso[b].r])
                    kb.op("dve", lambda e, d=d, b=b, n=n: e.tensor_copy(out=ot.t[:, d, :n], in_=pso[b].t[:, :n]),
                          reads=[pso[b].r], writes=[ot.rs[d]])
                    kb.op("act", lambda e, d=d, b=b, n=n: e.activation(out=sq.t[:, d, :n], in_=ot.t[:, d, :n], func=AF.Square),
                          reads=[ot.rs[d]], writes=[sq.rs[d]])
                if DBG < 5:
                    kb.dma(STQ, Xv[:, :, t0:t0 + n], xt.t[:, :, :n], reads=[xt.r] + ot.rs + sq.rs, writes=xr)
                    continue
                kb.op("pe", ssum, reads=sq.rs + [cst["ones_bf"].r], writes=[pss.r])
                rstd_from_sumsq(kb, pss, rt, rs, n, cst, 1.0 / D)
                if DBG < 6:
                    kb.dma(STQ, Xv[:, :, t0:t0 + n], xt.t[:, :, :n], reads=[xt.r, rt.r], writes=xr)
                    continue
                jn = 1 if j == 0 else 5
                for c in range(8):
                    b = c % 2
                    kb.op("dve", lambda e, c=c, b=b, n=n: e.scalar_tensor_tensor(
                        out=tmp[b].t[:, :n], in0=ot.t[:, c, :n], scalar=gcol(cst, "ngh", layer, jn, c),
                        op0=ALU.mult, in1=rt.t[:, :n], op1=ALU.mult), reads=[ot.rs[c], rt.r, cst["ngh"].r], writes=[tmp[b].r])
                    kb.op(ADD_ENG, lambda e, c=c, b=b, n=n: e.tensor_tensor(out=xt.t[:, c, :n], in0=xt.t[:, c, :n],
                                                                           in1=tmp[b].t[:, :n], op=ALU.add),
                          reads=[tmp[b].r, xt.r], writes=[xt.r])
                kb.dma(STQ, Xv[:, :, t0:t0 + n], xt.t[:, :, :n], reads=[xt.r], writes=xr)
        kb.barrier()


class NormCtx:
    def __init__(self, kb, st, cst, T):
        self.kb, self.cst, self.T = kb, cst, T
        self.sq = kb.sb(st, "nsq", [128, 8, T], BF16)
        self.rt = kb.sb(st, "nrt", [128, T], F32)
        self.rs = kb.sb(st, "nrs", [128, T], F32)
        self.tmp = [kb.sb(st, "ntmp%d" % i, [128, T], F32) for i in range(2)]
        self.pss = kb.ps(st, "npss", [128, 512])

    def _rstd(self, src, n):
        kb, cst = self.kb, self.cst
        kb.op("act", lambda e: e.activation(out=self.sq.t[:, :, :n], in_=src.t[:, :, :n], func=AF.Square),
              reads=src.rs, writes=[self.sq.r])

        def ssum(e):
            for c in range(8):
                ins = e.matmul(self.pss.t[:, :n], cst["ones_bf"].t[:], self.sq.t[:, c, :n], start=(c == 0), stop=(c == 7))
            return ins
        kb.op("pe", ssum, reads=[self.sq.r, cst["ones_bf"].r], writes=[self.pss.r])
        rstd_from_sumsq(kb, self.pss, self.rt, self.rs, n, cst, 1.0 / D)

    def prenorm(self, xt, ht, n, layer, jn):
        kb, cst = self.kb, self.cst
        self._rstd(xt, n)
        for c in range(8):
            kb.op("dve", lambda e, c=c: e.scalar_tensor_tensor(
                out=ht.t[:, c, :n], in0=xt.t[:, c, :n], scalar=gcol(cst, "ng", layer, jn, c),
                op0=ALU.mult, in1=self.rt.t[:, :n], op1=ALU.mult), reads=xt.rs + [self.rt.r, cst["ng"].r], writes=ht.rs)

    def postnorm_res(self, ot, xt, n, layer, jn, half):
        kb, cst = self.kb, self.cst
        self._rstd(ot, n)
        which = "ngh" if half else "ng"
        for c in range(8):
            b = c % 2
            kb.op("dve", lambda e, c=c, b=b: e.scalar_tensor_tensor(
                out=self.tmp[b].t[:, :n], in0=ot.t[:, c, :n], scalar=gcol(cst, which, layer, jn, c),
                op0=ALU.mult, in1=self.rt.t[:, :n], op1=ALU.mult), reads=ot.rs + [self.rt.r, cst[which].r], writes=[self.tmp[b].r])
            kb.op("pool", lambda e, c=c, b=b: e.tensor_tensor(out=xt.t[:, c, :n], in0=xt.t[:, c, :n],
                                                             in1=self.tmp[b].t[:, :n], op=ALU.add),
                  reads=[self.tmp[b].r] + xt.rs, writes=xt.rs)


def outproj_phase(kb, cst, Wap, bias_tile, U, uname, Kdim, X, seqs, layer):
    KC = Kdim // 128
    T = 512
    with contextlib.ExitStack() as st:
        wo = kb.sb(st, "wo", [128, KC, D], BF16, nres=KC)
        wl = WLoader(kb, st, D)
        for k in range(KC):
            wl.load(wo.t[:, k, :], Wap[k * 128:(k + 1) * 128, :], wo.rs[k], D)
        nrm = NormCtx(kb, st, cst, T)
        xt = kb.sb(st, "oxt", [128, 8, T], F32)
        ut = kb.sb(st, "out", [128, KC, T], BF16)
        ot = kb.sb(st, "oot", [128, 8, T], F32)
        pso = [kb.ps(st, "opso%d" % i, [128, 512]) for i in range(2)]
        for s, L in enumerate(seqs):
            Xv = X[s].rearrange("(c p) l -> p c l", p=128)
            Uv = U[s].rearrange("(k p) l -> p k l", p=128)
            for t0 in range(0, L, T):
                n = min(T, L - t0)
                xr = kb.dres(("X", s), t0, t0 + n)
                ur = kb.dres((uname, s), t0, t0 + n)
                kb.dma("sp", xt.t[:, :, :n], Xv[:, :, t0:t0 + n], reads=xr, writes=[xt.r])
                kb.dma("sp", ut.t[:, :, :n], Uv[:, 0:KC, t0:t0 + n], reads=ur, writes=[ut.r])
                for d in range(8):
                    b = d % 2

                    def mmo(e, d=d, b=b):
                        for k in range(KC):
                            ins = e.matmul(pso[b].t[:, :n], wo.t[:, k, d * 128:(d + 1) * 128], ut.t[:, k, :n],
                                           start=(k == 0), stop=(k == KC - 1))
                        return ins
                    kb.op("pe", mmo, reads=wo.rs + [ut.r], writes=[pso[b].r])
                    if bias_tile is None:
                        kb.op("act", lambda e, d=d, b=b: e.activation(out=ot.t[:, d, :n], in_=pso[b].t[:, :n], func=AF.Copy),
                              reads=[pso[b].r], writes=[ot.r])
                    else:
                        kb.op("act", lambda e, d=d, b=b: e.activation(out=ot.t[:, d, :n], in_=pso[b].t[:, :n], func=AF.Identity,
                                                                       bias=bias_tile.t[:, d:d + 1], scale=1.0),
                              reads=[pso[b].r, bias_tile.r], writes=[ot.r])
                nrm.postnorm_res(ot, xt, n, layer, 3, False)
                kb.dma(STQ, Xv[:, :, t0:t0 + n], xt.t[:, :, :n], reads=[xt.r], writes=xr)
        kb.barrier()


R_H, R_DK, R_DV = 4, 256, 512


def ret_stage(kb, cst, P, S, X, seqs, layer):
    nc = kb.nc
    T = 256
    W = P["b_w_in"]
    with contextlib.ExitStack() as st:
        wi = kb.sb(st, "rwi", [128, 8, 6144], BF16, nres=8)
        wl = WLoader(kb, st, 3072, nbuf=2)
        for c in range(8):
            for hh in range(2):
                wl.load(wi.t[:, c, hh * 3072:(hh + 1) * 3072], W[c * 128:(c + 1) * 128, hh * 3072:(hh + 1) * 3072], wi.rs[c], 3072)
        nrm = NormCtx(kb, st, cst, T)
        xt = kb.sb(st, "rxt", [128, 8, T], F32)
        ht = kb.sb(st, "rht", [128, 8, T], BF16)
        rope = kb.sb(st, "rrope", [128, 4, T], F32)
        qk = kb.sb(st, "rqk", [128, 16, T], BF16, nres=16)
        RSL = 2
        x1s = [kb.sb(st, "rx1%d" % i, [128, T], F32) for i in range(RSL)]
        x2s = [kb.sb(st, "rx2%d" % i, [128, T], F32) for i in range(RSL)]
        tas = [[kb.sb(st, "rta%d_%d" % (j, i), [128, T], F32) for i in range(2)] for j in range(RSL)]
        tbs = [[kb.sb(st, "rtb%d_%d" % (j, i), [128, T], F32) for i in range(2)] for j in range(RSL)]
        vt = [kb.sb(st, "rvt%d" % i, [128, 2048], BF16) for i in range(2)]
        gt = [kb.sb(st, "rgt%d" % i, [128, 2048], BF16) for i in range(2)]
        ps1s = [kb.ps(st, "rps1%d" % i, [128, 512]) for i in range(RSL)]
        ps2s = [kb.ps(st, "rps2%d" % i, [128, 512]) for i in range(RSL)]
        psv = [kb.ps(st, "rpsv%d" % i, [128, 512]) for i in range(2)]
        for s, L in enumerate(seqs):
            Xv = X[s].rearrange("(c p) l -> p c l", p=128)
            QKv = S["QK"][s].rearrange("(k p) l -> p k l", p=128)
            for t0 in range(0, L, T):
                n = min(T, L - t0)
                xr = kb.dres(("X", s), t0, t0 + n)
                kb.dma("sp", xt.t[:, :, :n], Xv[:, :, t0:t0 + n], reads=xr, writes=[xt.r])
                kb.dma("sp", rope.t[:, :, :n], P["rope"][:, :, t0:t0 + n], writes=[rope.r])
                nrm.prenorm(xt, ht, n, layer, 2)
                def rchain(pr, b):
                    isk = pr >= 4
                    ps1, ps2, x1, x2, ta, tb = ps1s[b], ps2s[b], x1s[b], x2s[b], tas[b], tbs[b]
                    for half, pst in ((0, ps1), (1, ps2)):
                        cc = 2 * pr + half

                        def mm(e):
                            for c in range(8):
                                ins = e.matmul(pst.t[:, :n], wi.t[:, c, cc * 128:(cc + 1) * 128], ht.t[:, c, :n],
                                               start=(c == 0), stop=(c == 7))
                            return ins
                        kb.op("pe", mm, reads=wi.rs + [ht.r], writes=[pst.r])
                    yield
                    kb.op("act", lambda e: e.activation(out=x1.t[:, :n], in_=ps1.t[:, :n], func=AF.Copy), reads=[ps1.r], writes=[x1.r])
                    kb.op("act", lambda e: e.activation(out=x2.t[:, :n], in_=ps2.t[:, :n], func=AF.Copy), reads=[ps2.r], writes=[x2.r])
                    yield
                    ci, si = (2, 3) if isk else (0, 1)
                    kb.op("dve", lambda e: e.tensor_tensor(out=ta[0].t[:, :n], in0=x1.t[:, :n], in1=rope.t[:, ci, :n], op=ALU.mult),
                          reads=[x1.r, rope.r], writes=[ta[0].r])
                    kb.op("pool", lambda e: e.tensor_tensor(out=tb[0].t[:, :n], in0=x2.t[:, :n], in1=rope.t[:, ci, :n], op=ALU.mult),
                          reads=[x2.r, rope.r], writes=[tb[0].r])
                    yield
                    kb.op("dve", lambda e: e.tensor_tensor(out=ta[1].t[:, :n], in0=x2.t[:, :n], in1=rope.t[:, si, :n], op=ALU.mult),
                          reads=[x2.r, rope.r], writes=[ta[1].r])
                    kb.op("pool", lambda e: e.tensor_tensor(out=tb[1].t[:, :n], in0=x1.t[:, :n], in1=rope.t[:, si, :n], op=ALU.mult),
                          reads=[x1.r, rope.r], writes=[tb[1].r])
                    yield
                    kb.op("dve", lambda e: e.tensor_tensor(out=qk.t[:, 2 * pr, :n], in0=ta[0].t[:, :n], in1=ta[1].t[:, :n], op=ALU.subtract),
                          reads=[ta[0].r, ta[1].r], writes=[qk.rs[2 * pr]])
                    kb.op("pool", lambda e: e.tensor_tensor(out=qk.t[:, 2 * pr + 1, :n], in0=tb[0].t[:, :n], in1=tb[1].t[:, :n], op=ALU.add),
                          reads=[tb[0].r, tb[1].r], writes=[qk.rs[2 * pr + 1]])
                    yield
                rolling(list(range(8)), RSL, rchain)
                kb.dma(HSTQ, QKv[:, :, t0:t0 + n], qk.t[:, :, :n], reads=qk.rs, writes=kb.dres(("QK", s), t0, t0 + n))
                for tb_i in range(n // 128):
                    bb = tb_i % 2
                    for cb in range(8):
                        pb = cb % 2

                        def mmv(e, cb=cb, pb=pb, tb_i=tb_i):
                            for c in range(8):
                                ins = e.matmul(psv[pb].t[:, :], ht.t[:, c, tb_i * 128:(tb_i + 1) * 128],
                                               wi.t[:, c, 2048 + cb * 512:2048 + (cb + 1) * 512], start=(c == 0), stop=(c == 7))
                            return ins
                        kb.op("pe", mmv, reads=wi.rs + [ht.r], writes=[psv[pb].r])
                        if cb < 4:
                            kb.op("act", lambda e, cb=cb, pb=pb, bb=bb: e.activation(out=vt[bb].t[:, cb * 512:(cb + 1) * 512], in_=psv[pb].t[:, :], func=AF.Copy),
                                  reads=[psv[pb].r], writes=[vt[bb].r])
                        else:
                            kb.op("act", lambda e, cb=cb, pb=pb, bb=bb: e.activation(out=gt[bb].t[:, (cb - 4) * 512:(cb - 3) * 512], in_=psv[pb].t[:, :], func=AF.Silu),
                                  reads=[psv[pb].r], writes=[gt[bb].r])
                    r0 = t0 + tb_i * 128
                    kb.dma(HSTQ, S["V"][s][r0:r0 + 128, :], vt[bb].t[:, :], reads=[vt[bb].r], writes=kb.dres(("V", s), r0, r0 + 128))
                    kb.dma(HSTQ, S["G"][s][r0:r0 + 128, :], gt[bb].t[:, :], reads=[gt[bb].r], writes=kb.dres(("G", s), r0, r0 + 128))
        kb.barrier()
    Lmax = max(seqs)
    NBmax = Lmax // 128
    with contextlib.ExitStack() as st:
        dl = kb.sb(st, "rdl", [128, 8], F32)
        lg = kb.sb(st, "rlg", [128, 8], F32)
        nlg = kb.sb(st, "rnlg", [128, 8], F32)
        dmat = kb.sb(st, "rdmat", [128, 128], F32)
        tri = kb.sb(st, "rtri", [128, 2, 128], F32)
        dvals = kb.sb(st, "rdvals", [128, 32], F32)
        sc = kb.sb(st, "rsc", [128, 8, 32], F32)
        Ff = kb.sb(st, "rFf", [128, 4, 128], F32)
        Fb = kb.sb(st, "rFb", [128, 4, 128], F32)
        Dg = kb.sb(st, "rDg", [128, 4, 128], F32)
        tmpm = kb.sb(st, "rtmpm", [128, 128], F32)
        gnw = kb.sb(st, "rgnw", [128, 2048], F32)
        kb.dma("sp", dl.t[:], P["b_decay_logit"][:, :], writes=[dl.r])
        kb.dma("sp", dmat.t[:], P["dmat"][:, :], writes=[dmat.r])
        kb.dma("sp", tri.t[:], P["tri"][:, :, :], writes=[tri.r])
        kb.dma("sp", dvals.t[:], P["dvals"][:, :], writes=[dvals.r])
        kb.dma("sp", gnw.t[:], P["b_gn_w"][:, :], writes=[gnw.r])
        kb.op("act", lambda e: e.activation(out=lg.t[:], in_=dl.t[:], func=AF.Exp, scale=math.log(2.0)), reads=[dl.r], writes=[lg.r])
        kb.op("act", lambda e: e.activation(out=lg.t[:], in_=lg.t[:], func=AF.Ln, scale=-1.0, bias=cst["one"].t[:, 0:1]),
              reads=[lg.r, cst["one"].r], writes=[lg.r])
        kb.op("dve", lambda e: e.tensor_scalar(out=nlg.t[:], in0=lg.t[:], scalar1=-1.0, scalar2=None, op0=ALU.mult), reads=[lg.r], writes=[nlg.r])
        for hd in range(4):
            kb.op("act", lambda e, hd=hd: e.activation(out=Ff.t[:, hd, :], in_=dmat.t[:], func=AF.Exp, scale=lg.t[:, hd:hd + 1]),
                  reads=[dmat.r, lg.r], writes=[Ff.r])
            kb.op("act", lambda e, hd=hd: e.activation(out=Fb.t[:, hd, :], in_=dmat.t[:], func=AF.Exp, scale=nlg.t[:, 4 + hd:5 + hd]),
                  reads=[dmat.r, nlg.r], writes=[Fb.r])
            kb.op("dve", lambda e, hd=hd: e.tensor_tensor(out=tmpm.t[:], in0=Ff.t[:, hd, :], in1=tri.t[:, 0, :], op=ALU.mult),
                  reads=[Ff.r, tri.r], writes=[tmpm.r])
            kb.op("dve", lambda e, hd=hd: e.tensor_tensor(out=Dg.t[:, hd, :], in0=Fb.t[:, hd, :], in1=tri.t[:, 1, :], op=ALU.mult),
                  reads=[Fb.r, tri.r], writes=[Dg.r])
            kb.op("dve", lambda e, hd=hd: e.tensor_tensor(out=Dg.t[:, hd, :], in0=Dg.t[:, hd, :], in1=tmpm.t[:], op=ALU.add),
                  reads=[Dg.r, tmpm.r], writes=[Dg.r])
        for k in range(8):
            kb.op("act", lambda e, k=k: e.activation(out=sc.t[:, k, :], in_=dvals.t[:], func=AF.Exp, scale=lg.t[:, k:k + 1]),
                  reads=[dvals.r, lg.r], writes=[sc.r])
        qT = kb.sb(st, "rqT", [128, 2, Lmax], BF16)
        kT = kb.sb(st, "rkT", [128, 2, Lmax], BF16)
        vh = kb.sb(st, "rvh", [128, NBmax, 512], BF16)
        pT = [kb.sb(st, "rpT%d" % i, [128, 128], BF16) for i in range(3)]
        sgt = [kb.sb(st, "rsg%d" % i, [128, 512], BF16) for i in range(2)]
        on = kb.sb(st, "ron", [128, 512], F32)
        on2 = kb.sb(st, "ron2", [128, 512], F32)
        utok = kb.sb(st, "rutok", [128, 512], BF16)
        uT = [kb.sb(st, "ruT%d" % i, [128, 4, 128], BF16) for i in range(2)]
        stats = kb.sb(st, "rstats", [128, 6], F32)
        mv = kb.sb(st, "rmv", [128, 2], F32)
        sd = kb.sb(st, "rsd", [128, 1], F32)
        rstd = kb.sb(st, "rrstd", [128, 1], F32)
        pss = [kb.ps(st, "rpss%d" % i, [128, 512]) for i in range(3)]
        pso = [kb.ps(st, "rpso%d" % i, [128, 512]) for i in range(2)]
        pst = kb.ps(st, "rpst", [128, 1024], BF16)
        for s, L in enumerate(seqs):
            NB = L // 128
            QKv = S["QK"][s].rearrange("(k p) l -> p k l", p=128)
            Vv = S["V"][s].rearrange("(b p) e -> p b e", p=128)
            Uv = S["U"][s].rearrange("(k p) l -> p k l", p=128)
            for hd in range(4):
                allr = kb.dres(("QK", s), 0, L)
                for l0 in range(0, L, 1024):
                    l1 = min(L, l0 + 1024)
                    kb.dma("sp", qT.t[:, :, l0:l1], QKv[:, 2 * hd:2 * hd + 2, l0:l1], reads=allr, writes=[qT.r])
                    kb.dma("sp", kT.t[:, :, l0:l1], QKv[:, 8 + 2 * hd:8 + 2 * hd + 2, l0:l1], reads=allr, writes=[kT.r])
                for b0 in range(0, NB, 4):
                    b1 = min(NB, b0 + 4)
                    kb.dma("sp", vh.t[:, b0:b1, :], Vv[:, b0:b1, hd * 512:(hd + 1) * 512], reads=kb.dres(("V", s), 0, L), writes=[vh.r])
                it = 0
                for i in range(NB):
                    ob = i % 2
                    kb.dma("sp", sgt[ob].t[:], S["G"][s][i * 128:(i + 1) * 128, hd * 512:(hd + 1) * 512],
                           reads=kb.dres(("G", s), i * 128, (i + 1) * 128), writes=[sgt[ob].r])
                    pend = []

                    def emit_mmo(j, b3, ob):
                        kb.op("pe", lambda e: e.matmul(pso[ob].t[:, :], pT[b3].t[:], vh.t[:, j, :], start=(j == 0), stop=(j == NB - 1)),
                              reads=[pT[b3].r, vh.r], writes=[pso[ob].r])
                    for j in range(NB):
                        b3 = it % 3
                        it += 1

                        def mms(e, j=j, i=i, b3=b3):
                            e.matmul(pss[b3].t[:, 0:128], kT.t[:, 0, j * 128:(j + 1) * 128], qT.t[:, 0, i * 128:(i + 1) * 128], start=True, stop=False)
                            return e.matmul(pss[b3].t[:, 0:128], kT.t[:, 1, j * 128:(j + 1) * 128], qT.t[:, 1, i * 128:(i + 1) * 128], start=False, stop=True)
                        kb.op("pe", mms, reads=[kT.r, qT.r], writes=[pss[b3].r])
                        if i == j:
                            kb.op("dve", lambda e, b3=b3: e.tensor_tensor(out=pT[b3].t[:], in0=pss[b3].t[:, 0:128], in1=Dg.t[:, hd, :], op=ALU.mult),
                                  reads=[pss[b3].r, Dg.r], writes=[pT[b3].r])
                        elif j < i:
                            kb.op("dve", lambda e, b3=b3, i=i, j=j: e.scalar_tensor_tensor(
                                out=pT[b3].t[:], in0=pss[b3].t[:, 0:128], scalar=sc.t[:, hd, (i - j):(i - j) + 1], op0=ALU.mult,
                                in1=Ff.t[:, hd, :], op1=ALU.mult), reads=[pss[b3].r, Ff.r, sc.r], writes=[pT[b3].r])
                        else:
                            kb.op("dve", lambda e, b3=b3, i=i, j=j: e.scalar_tensor_tensor(
                                out=pT[b3].t[:], in0=pss[b3].t[:, 0:128], scalar=sc.t[:, 4 + hd, (j - i):(j - i) + 1], op0=ALU.mult,
                                in1=Fb.t[:, hd, :], op1=ALU.mult), reads=[pss[b3].r, Fb.r, sc.r], writes=[pT[b3].r])
                        pend.append((j, b3))
                        if len(pend) > 2:
                            emit_mmo(*pend.pop(0), ob)
                    while pend:
                        emit_mmo(*pend.pop(0), ob)
                    kb.op("dve", lambda e, ob=ob: e.bn_stats(out=stats.t[:], in_=pso[ob].t[:, :]), reads=[pso[ob].r], writes=[stats.r])
                    kb.op("dve", lambda e: e.bn_aggr(out=mv.t[:], in_=stats.t[:]), reads=[stats.r], writes=[mv.r])
                    kb.op("act", lambda e: e.activation(out=sd.t[:], in_=mv.t[:, 1:2], func=AF.Sqrt, bias=cst["eps"].t[:, 0:1], scale=1.0),
                          reads=[mv.r, cst["eps"].r], writes=[sd.r])
                    kb.op("dve", lambda e: e.reciprocal(out=rstd.t[:], in_=sd.t[:]), reads=[sd.r], writes=[rstd.r])
                    kb.op("dve", lambda e, ob=ob: e.tensor_scalar(out=on.t[:], in0=pso[ob].t[:, :], scalar1=mv.t[:, 0:1], scalar2=rstd.t[:, 0:1],
                                                                   op0=ALU.subtract, op1=ALU.mult), reads=[pso[ob].r, mv.r, rstd.r], writes=[on.r])
                    kb.op("pool", lambda e: e.tensor_tensor(out=on2.t[:], in0=on.t[:], in1=gnw.t[:, hd * 512:(hd + 1) * 512], op=ALU.mult),
                          reads=[on.r, gnw.r], writes=[on2.r])
                    kb.op("pool", lambda e, ob=ob: e.tensor_tensor(out=utok.t[:], in0=on2.t[:], in1=sgt[ob].t[:], op=ALU.mult),
                          reads=[on2.r, sgt[ob].r], writes=[utok.r])

                    def tr(e):
                        for q in range(4):
                            ins = e.transpose(out=pst.t[:, q * 128:(q + 1) * 128], in_=utok.t[:, q * 128:(q + 1) * 128], identity=cst["ident_bf"].t[:])
                        return ins
                    kb.op("pe", tr, reads=[utok.r, cst["ident_bf"].r], writes=[pst.r])
                    kb.op("act", lambda e, ob=ob: e.activation(out=uT[ob].t[:].rearrange("p a b -> p (a b)"), in_=pst.t[:, 0:512], func=AF.Copy),
                          reads=[pst.r], writes=[uT[ob].r])
                    kb.dma(HSTQ, Uv[:, hd * 4:(hd + 1) * 4, i * 128:(i + 1) * 128], uT[ob].t[:], reads=[uT[ob].r],
                           writes=kb.dres(("U", s), i * 128, (i + 1) * 128))
        kb.barrier()
    outproj_phase(kb, cst, P["b_w_out"], None, S["U"], "U", 2048, X, seqs, layer)


G_H = 8


def gdn_stage(kb, cst, P, S, X, seqs, layer):
    nc = kb.nc
    ja = layer // 3
    T = 256
    W = P["a_w_in"][ja]
    with contextlib.ExitStack() as st:
        wi = kb.sb(st, "gwi", [128, 8, 4128], BF16, nres=8)
        wl = WLoader(kb, st, 2064, nbuf=2)
        for c in range(8):
            for hh in range(2):
                wl.load(wi.t[:, c, hh * 2064:(hh + 1) * 2064], W[c * 128:(c + 1) * 128, hh * 2064:(hh + 1) * 2064], wi.rs[c], 2064)
        N2 = T + 2
        nrm = NormCtx(kb, st, cst, N2)
        xt = kb.sb(st, "gxt", [128, 8, N2], F32)
        ht = kb.sb(st, "ght", [128, 8, N2], BF16)
        cw = kb.sb(st, "gcw", [128, 72], F32)
        kb.dma("sp", cw.t[:], P["a_conv_w"][ja], writes=[cw.r])
        nA = kb.sb(st, "gnA", [128, 16], F32)
        dtb = kb.sb(st, "gdtb", [128, 16], F32)
        kb.dma("sp", nA.t[:], P["a_a_log"][ja], writes=[nA.r])
        kb.dma("sp", dtb.t[:], P["a_dt_bias"][ja], writes=[dtb.r])
        kb.op("act", lambda e: e.activation(out=nA.t[:], in_=nA.t[:], func=AF.Exp), reads=[nA.r], writes=[nA.r])
        kb.op("dve", lambda e: e.tensor_scalar(out=nA.t[:], in0=nA.t[:], scalar1=-1.0, scalar2=None, op0=ALU.mult), reads=[nA.r], writes=[nA.r])
        NSL = 4
        pc = [kb.sb(st, "gpc%d" % i, [128, N2], F32) for i in range(NSL)]
        acc = [kb.sb(st, "gacc%d" % i, [128, T], F32) for i in range(NSL)]
        sl = [kb.sb(st, "gsl%d" % i, [128, T], F32) for i in range(NSL)]
        sqn = [kb.sb(st, "gsqn%d" % i, [128, T], BF16) for i in range(NSL)]
        rn = [kb.sb(st, "grn%d" % i, [128, T], F32) for i in range(NSL)]
        rn2 = [kb.sb(st, "grn2%d" % i, [128, T], F32) for i in range(NSL)]
        qkv = kb.sb(st, "gqkv", [128, 24, T], BF16, nres=24)
        zt = [kb.sb(st, "gzt%d" % i, [128, 1024], BF16) for i in range(2)]
        gbt = [kb.sb(st, "ggb%d" % i, [128, 32], F32) for i in range(2)]
        t16 = kb.sb(st, "gt16", [128, 2, 8], F32)
        psq = [kb.ps(st, "gpsq%d" % i, [128, 512]) for i in range(NSL)]
        psn_big = [kb.ps(st, "gpsn%d" % i, [128, 512]) for i in range(2)]
        psn = [Sub(psn_big[i // 2].t[:, (i % 2) * 256:(i % 2 + 1) * 256], psum=True) for i in range(NSL)]
        psz = [psq[0], psq[1]]
        psab = psq[2]

        def lockstep(gens):
            gens = list(gens)
            while gens:
                for g in list(gens):
                    try:
                        next(g)
                    except StopIteration:
                        gens.remove(g)
        for s, L in enumerate(seqs):
            Xv = X[s].rearrange("(c p) l -> p c l", p=128)
            QKVv = S["QKV"][s].rearrange("(k p) l -> p k l", p=128)
            for t0 in range(0, L, T):
                n = min(T, L - t0)
                n2 = n + 2
                lo, hi = max(t0 - 1, 0), min(t0 + n + 1, L)
                xr = kb.dres(("X", s), lo, hi)
                if t0 == 0:
                    kb.op("pool", lambda e: e.memset(xt.t[:, :, 0:1], 0.0), writes=[xt.r])
                if t0 + n == L:
                    kb.op("pool", lambda e: e.memset(xt.t[:, :, n + 1:n + 2], 0.0), writes=[xt.r])
                kb.dma("sp", xt.t[:, :, lo - (t0 - 1):hi - (t0 - 1)], Xv[:, :, lo:hi], reads=xr, writes=[xt.r])
                nrm.prenorm(xt, ht, n2, layer, 2)
                def chain(cc, b):
                    def mm(e):
                        for c in range(8):
                            ins = e.matmul(psq[b].t[:, :n2], wi.t[:, c, cc * 128:(cc + 1) * 128], ht.t[:, c, :n2],
                                           start=(c == 0), stop=(c == 7))
                        return ins
                    kb.op("pe", mm, reads=wi.rs + [ht.r], writes=[psq[b].r])
                    yield
                    kb.op("act", lambda e: e.activation(out=pc[b].t[:, :n2], in_=psq[b].t[:, :n2], func=AF.Copy), reads=[psq[b].r], writes=[pc[b].r])
                    yield
                    kb.op("pool", lambda e: e.tensor_scalar(out=acc[b].t[:, :n], in0=pc[b].t[:, 0:n], scalar1=cw.t[:, cc:cc + 1], scalar2=None, op0=ALU.mult),
                          reads=[pc[b].r, cw.r], writes=[acc[b].r])
                    yield
                    for tap in (1, 2):
                        kb.op("dve", lambda e: e.scalar_tensor_tensor(
                            out=acc[b].t[:, :n], in0=pc[b].t[:, tap:tap + n], scalar=cw.t[:, tap * 24 + cc:tap * 24 + cc + 1], op0=ALU.mult,
                            in1=acc[b].t[:, :n], op1=ALU.add), reads=[pc[b].r, cw.r, acc[b].r], writes=[acc[b].r])
                        yield
                    if cc >= 16:
                        kb.op("act", lambda e: e.activation(out=qkv.t[:, cc, :n], in_=acc[b].t[:, :n], func=AF.Silu), reads=[acc[b].r], writes=[qkv.rs[cc]])
                        yield
                    else:
                        kb.op("act", lambda e: e.activation(out=sl[b].t[:, :n], in_=acc[b].t[:, :n], func=AF.Silu), reads=[acc[b].r], writes=[sl[b].r])
                        yield
                        kb.op("act", lambda e: e.activation(out=sqn[b].t[:, :n], in_=sl[b].t[:, :n], func=AF.Square), reads=[sl[b].r], writes=[sqn[b].r])
                        yield
                        kb.op("pe", lambda e: e.matmul(psn[b].t[:, :n], cst["ones_bf"].t[:], sqn[b].t[:, :n], start=True, stop=True),
                              reads=[sqn[b].r, cst["ones_bf"].r], writes=[psn[b].r])
                        yield
                        kb.op("act", lambda e: e.activation(out=rn2[b].t[:, :n], in_=psn[b].t[:, :n], func=AF.Sqrt, bias=cst["eps"].t[:, 0:1], scale=1.0),
                              reads=[psn[b].r, cst["eps"].r], writes=[rn2[b].r])
                        yield
                        kb.op("dve", lambda e: e.reciprocal(out=rn[b].t[:, :n], in_=rn2[b].t[:, :n]), reads=[rn2[b].r], writes=[rn[b].r])
                        yield
                        scl = (128.0 ** -0.5) if cc < 8 else 1.0
                        kb.op("dve", lambda e: e.scalar_tensor_tensor(
                            out=qkv.t[:, cc, :n], in0=sl[b].t[:, :n], scalar=scl, op0=ALU.mult, in1=rn[b].t[:, :n], op1=ALU.mult),
                            reads=[sl[b].r, rn[b].r], writes=[qkv.rs[cc]])
                        yield
                rolling(list(range(24)), NSL, chain)
                kb.dma(HSTQ, QKVv[:, :, t0:t0 + n], qkv.t[:, :, :n], reads=qkv.rs, writes=kb.dres(("QKV", s), t0, t0 + n))
                for tb_i in range(n // 128):
                    bb = tb_i % 2
                    c0 = 1 + tb_i * 128
                    for zb in range(2):
                        def mmz(e, zb=zb, c0=c0):
                            for c in range(8):
                                ins = e.matmul(psz[zb].t[:, :], ht.t[:, c, c0:c0 + 128], wi.t[:, c, 3072 + zb * 512:3072 + (zb + 1) * 512],
                                               start=(c == 0), stop=(c == 7))
                            return ins
                        kb.op("pe", mmz, reads=wi.rs + [ht.r], writes=[psz[zb].r])
                        kb.op("act", lambda e, zb=zb, bb=bb: e.activation(out=zt[bb].t[:, zb * 512:(zb + 1) * 512], in_=psz[zb].t[:, :], func=AF.Silu),
                              reads=[psz[zb].r], writes=[zt[bb].r])

                    def mmab(e, c0=c0):
                        for c in range(8):
                            ins = e.matmul(psab.t[:, 0:32], ht.t[:, c, c0:c0 + 128], wi.t[:, c, 4096:4128], start=(c == 0), stop=(c == 7))
                        return ins
                    kb.op("pe", mmab, reads=wi.rs + [ht.r], writes=[psab.r])
                    abv = psab.t[:, 0:32].rearrange("p (d k) -> p d k", d=2)
                    kb.op("dve", lambda e: e.tensor_tensor(out=t16.t[:], in0=abv[:, :, 0:8], in1=dtb.t[:].rearrange("p (d k) -> p d k", d=2), op=ALU.add),
                          reads=[psab.r, dtb.r], writes=[t16.r])
                    kb.op("dve", lambda e, bb=bb: e.tensor_copy(out=gbt[bb].t[:, 16:32].rearrange("p (d k) -> p d k", d=2), in_=abv[:, :, 8:16]),
                          reads=[psab.r], writes=[gbt[bb].r])
                    kb.op("act", lambda e: e.activation(out=t16.t[:], in_=t16.t[:], func=AF.Exp), reads=[t16.r], writes=[t16.r])
                    kb.op("act", lambda e: e.activation(out=t16.t[:], in_=t16.t[:], func=AF.Ln, bias=cst["one"].t[:, 0:1], scale=1.0),
                          reads=[t16.r, cst["one"].r], writes=[t16.r])
                    kb.op("act", lambda e, bb=bb: e.activation(out=gbt[bb].t[:, 16:32], in_=gbt[bb].t[:, 16:32], func=AF.Sigmoid), reads=[gbt[bb].r], writes=[gbt[bb].r])
                    kb.op("dve", lambda e, bb=bb: e.tensor_tensor(out=gbt[bb].t[:, 0:16], in0=t16.t[:].rearrange("p d k -> p (d k)"), in1=nA.t[:], op=ALU.mult),
                          reads=[t16.r, nA.r, gbt[bb].r], writes=[gbt[bb].r])
                    r0 = t0 + tb_i * 128
                    kb.dma(HSTQ, S["Z"][s][r0:r0 + 128, :], zt[bb].t[:, :], reads=[zt[bb].r], writes=kb.dres(("Z", s), r0, r0 + 128))
                    kb.dma(HSTQ, S["GB"][s][r0:r0 + 128, :], gbt[bb].t[:, :], reads=[gbt[bb].r], writes=kb.dres(("GB", s), r0, r0 + 128))
        kb.barrier()
    AX = mybir.AxisListType
    HS = 4
    W4 = HS * 128
    with contextlib.ExitStack() as st:
        gm = kb.sb(st, "ggm", [128, 4, 128], F32)
        negm = kb.sb(st, "gnegm", [128, 4, W4], F32)
        ones_f = kb.sb(st, "gones", [128, 128], F32)
        nw = kb.sb(st, "gnw", [128, 1024], F32)
        bmk = kb.sb(st, "gbmk", [128, 4, W4], BF16)
        bmf = kb.sb(st, "gbmf", [128, W4], F32)
        gmb = kb.sb(st, "ggmb", [128, 4, 128], BF16)
        negb = kb.sb(st, "gnegb", [128, 4, W4], BF16)
        kb.dma("sp", gm.t[:], P["gmask"][:, :, :], writes=[gm.r])
        kb.op("dve", lambda e: e.tensor_copy(out=gmb.t[:].rearrange("p a b -> p (a b)"), in_=gm.t[:].rearrange("p a b -> p (a b)")), reads=[gm.r], writes=[gmb.r])
        for i in range(4):
            kb.dma("sp", negm.t[:, i, :], P["negm"][:, i, 0:W4], writes=[negm.r])
            kb.dma("sp", bmf.t[:], P["bmask"][:, i, 0:W4], writes=[bmf.r])
            kb.op("dve", lambda e, i=i: e.tensor_copy(out=bmk.t[:, i, :], in_=bmf.t[:]), reads=[bmf.r], writes=[bmk.r])
        kb.op("dve", lambda e: e.tensor_copy(out=negb.t[:].rearrange("p a b -> p (a b)"), in_=negm.t[:].rearrange("p a b -> p (a b)")), reads=[negm.r], writes=[negb.r])
        kb.dma("sp", nw.t[:], P["a_norm_w"][ja], writes=[nw.r])
        kb.op("pool", lambda e: e.memset(ones_f.t[:], 1.0), writes=[ones_f.r])

        def alloc_half(hh):
            t = {}
            n3 = lambda nm, dt: kb.sb(st, "g%s%d" % (nm, hh), [128, HS, 128], dt)
            t["Q"] = [kb.sb(st, "gQ%d_%d" % (hh, i), [128, 3, HS, 128], BF16) for i in range(2)]
            t["G"] = [kb.sb(st, "gG%d_%d" % (hh, i), [128, 32], F32) for i in range(2)]
            for nm in ("DAs", "DTc", "Yf", "Tf", "tav", "ot", "Sf"):
                t[nm] = n3(nm, F32)
            for nm in ("GmA", "GmT", "Yb", "Tb", "NGb", "vb", "kbg", "kdec", "nwT", "vn", "attnT", "Sb"):
                t[nm] = n3(nm, BF16)
            t["Pb"] = [n3("Pb%d_" % i, BF16) for i in range(2)]
            t["Ppb"] = [n3("Ppb%d_" % i, BF16) for i in range(2)]
            t["Bm"] = [n3("Bm%d_" % i, BF16) for i in range(3)]
            t["gc16"] = kb.sb(st, "ggc%d" % hh, [128, 2 * HS], F32)
            t["sm"] = kb.sb(st, "gsm%d" % hh, [128, 6, HS], F32)
            t["PX"] = kb.ps(st, "gPX%d" % hh, [128, 512])
            t["PY"] = kb.ps(st, "gPY%d" % hh, [128, 512])
            return t
        TL = [[alloc_half(0), alloc_half(1)], [alloc_half(2), alloc_half(3)]]

        def flat(t):
            return t.t[:].rearrange("p h m -> p (h m)")

        def hs(h):
            return slice(h * 128, (h + 1) * 128)

        def mm4(ps, lhs, rhs, lsel=None, rsel=None):
            def f(e):
                for h in range(HS):
                    l = lhs.t[:, h, :] if lsel is None else lhs.t[:, lsel, h, :]
                    r = rhs.t[:, h, :] if rsel is None else rhs.t[:, rsel, h, :]
                    ins = e.matmul(ps.t[:, hs(h)], l, r, start=True, stop=True)
                return ins
            return f

        def bfv(ps):
            return ps.t[:].bitcast(BF16)

        def tr4(ps, src, sel=None, off=0):
            def f(e):
                pv = bfv(ps)
                for h in range(HS):
                    i_ = src.t[:, h, :] if sel is None else src.t[:, sel, h, :]
                    ins = e.transpose(out=pv[:, off + h * 128:off + (h + 1) * 128], in_=i_, identity=cst["ident_bf"].t[:])
                return ins
            return f

        def half_iter(t, hh, s, dr, it, cb, QKVv4, Uv):
            h0 = hh * HS
            x0, x1 = (0, 1) if dr == 0 else (2, 3)
            c0 = cb * 128
            bq = it % 2
            Q, G = t["Q"][bq], t["G"][bq]
            PX, PY = t["PX"], t["PY"]
            sm, gc16 = t["sm"], t["gc16"]
            GmA, GmT, DAs, DTc = t["GmA"], t["GmT"], t["DAs"], t["DTc"]
            Pb, Ppb, Bm, Yf, Yb, Tf, Tb, NGb = t["Pb"], t["Ppb"], t["Bm"], t["Yf"], t["Yb"], t["Tf"], t["Tb"], t["NGb"]
            vb, kbg, kdec, nwT, vn, attnT, tav, ot, Sf, Sb = t["vb"], t["kbg"], t["kdec"], t["nwT"], t["vn"], t["attnT"], t["tav"], t["ot"], t["Sf"], t["Sb"]
            for tq in range(3):
                kb.dma("sp", Q.t[:, tq, :, :], QKVv4[:, tq, h0:h0 + HS, c0:c0 + 128], reads=kb.dres(("QKV", s), c0, c0 + 128), writes=[Q.r])
            kb.dma("sp", G.t[:], S["GB"][s][c0:c0 + 128, :], reads=kb.dres(("GB", s), c0, c0 + 128), writes=[G.r])
            yield
            gofs = dr * 8 + h0
            g8 = G.t[:, gofs:gofs + HS]
            be8 = G.t[:, 16 + gofs:16 + gofs + HS]
            for h in range(HS):
                gcolh = G.t[:, gofs + h:gofs + h + 1]
                kb.op("pool", lambda e: e.tensor_scalar(out=GmA.t[:, h, :], in0=gm.t[:, x1, :], scalar1=gcolh, scalar2=None, op0=ALU.mult),
                      reads=[gm.r, G.r], writes=[GmA.r])
                kb.op("pool", lambda e: e.tensor_scalar(out=GmT.t[:, h, :], in0=gm.t[:, x0, :], scalar1=gcolh, scalar2=None, op0=ALU.mult),
                      reads=[gm.r, G.r], writes=[GmT.r])
                yield

            def mmA(e):
                e.matmul(PX.t[:, :], gmb.t[:, x0, :], flat(GmA), start=True, stop=False)
                return e.matmul(PX.t[:, :], cst["ident_bf"].t[:], negb.t[:, x0, :], start=False, stop=True)
            kb.op("pe", mmA, reads=[gmb.r, GmA.r, negb.r, cst["ident_bf"].r], writes=[PX.r])
            kb.op("act", lambda e: e.activation(out=flat(DAs), in_=PX.t[:], func=AF.Exp), reads=[PX.r], writes=[DAs.r])
            yield

            def mmT(e):
                e.matmul(PY.t[:, :], gmb.t[:, x1, :], flat(GmT), start=True, stop=False)
                return e.matmul(PY.t[:, :], cst["ident_bf"].t[:], negb.t[:, x1, :], start=False, stop=True)
            kb.op("pe", mmT, reads=[gmb.r, GmT.r, negb.r, cst["ident_bf"].r], writes=[PY.r])
            kb.op("act", lambda e: e.activation(out=flat(DTc), in_=PY.t[:], func=AF.Exp), reads=[PY.r], writes=[DTc.r])
            yield

            def mmg(e):
                e.matmul(PX.t[:, 0:HS], gm.t[:, x0, :], g8, start=True, stop=True)
                return e.matmul(PX.t[:, HS:2 * HS], ones_f.t[:], g8, start=True, stop=True)
            kb.op("pe", mmg, reads=[gm.r, G.r, ones_f.r], writes=[PX.r])
            kb.op("dve", lambda e: e.tensor_copy(out=gc16.t[:], in_=PX.t[:, 0:2 * HS]), reads=[PX.r], writes=[gc16.r])
            yield
            kb.op("act", lambda e: e.activation(out=sm.t[:, 0:2, :].rearrange("p a k -> p (a k)"), in_=gc16.t[:], func=AF.Exp), reads=[gc16.r], writes=[sm.r])
            kb.op("dve", lambda e: e.tensor_tensor(out=sm.t[:, 5, :], in0=gc16.t[:, HS:2 * HS], in1=gc16.t[:, 0:HS], op=ALU.subtract), reads=[gc16.r, sm.r], writes=[sm.r])
            yield
            kb.op("act", lambda e: e.activation(out=sm.t[:, 2, :], in_=sm.t[:, 5, :], func=AF.Exp), reads=[sm.r], writes=[sm.r])
            kb.op("dve", lambda e: e.tensor_tensor(out=sm.t[:, 3, :], in0=sm.t[:, 0, :], in1=be8, op=ALU.mult), reads=[sm.r, G.r], writes=[sm.r])
            kb.op("dve", lambda e: e.tensor_scalar(out=sm.t[:, 4, :], in0=be8, scalar1=-1.0, scalar2=None, op0=ALU.mult), reads=[G.r, sm.r], writes=[sm.r])
            yield
            kb.op("pe", tr4(PY, Q, sel=1, off=0), reads=[Q.r, cst["ident_bf"].r], writes=[PY.r])
            kb.op("pe", tr4(PY, Q, sel=2, off=W4), reads=[Q.r, cst["ident_bf"].r], writes=[PY.r])
            yield
            for h in range(HS):
                kb.op("act", lambda e: e.activation(out=kbg.t[:, h, :], in_=bfv(PY)[:, hs(h)], func=AF.Copy, scale=sm.t[:, 3, h:h + 1]),
                      reads=[PY.r, sm.r], writes=[kbg.r])
                kb.op("act", lambda e: e.activation(out=kdec.t[:, h, :], in_=bfv(PY)[:, hs(h)], func=AF.Copy, scale=sm.t[:, 2, h:h + 1]),
                      reads=[PY.r, sm.r], writes=[kdec.r])
                kb.op("act", lambda e: e.activation(out=vb.t[:, h, :], in_=bfv(PY)[:, W4 + h * 128:W4 + (h + 1) * 128], func=AF.Copy, scale=G.t[:, 16 + gofs + h:16 + gofs + h + 1]),
                      reads=[PY.r, G.r], writes=[vb.r])
                yield
            kb.op("pe", mm4(PX, Q, Q, lsel=1, rsel=1), reads=[Q.r], writes=[PX.r])
            yield
            for h in range(HS):
                kb.op("dve", lambda e: e.scalar_tensor_tensor(out=Pb[0].t[:, h, :], in0=PX.t[:, hs(h)], scalar=sm.t[:, 4, h:h + 1], op0=ALU.mult,
                                                               in1=DAs.t[:, h, :], op1=ALU.mult), reads=[PX.r, sm.r, DAs.r], writes=[Pb[0].r])
                yield
            kb.op("pe", tr4(PY, Pb[0]), reads=[Pb[0].r, cst["ident_bf"].r], writes=[PY.r])
            kb.op("dve", lambda e: e.tensor_copy(out=flat(Ppb[0]), in_=bfv(PY)[:, 0:W4]), reads=[PY.r], writes=[Ppb[0].r])
            yield
            for lv in range(3):
                kb.op("pool", lambda e: e.tensor_tensor(out=flat(Bm[lv]), in0=flat(Ppb[0]), in1=bmk.t[:, 1 + lv, :], op=ALU.mult),
                      reads=[Ppb[0].r, bmk.r], writes=[Bm[lv].r])
                yield
            kb.op("pool", lambda e: e.tensor_tensor(out=flat(Pb[1]), in0=flat(Pb[0]), in1=bmk.t[:, 0, :], op=ALU.mult), reads=[Pb[0].r, bmk.r], writes=[Pb[1].r])
            kb.op("pool", lambda e: e.tensor_tensor(out=flat(Ppb[1]), in0=flat(Ppb[0]), in1=bmk.t[:, 0, :], op=ALU.mult), reads=[Ppb[0].r, bmk.r], writes=[Ppb[1].r])
            yield
            for h in range(HS):
                kb.op("pool", lambda e: e.tensor_tensor(out=Yf.t[:, h, :], in0=Ppb[1].t[:, h, :], in1=cst["ident_f"].t[:], op=ALU.add),
                      reads=[Ppb[1].r, cst["ident_f"].r], writes=[Yf.r])
            kb.op("act", lambda e: e.activation(out=flat(Yb), in_=flat(Yf), func=AF.Copy), reads=[Yf.r], writes=[Yb.r])
            yield
            for k in range(3):
                cur, nxt = 1 - (k % 2), k % 2
                kb.op("pe", mm4(PX, Ppb[cur], Pb[cur]), reads=[Ppb[cur].r, Pb[cur].r], writes=[PX.r])
                if k < 2:
                    kb.op("pe", mm4(PY, Pb[cur], Ppb[cur]), reads=[Ppb[cur].r, Pb[cur].r], writes=[PY.r])
                yield
                kb.op("act", lambda e: e.activation(out=flat(Pb[nxt]), in_=PX.t[:], func=AF.Copy), reads=[PX.r], writes=[Pb[nxt].r])
                if k < 2:
                    kb.op("dve", lambda e: e.tensor_copy(out=flat(Ppb[nxt]), in_=PY.t[:]), reads=[PY.r], writes=[Ppb[nxt].r])
                yield
                kb.op("pe", mm4(PX, Pb[nxt], Yb), reads=[Pb[nxt].r, Yb.r], writes=[PX.r])
                yield
                kb.op("dve", lambda e: e.tensor_tensor(out=flat(Yf), in0=PX.t[:], in1=flat(Yf), op=ALU.add), reads=[PX.r, Yf.r], writes=[Yf.r])
                yield
                kb.op("act", lambda e: e.activation(out=flat(Yb), in_=flat(Yf), func=AF.Copy), reads=[Yf.r], writes=[Yb.r])
                yield
            kb.op("pe", tr4(PY, Yb), reads=[Yb.r, cst["ident_bf"].r], writes=[PY.r])
            yield
            kb.op("dve", lambda e: e.tensor_copy(out=flat(Tf), in_=bfv(PY)[:, 0:W4]), reads=[PY.r], writes=[Tf.r])
            yield
            kb.op("act", lambda e: e.activation(out=flat(Tb), in_=flat(Tf), func=AF.Copy), reads=[Tf.r], writes=[Tb.r])
            yield
            for lv in range(3):
                kb.op("pe", mm4(PX, Bm[lv], Tb), reads=[Bm[lv].r, Tb.r], writes=[PX.r])
                yield
                kb.op("act", lambda e: e.activation(out=flat(NGb), in_=PX.t[:], func=AF.Copy), reads=[PX.r], writes=[NGb.r])
                yield
                kb.op("pe", mm4(PY, Yb, NGb), reads=[Yb.r, NGb.r], writes=[PY.r])
                yield
                kb.op("dve", lambda e: e.tensor_tensor(out=flat(Tf), in0=PY.t[:], in1=flat(Tf), op=ALU.add), reads=[PY.r, Tf.r], writes=[Tf.r])
                yield
                kb.op("act", lambda e: e.activation(out=flat(Tb), in_=flat(Tf), func=AF.Copy), reads=[Tf.r], writes=[Tb.r])
                yield
                kb.op("pe", tr4(PX, Tb), reads=[Tb.r, cst["ident_bf"].r], writes=[PX.r])
                yield
                kb.op("dve", lambda e: e.tensor_copy(out=flat(Yb), in_=bfv(PX)[:, 0:W4]), reads=[PX.r], writes=[Yb.r])
                yield
            kb.op("pe", mm4(PX, kbg, Yb), reads=[kbg.r, Yb.r], writes=[PX.r])
            yield
            kb.op("act", lambda e: e.activation(out=flat(nwT), in_=PX.t[:], func=AF.Copy, scale=-1.0), reads=[PX.r], writes=[nwT.r])
            yield

            def mmvn(e):
                for h in range(HS):
                    e.matmul(PY.t[:, hs(h)], Yb.t[:, h, :], vb.t[:, h, :], start=True, stop=False)
                    ins = e.matmul(PY.t[:, hs(h)], nwT.t[:, h, :], Sb.t[:, h, :], start=False, stop=True)
                return ins
            kb.op("pe", mmvn, reads=[Yb.r, vb.r, nwT.r, Sb.r], writes=[PY.r])
            yield
            kb.op("act", lambda e: e.activation(out=flat(vn), in_=PY.t[:], func=AF.Copy), reads=[PY.r], writes=[vn.r])
            kb.op("pe", mm4(PX, Q, Q, lsel=1, rsel=0), reads=[Q.r], writes=[PX.r])
            yield
            kb.op("dve", lambda e: e.tensor_tensor(out=flat(attnT), in0=PX.t[:], in1=flat(DTc), op=ALU.mult), reads=[PX.r, DTc.r], writes=[attnT.r])
            kb.op("pe", mm4(PY, Q, Sb, lsel=0), reads=[Q.r, Sb.r], writes=[PY.r])
            yield
            kb.op("pe", mm4(PX, attnT, vn), reads=[attnT.r, vn.r], writes=[PX.r])
            yield
            kb.op("act", lambda e: e.activation(out=flat(tav), in_=PX.t[:], func=AF.Copy), reads=[PX.r], writes=[tav.r])
            yield
            for h in range(HS):
                kb.op("dve", lambda e: e.scalar_tensor_tensor(out=ot.t[:, h, :], in0=PY.t[:, hs(h)], scalar=sm.t[:, 0, h:h + 1], op0=ALU.mult,
                                                               in1=tav.t[:, h, :], op1=ALU.add), reads=[PY.r, sm.r, tav.r], writes=[ot.r])
                yield
            kb.op("pe", mm4(PX, kdec, vn), reads=[kdec.r, vn.r], writes=[PX.r])
            yield
            for h in range(HS):
                kb.op("dve", lambda e: e.scalar_tensor_tensor(out=Sf.t[:, h, :], in0=Sf.t[:, h, :], scalar=sm.t[:, 1, h:h + 1], op0=ALU.mult,
                                                               in1=PX.t[:, hs(h)], op1=ALU.add), reads=[PX.r, sm.r, Sf.r], writes=[Sf.r])
                yield
            kb.op("act", lambda e: e.activation(out=flat(Sb), in_=flat(Sf), func=AF.Copy), reads=[Sf.r], writes=[Sb.r])
            yield
            onm = "OF" if dr == 0 else "OB"
            kb.dma(HSTQ, S[onm][s][c0:c0 + 128, h0 * 128:(h0 + HS) * 128], flat(ot), reads=[ot.r], writes=kb.dres((onm + str(hh), s), c0, c0 + 128))
            yield

        def lockstep(gens):
            gens = list(gens)
            while gens:
                for g in list(gens):
                    try:
                        next(g)
                    except StopIteration:
                        gens.remove(g)

        for s, L in enumerate(seqs):
            NB = L // 128
            QKVv4 = S["QKV"][s].rearrange("(t h p) l -> p t h l", t=3, h=8, p=128)
            for dr in range(2):
                for hh in range(2):
                    kb.op("pool", lambda e: e.memset(flat(TL[dr][hh]["Sf"]), 0.0), writes=[TL[dr][hh]["Sf"].r])
                    kb.op("pool", lambda e: e.memset(flat(TL[dr][hh]["Sb"]), 0.0), writes=[TL[dr][hh]["Sb"].r])
            for it in range(NB):
                if GDN_SKIP_B:
                    break
                lockstep([half_iter(TL[dr][hh], hh, s, dr, it, (it if dr == 0 else NB - 1 - it), QKVv4, None)
                          for dr in range(2) for hh in range(2)])
        kb.barrier()
    with contextlib.ExitStack() as st:
        nw = kb.sb(st, "gnw2", [128, 1024], F32)
        kb.dma("sp", nw.t[:], P["a_norm_w"][ja], writes=[nw.r])
        NCH = 3

        def talloc(i):
            t = {}
            t["of"] = kb.sb(st, "gtof%d" % i, [128, 8, 128], F32)
            t["ob"] = kb.sb(st, "gtob%d" % i, [128, 8, 128], F32)
            t["zl"] = kb.sb(st, "gtzl%d" % i, [128, 1024], BF16)
            t["sq"] = kb.sb(st, "gtsq%d" % i, [128, 8, 128], F32)
            t["utok"] = kb.sb(st, "gtut%d" % i, [128, 8, 128], BF16)
            t["uT"] = kb.sb(st, "gtuT%d" % i, [128, 8, 128], BF16)
            t["ss8"] = kb.sb(st, "gtss%d" % i, [128, 8], F32)
            t["rs8"] = kb.sb(st, "gtrs%d" % i, [128, 8], F32)
            t["PK"] = kb.ps(st, "gtPK%d" % i, [128, 1024], BF16)
            return t
        TT = [talloc(i) for i in range(NCH)]

        def flat(t):
            return t.t[:].rearrange("p h m -> p (h m)")

        def tail(t, s, c0, Uv):
            of, ob, zl, sq, utok, uT, ss8, rs8, PK = t["of"], t["ob"], t["zl"], t["sq"], t["utok"], t["uT"], t["ss8"], t["rs8"], t["PK"]
            kb.dma("sp", flat(of), S["OF"][s][c0:c0 + 128, :], reads=kb.dres(("OF0", s), c0, c0 + 128) + kb.dres(("OF1", s), c0, c0 + 128), writes=[of.r])
            kb.dma("sp", flat(ob), S["OB"][s][c0:c0 + 128, :], reads=kb.dres(("OB0", s), c0, c0 + 128) + kb.dres(("OB1", s), c0, c0 + 128), writes=[ob.r])
            kb.dma("sp", zl.t[:], S["Z"][s][c0:c0 + 128, :], reads=kb.dres(("Z", s), c0, c0 + 128), writes=[zl.r])
            yield
            kb.op("pool", lambda e: e.tensor_tensor(out=flat(of), in0=flat(of), in1=flat(ob), op=ALU.add), reads=[of.r, ob.r], writes=[of.r])
            yield
            kb.op("act", lambda e: e.activation(out=flat(sq), in_=flat(of), func=AF.Square), reads=[of.r], writes=[sq.r])
            yield
            kb.op("dve", lambda e: e.tensor_reduce(out=ss8.t[:], in_=sq.t[:], axis=AX.X, op=ALU.add), reads=[sq.r], writes=[ss8.r])
            yield
            kb.op("act", lambda e: e.activation(out=rs8.t[:], in_=ss8.t[:], func=AF.Sqrt, bias=cst["eps"].t[:, 0:1], scale=1.0 / 128.0),
                  reads=[ss8.r, cst["eps"].r], writes=[rs8.r])
            yield
            kb.op("dve", lambda e: e.reciprocal(out=rs8.t[:], in_=rs8.t[:]), reads=[rs8.r], writes=[rs8.r])
            yield
            for h in range(8):
                kb.op("dve", lambda e: e.scalar_tensor_tensor(out=sq.t[:, h, :], in0=of.t[:, h, :], scalar=rs8.t[:, h:h + 1], op0=ALU.mult,
                                                               in1=nw.t[:, h * 128:(h + 1) * 128], op1=ALU.mult), reads=[of.r, rs8.r, nw.r, sq.r], writes=[sq.r])
                yield
            kb.op("pool", lambda e: e.tensor_tensor(out=flat(utok), in0=flat(sq), in1=zl.t[:], op=ALU.mult), reads=[sq.r, zl.r], writes=[utok.r])
            yield

            def tr(e):
                for h in range(8):
                    ins = e.transpose(out=PK.t[:, h * 128:(h + 1) * 128], in_=utok.t[:, h, :], identity=cst["ident_bf"].t[:])
                return ins
            kb.op("pe", tr, reads=[utok.r, cst["ident_bf"].r], writes=[PK.r])
            yield
            kb.op("act", lambda e: e.activation(out=flat(uT), in_=PK.t[:], func=AF.Copy), reads=[PK.r], writes=[uT.r])
            yield
            kb.dma(HSTQ, Uv[:, 0:8, c0:c0 + 128], uT.t[:], reads=[uT.r], writes=kb.dres(("U", s), c0, c0 + 128))
            yield

        def lockstep2(gens):
            gens = list(gens)
            while gens:
                for g in list(gens):
                    try:
                        next(g)
                    except StopIteration:
                        gens.remove(g)
        for s, L in enumerate(seqs):
            Uv = S["U"][s].rearrange("(k p) l -> p k l", p=128)
            blocks = list(range(0, L, 128))
            for g0 in range(0, len(blocks), NCH):
                lockstep2([tail(TT[i], s, c0, Uv) for i, c0 in enumerate(blocks[g0:g0 + NCH])])
        kb.barrier()
    outproj_phase(kb, cst, P["a_w_out"][ja], None, S["U"], "U", 1024, X, seqs, layer)


TWO_PI = 2.0 * math.pi


def cast_load(kb, wl, dst2d, src2d, res, rows, width, chunk=1024):
    for c0 in range(0, width, chunk):
        c1 = min(width, c0 + chunk)
        wl.load(dst2d[:, c0:c1], src2d[:, c0:c1], res, c1 - c0, rows=rows)


def hy_tables(kb, st, P, L):
    N1 = 2 * L // 64
    tb = {}
    tb["f1t"] = kb.sb(st, "hf1t", [N1, 2 * N1], BF16)
    tb["tw"] = kb.sb(st, "htw", [64, N1 * 192], BF16)
    tb["twt"] = kb.sb(st, "htwt", [64, N1 * 192], BF16)
    tb["g2t"] = kb.sb(st, "hg2t", [N1, N1], BF16)
    with contextlib.ExitStack() as st_tmp:
        wl = WLoader(kb, st_tmp, 1024, nbuf=2)
        cast_load(kb, wl, tb["f1t"].t[:, :], P["hy_f1t_%d" % L][:, :], tb["f1t"].r, N1, 2 * N1)
        cast_load(kb, wl, tb["tw"].t[:, :], P["hy_tw_%d" % L][:, :], tb["tw"].r, 64, N1 * 192)
        cast_load(kb, wl, tb["twt"].t[:, :], P["hy_twt_%d" % L][:, :], tb["twt"].r, 64, N1 * 192)
        cast_load(kb, wl, tb["g2t"].t[:, :], P["hy_g2t_%d" % L][:, :], tb["g2t"].r, N1, N1)
        kb.barrier()
    return tb


def hy_F1(kb, tb, Q, ab, srcv, K, N1, C, D1, d1name):
    xt = tb["xt"]

    def chain(n2, c5, b):
        cs = slice(c5 * 512, (c5 + 1) * 512)
        kb.dma("sp", xt[b].t[:K, :], srcv[0:K, n2, cs], reads=tb["src_res"], writes=[xt[b].r])
        yield

        def mm(e):
            for ri in range(2):
                ins = e.matmul(tb["QF"][2 * b + ri].t[:N1, 0:512], tb["f1t"].t[:K, ri * N1:(ri + 1) * N1], xt[b].t[:K, :], start=True, stop=True)
            return ins
        kb.op("pe", mm, reads=[tb["f1t"].r, xt[b].r], writes=[tb["QF"][2 * b].r, tb["QF"][2 * b + 1].r])
        yield
        kb.op("act", lambda e: e.activation(out=ab[b].t[:N1, 0, :], in_=tb["QF"][2 * b].t[:N1, 0:512], func=AF.Copy), reads=[tb["QF"][2 * b].r], writes=[ab[b].r])
        kb.op("dve", lambda e: e.tensor_copy(out=ab[b].t[:N1, 1, :], in_=tb["QF"][2 * b + 1].t[:N1, 0:512]), reads=[tb["QF"][2 * b + 1].r], writes=[ab[b].r])
        yield
        kb.dma(HSTQ, D1[0:N1, n2, :, cs], ab[b].t[:N1, :, :], reads=[ab[b].r], writes=kb.dres(d1name, 0, 128))
        yield

    items = [(n2, c5) for n2 in range(64) for c5 in range(C // 512)]
    for i0 in range(0, len(items), 4):
        hy_lockstep([chain(n2, c5, b) for b, (n2, c5) in enumerate(items[i0:i0 + 4])])


def hy_lockstep(gens):
    gens = list(gens)
    while gens:
        for g in list(gens):
            try:
                next(g)
            except StopIteration:
                gens.remove(g)


def hy_F2(kb, tb, Q, N1, C, D1, d1name, mode, HS=None, hsname=None, order=0, D2=None, d2name=None):
    a2, hb, xr, xi, tt, yb, zb = tb["a2"], tb["hb"], tb["xr"], tb["xi"], tb["tt"], tb["yb"], tb["zb"]
    tw, twt = tb["tw"], tb["twt"]

    def chain(k1, c5, b):
        QR, QI = tb["QF"][2 * b], tb["QF"][2 * b + 1]
        cw = slice(c5 * 512, (c5 + 1) * 512)
        kb.dma("sp", a2[b].t[:, :, :], D1[k1, :, :, cw], reads=kb.dres(d1name, 0, 128), writes=[a2[b].r])
        if mode != "H":
            kb.dma("sp", hb[b].t[:, :, :], HS[k1, :, :, order * 1024 + c5 * 512:order * 1024 + (c5 + 1) * 512], reads=kb.dres(hsname, 0, 128), writes=[hb[b].r])
        yield

        def tws(j):
            return tw.t[:, (k1 * 3 + j) * 64:(k1 * 3 + j + 1) * 64]

        def twts(j):
            return twt.t[:, (k1 * 3 + j) * 64:(k1 * 3 + j + 1) * 64]

        def mmre(e):
            for hf in range(1):
                cs = slice(0, 512)
                e.matmul(QR.t[:64, cs], tws(0), a2[b].t[:, 0, cs], start=True, stop=False)
                ins = e.matmul(QR.t[:64, cs], tws(2), a2[b].t[:, 1, cs], start=False, stop=True)
            return ins

        def mmim(e):
            for hf in range(1):
                cs = slice(0, 512)
                e.matmul(QI.t[:64, cs], tws(1), a2[b].t[:, 0, cs], start=True, stop=False)
                ins = e.matmul(QI.t[:64, cs], tws(0), a2[b].t[:, 1, cs], start=False, stop=True)
            return ins
        kb.op("pe", mmre, reads=[tw.r, a2[b].r], writes=[QR.r])
        kb.op("pe", mmim, reads=[tw.r, a2[b].r], writes=[QI.r])
        yield
        if mode == "H":
            kb.op("act", lambda e: e.activation(out=hb[b].t[:, 0, :], in_=QR.t[:64, 0:512], func=AF.Copy), reads=[QR.r], writes=[hb[b].r])
            kb.op("dve", lambda e: e.tensor_copy(out=hb[b].t[:, 1, :], in_=QI.t[:64, 0:512]), reads=[QI.r], writes=[hb[b].r])
            yield
            kb.dma(HSTQ, HS[k1, :, :, cw], hb[b].t[:, :, :], reads=[hb[b].r], writes=kb.dres(hsname, 0, 128))
            yield
            return
        kb.op("act", lambda e: e.activation(out=xr[b].t[:, :], in_=QR.t[:64, 0:512], func=AF.Copy), reads=[QR.r], writes=[xr[b].r])
        kb.op("dve", lambda e: e.tensor_copy(out=xi[b].t[:, :], in_=QI.t[:64, 0:512]), reads=[QI.r], writes=[xi[b].r])
        yield
        t0_, t1_ = tt[b]
        kb.op("dve", lambda e: e.tensor_tensor(out=t0_.t[:, :], in0=xr[b].t[:, :], in1=hb[b].t[:, 0, :], op=ALU.mult), reads=[xr[b].r, hb[b].r], writes=[t0_.r])
        kb.op("pool", lambda e: e.tensor_tensor(out=t1_.t[:, :], in0=xi[b].t[:, :], in1=hb[b].t[:, 1, :], op=ALU.mult), reads=[xi[b].r, hb[b].r], writes=[t1_.r])
        yield
        kb.op("dve", lambda e: e.tensor_tensor(out=yb[b].t[:, 0, :], in0=t0_.t[:, :], in1=t1_.t[:, :], op=ALU.subtract), reads=[t0_.r, t1_.r], writes=[yb[b].r])
        yield
        kb.op("pool", lambda e: e.tensor_tensor(out=t1_.t[:, :], in0=xr[b].t[:, :], in1=hb[b].t[:, 1, :], op=ALU.mult), reads=[xr[b].r, hb[b].r], writes=[t1_.r])
        kb.op("dve", lambda e: e.tensor_tensor(out=t0_.t[:, :], in0=xi[b].t[:, :], in1=hb[b].t[:, 0, :], op=ALU.mult), reads=[xi[b].r, hb[b].r], writes=[t0_.r])
        yield
        kb.op("pool", lambda e: e.tensor_tensor(out=yb[b].t[:, 1, :], in0=t0_.t[:, :], in1=t1_.t[:, :], op=ALU.add), reads=[t0_.r, t1_.r], writes=[yb[b].r])
        yield

        def mzre(e):
            for hf in range(1):
                cs = slice(0, 512)
                e.matmul(QR.t[:64, cs], twts(0), yb[b].t[:, 0, cs], start=True, stop=False)
                ins = e.matmul(QR.t[:64, cs], twts(1), yb[b].t[:, 1, cs], start=False, stop=True)
            return ins

        def mzim(e):
            for hf in range(1):
                cs = slice(0, 512)
                e.matmul(QI.t[:64, cs], twts(0), yb[b].t[:, 1, cs], start=True, stop=False)
                ins = e.matmul(QI.t[:64, cs], twts(2), yb[b].t[:, 0, cs], start=False, stop=True)
            return ins
        kb.op("pe", mzre, reads=[twt.r, yb[b].r], writes=[QR.r])
        kb.op("pe", mzim, reads=[twt.r, yb[b].r], writes=[QI.r])
        yield
        kb.op("act", lambda e: e.activation(out=zb[b].t[:, 0, :], in_=QR.t[:64, 0:512], func=AF.Copy), reads=[QR.r], writes=[zb[b].r])
        kb.op("dve", lambda e: e.tensor_copy(out=zb[b].t[:, 1, :], in_=QI.t[:64, 0:512]), reads=[QI.r], writes=[zb[b].r])
        yield
        kb.dma(HSTQ, D2[k1, :, :, cw], zb[b].t[:, :, :], reads=[zb[b].r], writes=kb.dres(d2name, 0, 128))
        yield

    items = [(k1, c5) for k1 in range(N1) for c5 in range(C // 512)]
    for i0 in range(0, len(items), 4):
        hy_lockstep([chain(k1, c5, b) for b, (k1, c5) in enumerate(items[i0:i0 + 4])])


def hyena_stage(kb, cst, P, S, X, seqs, layer):
    nc = kb.nc
    for L in sorted(set(seqs)):
        if "H0" in HY_SKIP:
            break
        N = 2 * L
        N1 = N // 64
        NR = N // 128
        with contextlib.ExitStack() as st:
            w1 = kb.sb(st, "hw1", [33, 64], F32)
            w2 = kb.sb(st, "hw2", [64, 64], F32)
            w3 = kb.sb(st, "hw3", [64, 4096], F32)
            prm = kb.sb(st, "hprm", [64, 4], F32)
            bf = kb.sb(st, "hbf", [64, 2], F32)
            kb.dma("sp", w1.t[:], P["c_f_w1"][:, :], writes=[w1.r])
            kb.dma("sp", w2.t[:], P["c_f_w2"][:, :], writes=[w2.r])
            kb.dma("sp", w3.t[:], P["c_f_w3"][:, :], writes=[w3.r])
            kb.dma("sp", prm.t[:], P["c_f_prm"][:, :], writes=[prm.r])
            kb.op("dve", lambda e: e.tensor_tensor(out=bf.t[:, 0:1], in0=prm.t[:, 0:1], in1=prm.t[:, 1:2], op=ALU.mult), reads=[prm.r], writes=[bf.r])
            kb.op("dve", lambda e: e.tensor_tensor(out=bf.t[:, 1:2], in0=prm.t[:, 2:3], in1=prm.t[:, 3:4], op=ALU.mult), reads=[prm.r, bf.r], writes=[bf.r])
            zt = kb.sb(st, "hzt", [33, 512], F32)
            aa = kb.sb(st, "haa", [64, 512], F32)
            m1 = kb.sb(st, "hm1", [64, 512], F32)
            m2 = kb.sb(st, "hm2", [64, 512], F32)
            h1 = kb.sb(st, "hh1", [64, 512], F32)
            h2 = kb.sb(st, "hh2", [64, N], F32)
            ntp = kb.sb(st, "hntp", [128, NR], F32)
            dlt = kb.sb(st, "hdlt", [128, 1024], F32)
            win = kb.sb(st, "hwin", [128, 1024], F32)
            fl = [kb.sb(st, "hfl%d" % i, [128, 2048], BF16) for i in range(2)]
            kb.dma("sp", ntp.t[:], P["hy_ntp_%d" % L][:, :], writes=[ntp.r])
            kb.dma("sp", dlt.t[:], P["hy_delta"][:, :], writes=[dlt.r])
            Q = [kb.ps(st, "hQ%d" % i, [128, 1024]) for i in range(4)]

            def sin_layer(ps, col_f, col_b, dst):
                kb.op("act", lambda e: e.activation(out=aa.t[:, :], in_=ps.t[:64, 0:512], func=AF.Identity, bias=bf.t[:, col_b:col_b + 1], scale=prm.t[:, col_f:col_f + 1]),
                      reads=[ps.r, bf.r, prm.r], writes=[aa.r])
                kb.op("dve", lambda e: e.tensor_scalar(out=m1.t[:, :], in0=aa.t[:, :], scalar1=math.pi, scalar2=-TWO_PI, op0=ALU.is_gt, op1=ALU.mult), reads=[aa.r], writes=[m1.r])
                kb.op("dve", lambda e: e.tensor_scalar(out=m2.t[:, :], in0=aa.t[:, :], scalar1=-math.pi, scalar2=TWO_PI, op0=ALU.is_lt, op1=ALU.mult), reads=[aa.r], writes=[m2.r])
                kb.op("dve", lambda e: e.tensor_tensor(out=m1.t[:, :], in0=m1.t[:, :], in1=m2.t[:, :], op=ALU.add), reads=[m1.r, m2.r], writes=[m1.r])
                kb.op("dve", lambda e: e.tensor_tensor(out=aa.t[:, :], in0=aa.t[:, :], in1=m1.t[:, :], op=ALU.add), reads=[aa.r, m1.r], writes=[aa.r])
                kb.op("act", lambda e: e.activation(out=dst, in_=aa.t[:, :], func=AF.Sin), reads=[aa.r], writes=[h1.r, h2.r])

            for c0 in range(0, N, 512):
                kb.dma("sp", zt.t[:, :], P["hy_z_%d" % L][:, c0:c0 + 512], writes=[zt.r])
                kb.op("pe", lambda e: e.matmul(Q[0].t[:64, 0:512], w1.t[:, :], zt.t[:, :], start=True, stop=True), reads=[w1.r, zt.r], writes=[Q[0].r])
                sin_layer(Q[0], 1, 0, h1.t[:, :])
                kb.op("pe", lambda e: e.matmul(Q[1].t[:64, 0:512], w2.t[:, :], h1.t[:, :], start=True, stop=True), reads=[w2.r, h1.r], writes=[Q[1].r])
                sin_layer(Q[1], 3, 1, h2.t[:, c0:c0 + 512])
            FILT = S["FILT"][L]
            for r in range(NR):
                b = r % 2
                dsel = 0 if r * 128 < L else 1
                kb.op("act", lambda e, r=r: e.activation(out=win.t[:, :], in_=dlt.t[:, :], func=AF.Exp, scale=ntp.t[:, r:r + 1]), reads=[dlt.r, ntp.r], writes=[win.r])
                for q in range(4):
                    pq = Q[q % 2 + 2]
                    kb.op("pe", lambda e, q=q, pq=pq, r=r: e.matmul(pq.t[:, 0:512], h2.t[:, r * 128:(r + 1) * 128], w3.t[:, dsel * 2048 + q * 512:dsel * 2048 + (q + 1) * 512],
                                                                     start=True, stop=True), reads=[h2.r, w3.r], writes=[pq.r])
                    kb.op("dve", lambda e, q=q, pq=pq, b=b: e.tensor_tensor(out=fl[b].t[:, q * 512:(q + 1) * 512], in0=pq.t[:, 0:512], in1=win.t[:, (q % 2) * 512:(q % 2 + 1) * 512], op=ALU.mult),
                          reads=[pq.r, win.r], writes=[fl[b].r])
                kb.dma(HSTQ, FILT[r * 128:(r + 1) * 128, :], fl[b].t[:, :], reads=[fl[b].r], writes=kb.dres(("FILT", L), 0, 128))
            kb.barrier()
        with contextlib.ExitStack() as st:
            tb = hy_tables(kb, st, P, L)
            hy_work_tiles(kb, st, tb)
            Q = [kb.ps(st, "hQ%d" % i, [128, 512]) for i in range(4)]
            tb["QF"] = Q + [kb.ps(st, "hQF%d" % i, [128, 512]) for i in range(4)]
            tb["src_res"] = kb.dres(("FILT", L), 0, 128)
            hy_F1(kb, tb, Q, tb["ab"], S["FILT"][L].rearrange("(a b) c -> a b c", b=64), N1, N1, 2048, S["D1F"][L], ("D1F", L))
            hy_F2(kb, tb, Q, N1, 2048, S["D1F"][L], ("D1F", L), "H", HS=S["HS"][L], hsname=("HS", L))
            kb.barrier()
    with contextlib.ExitStack() as st:
        wi = kb.sb(st, "hwi", [128, 8, 3072], BF16, nres=8)
        wl = WLoader(kb, st, 3072, nbuf=2)
        for c in range(8):
            wl.load(wi.t[:, c, :], P["c_w_in"][c * 128:(c + 1) * 128, :], wi.rs[c], 3072)
        T = 256
        nrm = NormCtx(kb, st, cst, T)
        xt = kb.sb(st, "hxt", [128, 8, T], F32)
        ht = kb.sb(st, "hht", [128, 8, T], BF16)
        binb = kb.sb(st, "hbinb", [128, 3072], F32)
        kb.dma("sp", binb.t[:], P["c_b_in"][:, :], writes=[binb.r])
        pt = [kb.sb(st, "hpt%d" % i, [128, 3072], F32) for i in range(2)]
        zr = kb.sb(st, "hzr", [1, 3072], F32)
        kb.op("pool", lambda e: e.memset(zr.t[:], 0.0), writes=[zr.r])
        psp = [kb.ps(st, "hpsp%d" % i, [128, 512]) for i in range(2)]
        for s, L in enumerate(seqs):
            Xv = X[s].rearrange("(c p) l -> p c l", p=128)
            PP = S["PP"][s]
            kb.dma(HSTQ, PP[0:1, :], zr.t[:], reads=[zr.r], writes=kb.dres(("PP", s), 0, 1))
            kb.dma(HSTQ, PP[L + 1:L + 2, :], zr.t[:], reads=[zr.r], writes=kb.dres(("PP", s), L + 1, L + 2))
            for t0 in range(0, L, T):
                n = min(T, L - t0)
                kb.dma("sp", xt.t[:, :, :n], Xv[:, :, t0:t0 + n], reads=kb.dres(("X", s), t0, t0 + n), writes=[xt.r])
                nrm.prenorm(xt, ht, n, layer, 2)
                for tb_i in range(n // 128):
                    bb = tb_i % 2
                    for cb in range(6):
                        pb = cb % 2

                        def mmp(e, cb=cb, pb=pb, tb_i=tb_i):
                            for c in range(8):
                                ins = e.matmul(psp[pb].t[:, :], ht.t[:, c, tb_i * 128:(tb_i + 1) * 128], wi.t[:, c, cb * 512:(cb + 1) * 512],
                                               start=(c == 0), stop=(c == 7))
                            return ins
                        kb.op("pe", mmp, reads=wi.rs + [ht.r], writes=[psp[pb].r])
                        kb.op("dve", lambda e, cb=cb, pb=pb, bb=bb: e.tensor_tensor(out=pt[bb].t[:, cb * 512:(cb + 1) * 512], in0=psp[pb].t[:, :],
                                                                                     in1=binb.t[:, cb * 512:(cb + 1) * 512], op=ALU.add),
                              reads=[psp[pb].r, binb.r], writes=[pt[bb].r])
                    r0 = t0 + tb_i * 128
                    kb.dma(HSTQ, PP[1 + r0:1 + r0 + 128, :], pt[bb].t[:, :], reads=[pt[bb].r], writes=kb.dres(("PP", s), 1 + r0, 1 + r0 + 128))
        kb.barrier()
    with contextlib.ExitStack() as st:
        cwb = kb.sb(st, "hcwb", [128, 4, 3072], F32)
        for i in range(4):
            kb.dma("sp", cwb.t[:, i, :], P["c_conv"][:, i, :], writes=[cwb.r])
        pa = [[kb.sb(st, "hpa%d_%d" % (j, i), [128, 3072], F32) for i in range(3)] for j in range(2)]
        acc = [kb.sb(st, "hacc%d" % j, [128, 3072], F32) for j in range(2)]
        ob = [kb.sb(st, "hob%d" % j, [128, 3072], BF16) for j in range(2)]

        def cchain(s, r0, j):
            PP = S["PP"][s]
            p_, a_, o_ = pa[j], acc[j], ob[j]
            for i in range(3):
                kb.dma("sp", p_[i].t[:, :], PP[r0 + i:r0 + i + 128, :], reads=kb.dres(("PP", s), r0 + i, r0 + i + 128), writes=[p_[i].r])
            yield
            kb.op("dve", lambda e: e.tensor_tensor(out=a_.t[:, :], in0=p_[0].t[:, :], in1=cwb.t[:, 0, :], op=ALU.mult), reads=[p_[0].r, cwb.r], writes=[a_.r])
            kb.op("pool", lambda e: e.tensor_tensor(out=p_[1].t[:, :], in0=p_[1].t[:, :], in1=cwb.t[:, 1, :], op=ALU.mult), reads=[p_[1].r, cwb.r], writes=[p_[1].r])
            yield
            kb.op("dve", lambda e: e.tensor_tensor(out=a_.t[:, :], in0=a_.t[:, :], in1=cwb.t[:, 3, :], op=ALU.add), reads=[a_.r, cwb.r], writes=[a_.r])
            kb.op("pool", lambda e: e.tensor_tensor(out=p_[2].t[:, :], in0=p_[2].t[:, :], in1=cwb.t[:, 2, :], op=ALU.mult), reads=[p_[2].r, cwb.r], writes=[p_[2].r])
            yield
            kb.op("dve", lambda e: e.tensor_tensor(out=a_.t[:, :], in0=a_.t[:, :], in1=p_[1].t[:, :], op=ALU.add), reads=[a_.r, p_[1].r], writes=[a_.r])
            yield
            kb.op("pool", lambda e: e.tensor_tensor(out=o_.t[:, :], in0=a_.t[:, :], in1=p_[2].t[:, :], op=ALU.add), reads=[a_.r, p_[2].r], writes=[o_.r])
            yield
            kb.dma(HSTQ, S["HV"][s][0][r0:r0 + 128, :], o_.t[:, 0:1024], reads=[o_.r], writes=kb.dres(("HV0", s), r0, r0 + 128))
            kb.dma(HSTQ, S["HG"][s][r0:r0 + 128, :], o_.t[:, 1024:3072], reads=[o_.r], writes=kb.dres(("HG", s), r0, r0 + 128))
            yield
        for s, L in enumerate(seqs):
            blocks = list(range(0, L, 128))
            for i0 in range(0, len(blocks), 2):
                hy_lockstep([cchain(s, r0, j) for j, r0 in enumerate(blocks[i0:i0 + 2])])
        kb.barrier()
    for L in sorted(set(seqs)):
        if "H2" in HY_SKIP:
            break
        N1 = 2 * L // 64
        K = N1 // 2
        with contextlib.ExitStack() as st:
            tb = hy_tables(kb, st, P, L)
            hy_work_tiles(kb, st, tb)
            Q = [kb.ps(st, "hQ%d" % i, [128, 512]) for i in range(4)]
            tb["QF"] = Q + [kb.ps(st, "hQF%d" % i, [128, 512]) for i in range(4)]
            skb = kb.sb(st, "hskb", [128, 2, 1024], F32)
            kb.dma("sp", skb.t[:], P["c_bias_d"][:, :, :], writes=[skb.r])
            zt = [kb.sb(st, "hzt%d" % i, [128, 2, 512], BF16) for i in range(4)]
            ut = [kb.sb(st, "hut%d" % i, [64, 512], BF16) for i in range(4)]
            gt = [kb.sb(st, "hgt%d" % i, [64, 512], BF16) for i in range(4)]
            t1 = [kb.sb(st, "ht1%d" % i, [64, 512], BF16) for i in range(4)]
            t2 = [kb.sb(st, "ht2%d" % i, [64, 512], F32) for i in range(4)]
            zo = [kb.sb(st, "hzo%d" % i, [64, 512], BF16) for i in range(4)]
            for s, Ls in enumerate(seqs):
                if Ls != L:
                    continue
                for order in range(2):
                    Uin = S["HV"][s][order]
                    Uout = S["HV"][s][order + 1]
                    uin_name, uout_name = ("HV%d" % order, s), ("HV%d" % (order + 1), s)
                    Uiv = Uin.rearrange("(a b) c -> a b c", b=64)
                    Uov = Uout.rearrange("(a b) c -> a b c", b=64)
                    Gv = S["HG"][s].rearrange("(a b) c -> a b c", b=64)
                    tb["src_res"] = kb.dres(uin_name, 0, L)
                    hy_F1(kb, tb, Q, tb["ab"], Uiv, K, N1, 1024, S["D1"], "D1")
                    hy_F2(kb, tb, Q, N1, 1024, S["D1"], "D1", "conv", HS=S["HS"][L], hsname=("HS", L), order=order, D2=S["D2"], d2name="D2")

                    def g2chain(n2, hf, b):
                        cs = slice(hf * 512, (hf + 1) * 512)
                        kb.dma("sp", zt[b].t[:N1, :, :], S["D2"][0:N1, n2, :, cs], reads=kb.dres("D2", 0, 128), writes=[zt[b].r])
                        kb.dma("sp", ut[b].t[:K, :], Uiv[0:K, n2, cs], reads=kb.dres(uin_name, 0, L), writes=[ut[b].r])
                        kb.dma("sp", gt[b].t[:K, :], Gv[0:K, n2, order * 1024 + hf * 512:order * 1024 + (hf + 1) * 512], reads=kb.dres(("HG", s), 0, L), writes=[gt[b].r])
                        yield

                        def mmy(e):
                            e.matmul(Q[b].t[:K, 0:512], tb["g2t"].t[:N1, 0:K], zt[b].t[:N1, 0, :], start=True, stop=False)
                            return e.matmul(Q[b].t[:K, 0:512], tb["g2t"].t[:N1, K:2 * K], zt[b].t[:N1, 1, :], start=False, stop=True)
                        kb.op("pe", mmy, reads=[tb["g2t"].r, zt[b].r], writes=[Q[b].r])
                        kb.op("pool", lambda e: e.tensor_tensor(out=t1[b].t[:K, :], in0=ut[b].t[:K, :], in1=skb.t[:K, order, cs], op=ALU.mult), reads=[ut[b].r, skb.r], writes=[t1[b].r])
                        yield
                        kb.op("dve", lambda e: e.tensor_tensor(out=t2[b].t[:K, :], in0=Q[b].t[:K, 0:512], in1=t1[b].t[:K, :], op=ALU.add), reads=[Q[b].r, t1[b].r], writes=[t2[b].r])
                        yield
                        kb.op("pool", lambda e: e.tensor_tensor(out=zo[b].t[:K, :], in0=t2[b].t[:K, :], in1=gt[b].t[:K, :], op=ALU.mult), reads=[t2[b].r, gt[b].r], writes=[zo[b].r])
                        yield
                        kb.dma(HSTQ, Uov[0:K, n2, cs], zo[b].t[:K, :], reads=[zo[b].r], writes=kb.dres(uout_name, 0, L))
                        yield
                    items = [(n2, hf) for n2 in range(64) for hf in range(2)]
                    for i0 in range(0, len(items), 4):
                        hy_lockstep([g2chain(n2, hf, b) for b, (n2, hf) in enumerate(items[i0:i0 + 4])])
            kb.barrier()
    with contextlib.ExitStack() as st:
        zl = [kb.sb(st, "hzl%d" % i, [128, 8, 128], BF16) for i in range(2)]
        uT = [kb.sb(st, "huT%d" % i, [128, 8, 128], BF16) for i in range(2)]
        PK = kb.ps(st, "hPK", [128, 1024], BF16)
        for s, L in enumerate(seqs):
            Uv = S["U"][s].rearrange("(k p) l -> p k l", p=128)
            for i, r0 in enumerate(range(0, L, 128)):
                b = i % 2
                kb.dma("sp", zl[b].t[:].rearrange("p h m -> p (h m)"), S["HV"][s][2][r0:r0 + 128, :], reads=kb.dres(("HV2", s), 0, L), writes=[zl[b].r])

                def tr(e, b=b):
                    for h in range(8):
                        ins = e.transpose(out=PK.t[:, h * 128:(h + 1) * 128], in_=zl[b].t[:, h, :], identity=cst["ident_bf"].t[:])
                    return ins
                kb.op("pe", tr, reads=[zl[b].r, cst["ident_bf"].r], writes=[PK.r])
                kb.op("act", lambda e, b=b: e.activation(out=uT[b].t[:].rearrange("p h m -> p (h m)"), in_=PK.t[:], func=AF.Copy), reads=[PK.r], writes=[uT[b].r])
                kb.dma(HSTQ, Uv[:, 0:8, r0:r0 + 128], uT[b].t[:], reads=[uT[b].r], writes=kb.dres(("U", s), r0, r0 + 128))
        kb.barrier()
    with contextlib.ExitStack() as st:
        bo = kb.sb(st, "hbo", [128, 8], F32)
        kb.dma("sp", bo.t[:], P["c_b_out"][:, :], writes=[bo.r])
        outproj_phase(kb, cst, P["c_w_out"], bo, S["U"], "U", 1024, X, seqs, layer)


def hy_work_tiles(kb, st, tb):
    tb["xt"] = [kb.sb(st, "hwxt%d" % i, [128, 512], BF16) for i in range(4)]
    tb["ab"] = [kb.sb(st, "hwab%d" % i, [128, 2, 512], BF16) for i in range(4)]
    tb["a2"] = [kb.sb(st, "hwa2%d" % i, [64, 2, 512], BF16) for i in range(4)]
    tb["hb"] = [kb.sb(st, "hwhb%d" % i, [64, 2, 512], BF16) for i in range(4)]
    tb["xr"] = [kb.sb(st, "hwxr%d" % i, [64, 512], BF16) for i in range(4)]
    tb["xi"] = [kb.sb(st, "hwxi%d" % i, [64, 512], BF16) for i in range(4)]
    tb["tt"] = [[kb.sb(st, "hwtt%d_%d" % (i, j), [64, 512], BF16) for j in range(2)] for i in range(4)]
    tb["yb"] = [kb.sb(st, "hwyb%d" % i, [64, 2, 512], BF16) for i in range(4)]
    tb["zb"] = [kb.sb(st, "hwzb%d" % i, [64, 2, 512], BF16) for i in range(4)]


def copy_stage(kb, Xin, X, seqs):
    with contextlib.ExitStack() as st:
        xt = [kb.sb(st, "cxt%d" % i, [128, 8, 512], F32) for i in range(2)]
        k = 0
        for s, L in enumerate(seqs):
            Xv = X[s].rearrange("(c p) l -> p c l", p=128)
            Xiv = Xin[s].rearrange("(c p) l -> p c l", p=128)
            for t0 in range(0, L, 512):
                n = min(512, L - t0)
                b = k % 2
                k += 1
                kb.dma("sp", xt[b].t[:, :, :n], Xiv[:, :, t0:t0 + n], writes=[xt[b].r])
                kb.dma(STQ, Xv[:, :, t0:t0 + n], xt[b].t[:, :, :n], reads=[xt[b].r], writes=kb.dres(("X", s), t0, t0 + n))
        kb.barrier()


def build(seqs, stages):
    nc = bass.Bass("TRN2", target_bir_lowering=False)
    P = {}

    def din(name, shape):
        P[name] = nc.dram_tensor(name, list(shape), F32, kind="ExternalInput").ap()

    X = []
    for s, L in enumerate(seqs):
        X.append(nc.dram_tensor("xin%d" % s, [D, L], F32, kind="ExternalInput").ap())
    Y = []
    for s, L in enumerate(seqs):
        Y.append(nc.dram_tensor("yout%d" % s, [D, L], F32, kind="ExternalOutput").ap())
    din("norm_g", [128, 192])
    din("ffn_w_gate", [8, D, DFF])
    din("ffn_w_up", [8, D, DFF])
    din("ffn_w_down", [8, DFF, D])
    din("ident", [128, 128])
    din("rope", [128, 4, 4096])
    din("dmat", [128, 128])
    din("tri", [128, 2, 128])
    din("dvals", [128, 32])
    din("b_w_in", [D, 6144])
    din("b_decay_logit", [128, 8])
    din("b_gn_w", [128, 2048])
    din("b_w_out", [2048, D])
    din("a_w_in", [2, D, 4128])
    din("a_conv_w", [2, 128, 72])
    din("a_a_log", [2, 128, 16])
    din("a_dt_bias", [2, 128, 16])
    din("a_norm_w", [2, 128, 1024])
    din("a_w_out", [2, D, D])
    din("gmask", [128, 4, 128])
    din("negm", [128, 4, 1024])
    din("bmask", [128, 4, 1024])
    din("c_w_in", [D, 3072])
    din("c_b_in", [128, 3072])
    din("c_conv", [128, 4, 3072])
    din("c_f_w1", [33, 64])
    din("c_f_w2", [64, 64])
    din("c_f_w3", [64, 4096])
    din("c_f_prm", [64, 4])
    din("c_bias_d", [128, 2, 1024])
    din("c_w_out", [D, D])
    din("c_b_out", [128, 8])
    din("hy_delta", [128, 1024])
    S = {k: [] for k in ("QK", "V", "G", "U", "QKV", "Z", "GB", "OF", "OB", "PP", "HV", "HG")}
    S["FILT"], S["D1F"], S["HS"] = {}, {}, {}
    for L in sorted(set(seqs)):
        N1 = 2 * L // 64
        din("hy_f1t_%d" % L, [N1, 2 * N1])
        din("hy_tw_%d" % L, [64, N1 * 192])
        din("hy_twt_%d" % L, [64, N1 * 192])
        din("hy_g2t_%d" % L, [N1, N1])
        din("hy_z_%d" % L, [33, 2 * L])
        din("hy_ntp_%d" % L, [128, 2 * L // 128])
        S["FILT"][L] = _scr(nc, "s_filt%d" % L, [2 * L, 2048], BF16).ap()
        S["D1F"][L] = _scr(nc, "s_d1f%d" % L, [N1, 64, 2, 2048], BF16).ap()
        S["HS"][L] = _scr(nc, "s_hs%d" % L, [N1, 64, 2, 2048], BF16).ap()
    S["D1"] = _scr(nc, "s_d1", [128, 64, 2, 1024], BF16).ap()
    S["D2"] = _scr(nc, "s_d2", [128, 64, 2, 1024], BF16).ap()
    for s, L in enumerate(seqs):
        S["QK"].append(_scr(nc, "s_qk%d" % s, [2048, L], BF16).ap())
        S["V"].append(_scr(nc, "s_v%d" % s, [L, 2048], BF16).ap())
        S["G"].append(_scr(nc, "s_g%d" % s, [L, 2048], BF16).ap())
        S["U"].append(_scr(nc, "s_u%d" % s, [2048, L], BF16).ap())
        S["PP"].append(_scr(nc, "s_pp%d" % s, [L + 2, 3072], F32).ap())
        S["HV"].append([_scr(nc, "s_hv%d_%d" % (s, i), [L, 1024], BF16).ap() for i in range(3)])
        S["HG"].append(_scr(nc, "s_hg%d" % s, [L, 2048], BF16).ap())
        S["QKV"].append(_scr(nc, "s_qkv%d" % s, [3072, L], BF16).ap())
        S["Z"].append(_scr(nc, "s_z%d" % s, [L, 1024], BF16).ap())
        S["GB"].append(_scr(nc, "s_gb%d" % s, [L, 32], F32).ap())
        S["OF"].append(_scr(nc, "s_of%d" % s, [L, 1024], F32).ap())
        S["OB"].append(_scr(nc, "s_ob%d" % s, [L, 1024], F32).ap())
    kb = KB(nc)
    with kb.st:
        cst = load_consts(kb, kb.st, P)
        first = True
        for stg in stages:
            Xin = X if first else Y
            first = False
            if stg[0] == "ffn":
                ffn_stage(kb, cst, P, Xin, Y, seqs, stg[1], stg[2])
            elif stg[0] == "copy":
                copy_stage(kb, Xin, Y, seqs)
            elif stg[0] == "mix":
                kind = stg[1] % 3
                if kind == 1:
                    ret_stage(kb, cst, P, S, Y, seqs, stg[1])
                elif kind == 0:
                    gdn_stage(kb, cst, P, S, Y, seqs, stg[1])
                else:
                    hyena_stage(kb, cst, P, S, Y, seqs, stg[1])
        kb.barrier()
    return nc, kb


def hyena_consts(L):
    N = 2 * L
    N1 = N // 64
    K = N1 // 2
    o = {}
    n1 = np.arange(N1, dtype=np.float64)
    ph = 2 * np.pi * np.outer(n1, n1) / N1
    o["hy_f1t_%d" % L] = np.concatenate([np.cos(ph), -np.sin(ph)], 1).astype(np.float32)
    n2 = np.arange(64, dtype=np.float64)
    k1 = np.arange(N1, dtype=np.float64)
    k2 = np.arange(64, dtype=np.float64)
    th = 2 * np.pi * n2[:, None, None] * (k1[None, :, None] + N1 * k2[None, None, :]) / N
    C, Sm = np.cos(th), -np.sin(th)
    tw = np.stack([C, Sm, -Sm], 2)
    o["hy_tw_%d" % L] = np.ascontiguousarray(tw.reshape(64, N1 * 192).astype(np.float32))
    twt = np.stack([C, Sm, -Sm], 0).transpose(3, 2, 0, 1)
    o["hy_twt_%d" % L] = np.ascontiguousarray(twt.reshape(64, N1 * 192).astype(np.float32))
    ph2 = 2 * np.pi * np.outer(k1, np.arange(K, dtype=np.float64)) / N1
    o["hy_g2t_%d" % L] = np.concatenate([np.cos(ph2) / N, -np.sin(ph2) / N], 1).astype(np.float32)
    n = np.arange(N)
    pos = np.where(n < L, n, N - n).astype(np.float64)
    pos[L] = 0
    t = pos / (L - 1)
    om = 2 * np.pi * pos / L
    f = np.linspace(1e-4, 15.0, 16)
    z = np.concatenate([t[None, :], np.cos(f[:, None] * om[None, :]), -np.sin(f[:, None] * om[None, :])], 0)
    o["hy_z_%d" % L] = np.ascontiguousarray(z.astype(np.float32))
    nt = -t
    nt[L] = -1e4
    o["hy_ntp_%d" % L] = np.ascontiguousarray(nt.reshape(N // 128, 128).T.astype(np.float32))
    return o


def prep_params(inp, seqs=(2048, 4096)):
    out = {}
    for L in sorted(set(seqs)):
        out.update(hyena_consts(L))
    min_decay = math.log(1e-2) / 1.5
    max_decay = math.log(1e-2) / 0.3
    deltas = np.abs(np.linspace(min_decay, max_decay, D))
    out["hy_delta"] = np.ascontiguousarray(np.broadcast_to(deltas.astype(np.float32)[None, :], (128, D)))
    f32 = lambda k: np.asarray(inp[k], np.float32)
    out["c_w_in"] = np.ascontiguousarray(f32("c_w_in")[0])
    out["c_b_in"] = np.ascontiguousarray(np.broadcast_to(f32("c_b_in")[0][None, :], (128, 3072)))
    cc = np.concatenate([f32("c_conv_w")[0], f32("c_conv_b")[0][None, :]], 0)
    out["c_conv"] = np.ascontiguousarray(np.broadcast_to(cc[None], (128, 4, 3072)))
    out["c_f_w1"] = np.ascontiguousarray(f32("c_f_w1")[0])
    out["c_f_w2"] = np.ascontiguousarray(f32("c_f_w2")[0])
    out["c_f_w3"] = np.ascontiguousarray(f32("c_f_w3")[0])
    out["c_f_prm"] = np.ascontiguousarray(np.stack([f32("c_f_b1")[0], f32("c_f_freq1")[0], f32("c_f_b2")[0], f32("c_f_freq2")[0]], 1))
    out["c_bias_d"] = np.ascontiguousarray(np.broadcast_to(f32("c_bias_d")[0][None], (128, 2, 1024)))
    out["c_w_out"] = np.ascontiguousarray(f32("c_w_out")[0])
    out["c_b_out"] = np.ascontiguousarray(f32("c_b_out")[0].reshape(8, 128).T)
    ng = np.asarray(inp["norm_g"], np.float32)
    out["norm_g"] = np.ascontiguousarray(ng.reshape(4, 6, 8, 128).transpose(3, 0, 1, 2).reshape(128, 192))
    out["ffn_w_gate"] = np.ascontiguousarray(np.asarray(inp["ffn_w_gate"], np.float32).reshape(8, D, DFF))
    out["ffn_w_up"] = np.ascontiguousarray(np.asarray(inp["ffn_w_up"], np.float32).reshape(8, D, DFF))
    out["ffn_w_down"] = np.ascontiguousarray(np.asarray(inp["ffn_w_down"], np.float32).reshape(8, DFF, D))
    out["ident"] = np.eye(128, dtype=np.float32)
    inv = (10000.0 ** (-np.arange(0, 256, 2, dtype=np.float64) / 256.0))
    ang = inv[:, None] * np.arange(4096, dtype=np.float64)[None, :]
    rope = np.stack([np.cos(ang), np.sin(ang), np.cos(ang) / 16.0, np.sin(ang) / 16.0], 1)
    out["rope"] = np.ascontiguousarray(rope.astype(np.float32))
    m = np.arange(128)
    out["dmat"] = np.ascontiguousarray((m[None, :] - m[:, None]).astype(np.float32))
    out["tri"] = np.ascontiguousarray(np.stack([(m[None, :] >= m[:, None]), (m[:, None] >= m[None, :])], 1).astype(np.float32))
    out["dvals"] = np.ascontiguousarray(np.broadcast_to(128.0 * np.arange(32, dtype=np.float32)[None, :], (128, 32)))
    le = (m[:, None] <= m[None, :]).astype(np.float32)
    gt = (m[:, None] > m[None, :]).astype(np.float32)
    ge = (m[:, None] >= m[None, :]).astype(np.float32)
    lt = (m[:, None] < m[None, :]).astype(np.float32)
    out["gmask"] = np.ascontiguousarray(np.stack([le, gt, ge, lt], 1))
    out["negm"] = np.ascontiguousarray(np.stack([np.tile(-1e4 * x, (1, 8)) for x in (le, gt, ge, lt)], 1))
    blk = lambda b: (m[:, None] // b) == (m[None, :] // b)
    bms = [blk(16), blk(32) & ~blk(16), blk(64) & ~blk(32), ~blk(64)]
    out["bmask"] = np.ascontiguousarray(np.stack([np.tile(x.astype(np.float32), (1, 8)) for x in bms], 1))
    out["a_w_in"] = np.ascontiguousarray(np.asarray(inp["a_w_in"], np.float32))
    cwa = np.asarray(inp["a_conv_w"], np.float32)
    out["a_conv_w"] = np.ascontiguousarray(cwa.reshape(2, 3, 24, 128).transpose(0, 3, 1, 2).reshape(2, 128, 72))
    out["a_a_log"] = np.ascontiguousarray(np.broadcast_to(np.asarray(inp["a_a_log"], np.float32).reshape(2, 1, 16), (2, 128, 16)))
    out["a_dt_bias"] = np.ascontiguousarray(np.broadcast_to(np.asarray(inp["a_dt_bias"], np.float32).reshape(2, 1, 16), (2, 128, 16)))
    out["a_norm_w"] = np.ascontiguousarray(np.broadcast_to(np.tile(np.asarray(inp["a_norm_w"], np.float32), (1, 8)).reshape(2, 1, 1024), (2, 128, 1024)))
    out["a_w_out"] = np.ascontiguousarray(np.asarray(inp["a_w_out"], np.float32))
    out["b_w_in"] = np.ascontiguousarray(np.asarray(inp["b_w_in"], np.float32)[0])
    out["b_decay_logit"] = np.ascontiguousarray(np.broadcast_to(np.asarray(inp["b_decay_logit"], np.float32)[0].reshape(1, 8), (128, 8)))
    out["b_gn_w"] = np.ascontiguousarray(np.broadcast_to(np.asarray(inp["b_gn_w"], np.float32)[0].reshape(1, 2048), (128, 2048)))
    out["b_w_out"] = np.ascontiguousarray(np.asarray(inp["b_w_out"], np.float32)[0])
    return out


DEBUG_OUT = set()


def _scr(nc, name, shape, dt):
    kind = "ExternalOutput" if name in DEBUG_OUT else "Internal"
    return nc.dram_tensor(name, list(shape), dt, kind=kind)


ALL_STAGES = []
for _l in range(4):
    ALL_STAGES += [("ffn", _l, 0), ("mix", _l), ("ffn", _l, 1)]


def kernel(**inp):
    xp = np.asarray(inp["x_prompt"], np.float32)
    xs = np.asarray(inp["x_sample"], np.float32)
    seqs = [xp.shape[1], xs.shape[1], xs.shape[1]]
    nc, kb = build(seqs, ALL_STAGES)
    pp = prep_params(inp, seqs)
    in_maps = []
    for c in range(N_CORES):
        m = dict(pp)
        m["xin0"] = np.ascontiguousarray(xp[c].T)
        m["xin1"] = np.ascontiguousarray(xs[2 * c].T)
        m["xin2"] = np.ascontiguousarray(xs[2 * c + 1].T)
        in_maps.append(m)
    res = run_bass_kernel_spmd(nc, in_maps, core_ids=list(range(N_CORES)))
    yp = np.stack([res.results[c]["yout0"].T for c in range(N_CORES)], 0)
    ys = np.stack([res.results[c][k].T for c in range(N_CORES) for k in ("yout1", "yout2")], 0)
    return (np.ascontiguousarray(yp, dtype=np.float32), np.ascontiguousarray(ys, dtype=np.float32))
```

```python
import contextlib
import math
import numpy as np
import concourse.bass as bass
import concourse.mybir as mybir
from concourse.bass_utils import run_bass_kernel_spmd

F32 = mybir.dt.float32
BF16 = mybir.dt.bfloat16
AF = mybir.ActivationFunctionType
ALU = mybir.AluOpType

D = 1024
DFF = 2816
NFF = DFF // 128
EPS = 1e-6
N_CORES = 8
DBG = 99
STQ = "sp"
HSTQ = "act"
HY_SKIP = set()
OWN_RAW = True
GDN_SKIP_B = False
ADD_ENG = "dve"


class Res:
    __slots__ = ("w", "r", "psum")

    def __init__(self):
        self.w = None
        self.r = {}
        self.psum = False


class Tile:
    def __init__(self, t, nres=1):
        self.t = t
        self.rs = [Res() for _ in range(nres)]

    @property
    def r(self):
        return self.rs[0]


class KB:
    def __init__(self, nc):
        self.nc = nc
        self.st = contextlib.ExitStack()
        self.engs = {"pe": nc.tensor, "act": nc.scalar, "dve": nc.vector, "pool": nc.gpsimd, "sp": nc.sync}
        self.semh = {}
        self.cnt = {}
        self.seen = {e: {} for e in self.engs}
        for e in self.engs:
            self.semh[e] = self.st.enter_context(nc.semaphore("s_" + e))
            self.cnt[e] = 0
        self.dslots = {"sp": 24, "pool": 2, "act": 16}
        self.dnext = {q: 0 for q in self.dslots}
        self.dtot = {}
        for q, n in self.dslots.items():
            for i in range(n):
                sk = ("d", q, i)
                self.semh[sk] = self.st.enter_context(nc.semaphore("d_%s_%d" % (q, i)))
                self.dtot[sk] = 0
        self.dram_res = {}
        self.ninst = 0
        self.uid = 0

    def dres(self, name, a, b, blk=128):
        out = []
        for i in range(a // blk, (b + blk - 1) // blk):
            key = (name, i)
            if key not in self.dram_res:
                self.dram_res[key] = Res()
            out.append(self.dram_res[key])
        return out

    def _need(self, e, reads, writes, own_raw=True):
        need = {}

        def add(sk, v):
            if need.get(sk, 0) < v:
                need[sk] = v

        for r in reads:
            if r.w is not None:
                add(*r.w)
        for r in writes:
            if r.w is not None:
                add(*r.w)
            for sk, v in r.r.items():
                if sk == e:
                    continue
                add(sk, v)
        if not own_raw:
            need.pop(e, None)
        return need

    def _wait(self, e, need):
        eng = self.engs[e]
        seen = self.seen[e]
        for sk, v in need.items():
            if seen.get(sk, 0) < v:
                eng.wait_ge(self.semh[sk], v)
                seen[sk] = v
                self.ninst += 1

    def op(self, e, fn, reads=(), writes=()):
        need = self._need(e, reads, writes, own_raw=(e != "pe") and (OWN_RAW or e == "pool"))
        self._wait(e, need)
        ins = fn(self.engs[e])
        self.cnt[e] += 1
        c = self.cnt[e]
        ins.then_inc(self.semh[e], 1)
        self.ninst += 1
        for r in reads:
            if r.psum:
                assert all(k == e for k in r.r), ("two engines read one PSUM tile in the same epoch", e, list(r.r))
            r.r[e] = c
        for r in writes:
            r.w = (e, c)
            r.r = {}
        return c

    def dma(self, q, out, in_, reads=(), writes=()):
        i = self.dnext[q]
        self.dnext[q] = (i + 1) % self.dslots[q]
        sk = ("d", q, i)
        tot = self.dtot[sk]
        need = self._need(q, reads, writes)
        if need.get(sk, 0) < tot:
            need[sk] = tot
        self._wait(q, need)
        self.engs[q].dma_start(out=out, in_=in_).then_inc(self.semh[sk], 16)
        self.ninst += 1
        tot += 16
        self.dtot[sk] = tot
        for r in reads:
            r.r[sk] = tot
        for r in writes:
            r.w = (sk, tot)
            r.r = {}

    def barrier(self):
        cur = {}
        for e in self.engs:
            cur[e] = self.cnt[e]
        for sk, v in self.dtot.items():
            cur[sk] = v
        for e in self.engs:
            need = {sk: v for sk, v in cur.items() if sk != e and v > 0}
            self._wait(e, need)

    def sb(self, st, name, shape, dt, nres=1):
        self.uid += 1
        return Tile(st.enter_context(self.nc.sbuf_tensor("%s_%d" % (name, self.uid), list(shape), dt)), nres)

    def ps(self, st, name, shape, dt=F32, nres=1):
        self.uid += 1
        t = Tile(st.enter_context(self.nc.psum_tensor("%s_%d" % (name, self.uid), list(shape), dt)), nres)
        for r in t.rs:
            r.psum = True
        return t


def load_consts(kb, st, P):
    c = {}
    c["ones_bf"] = kb.sb(st, "ones_bf", [128, 128], BF16)
    kb.op("dve", lambda e: e.memset(c["ones_bf"].t[:], 1.0), writes=[c["ones_bf"].r])
    c["ng"] = kb.sb(st, "ng", [128, 4 * 6 * 8], F32)
    kb.dma("sp", c["ng"].t[:], P["norm_g"][:, :], writes=[c["ng"].r])
    c["ngh"] = kb.sb(st, "ngh", [128, 4 * 6 * 8], F32)
    kb.op("dve", lambda e: e.tensor_scalar(out=c["ngh"].t[:], in0=c["ng"].t[:], scalar1=0.5, scalar2=None,
                                           op0=ALU.mult), reads=[c["ng"].r], writes=[c["ngh"].r])
    c["eps"] = kb.sb(st, "eps", [128, 1], F32)
    kb.op("dve", lambda e: e.memset(c["eps"].t[:], EPS), writes=[c["eps"].r])
    c["one"] = kb.sb(st, "one", [128, 1], F32)
    kb.op("dve", lambda e: e.memset(c["one"].t[:], 1.0), writes=[c["one"].r])
    c["ident_f"] = kb.sb(st, "ident_f", [128, 128], F32)
    kb.dma("sp", c["ident_f"].t[:], P["ident"][:, :], writes=[c["ident_f"].r])
    c["ident_bf"] = kb.sb(st, "ident_bf", [128, 128], BF16)
    kb.op("dve", lambda e: e.tensor_copy(out=c["ident_bf"].t[:], in_=c["ident_f"].t[:]), reads=[c["ident_f"].r], writes=[c["ident_bf"].r])
    return c


def gcol(c, which, l, j, ch):
    idx = (l * 6 + j) * 8 + ch
    return c[which].t[:, idx:idx + 1]


def rstd_from_sumsq(kb, ps_ss, rt, scratch, n, cst, scale):
    kb.op("act", lambda e: e.activation(out=scratch.t[:, :n], in_=ps_ss.t[:, :n], func=AF.Sqrt,
                                        bias=cst["eps"].t[:, 0:1], scale=scale),
          reads=[ps_ss.r, cst["eps"].r], writes=[scratch.r])
    kb.op("dve", lambda e: e.reciprocal(out=rt.t[:, :n], in_=scratch.t[:, :n]), reads=[scratch.r], writes=[rt.r])


class Sub:
    def __init__(self, ap, psum=False):
        self.t = ap
        self.rs = [Res()]
        self.rs[0].psum = psum

    @property
    def r(self):
        return self.rs[0]


def rolling(items, width, make):
    pending = list(items)
    active = {}
    for b in range(width):
        if pending:
            active[b] = make(pending.pop(0), b)
    while active:
        for b in list(active.keys()):
            try:
                next(active[b])
            except StopIteration:
                if pending:
                    active[b] = make(pending.pop(0), b)
                else:
                    del active[b]


class WLoader:
    def __init__(self, kb, st, width, nbuf=3):
        self.kb = kb
        self.stg = [kb.sb(st, "wstg%d" % i, [128, width], F32) for i in range(nbuf)]
        self.i = 0
        self.engs = ["act", "dve", "pool"]

    def load(self, dst, src, res, width, rows=128):
        kb = self.kb
        k = self.i
        self.i += 1
        stg = self.stg[k % len(self.stg)]
        eng = self.engs[k % len(self.engs)]
        kb.dma("sp", stg.t[:rows, :width], src, writes=[stg.r])
        if eng == "act":
            kb.op("act", lambda e: e.activation(out=dst, in_=stg.t[:rows, :width], func=AF.Copy), reads=[stg.r], writes=[res])
        else:
            kb.op(eng, lambda e: e.tensor_copy(out=dst, in_=stg.t[:rows, :width]), reads=[stg.r], writes=[res])


def ffn_stage(kb, cst, P, Xin, X, seqs, layer, j):
    nc = kb.nc
    T = 256
    with contextlib.ExitStack() as st:
        wg = kb.sb(st, "wg", [128, 8, DFF], BF16, nres=8)
        wu = kb.sb(st, "wu", [128, 8, DFF], BF16, nres=8)
        wd = kb.sb(st, "wd", [128, NFF, D], BF16, nres=NFF)
        xt = kb.sb(st, "xt", [128, 8, T], F32)
        ht = kb.sb(st, "ht", [128, 8, T], BF16)
        at = kb.sb(st, "at", [128, NFF, T], BF16, nres=NFF)
        ot = kb.sb(st, "ot", [128, 8, T], F32, nres=8)
        sq = kb.sb(st, "sq", [128, 8, T], BF16, nres=8)
        sg = [kb.sb(st, "sg%d" % i, [128, T], F32) for i in range(2)]
        rt = kb.sb(st, "rt", [128, T], F32)
        rs = kb.sb(st, "rs", [128, T], F32)
        tmp = [kb.sb(st, "tmp%d" % i, [128, T], F32) for i in range(2)]
        psg = [kb.ps(st, "psg%d" % i, [128, 512]) for i in range(2)]
        psu = [kb.ps(st, "psu%d" % i, [128, 512]) for i in range(2)]
        pso = [kb.ps(st, "pso%d" % i, [128, 512]) for i in range(2)]
        pss = kb.ps(st, "pss", [128, 512])

        Wg = P["ffn_w_gate"][layer * 2 + j]
        Wu = P["ffn_w_up"][layer * 2 + j]
        Wd = P["ffn_w_down"][layer * 2 + j]
        wl = WLoader(kb, st, DFF, nbuf=2)
        for c in range(8):
            wl.load(wg.t[:, c, :], Wg[c * 128:(c + 1) * 128, :], wg.rs[c], DFF)
            wl.load(wu.t[:, c, :], Wu[c * 128:(c + 1) * 128, :], wu.rs[c], DFF)
        for f in range(NFF):
            wl.load(wd.t[:, f, :], Wd[f * 128:(f + 1) * 128, :], wd.rs[f], D)

        for s, L in enumerate(seqs):
            if DBG < 1:
                break
            Xv = X[s].rearrange("(c p) l -> p c l", p=128)
            Xiv = Xin[s].rearrange("(c p) l -> p c l", p=128)
            for t0 in range(0, L, T):
                n = min(T, L - t0)
                xr = kb.dres(("X", s), t0, t0 + n)
                kb.dma("sp", xt.t[:, :, :n], Xiv[:, :, t0:t0 + n], reads=xr, writes=[xt.r])
                if DBG < 2:
                    kb.dma(STQ, Xv[:, :, t0:t0 + n], xt.t[:, :, :n], reads=[xt.r], writes=xr)
                    continue
                kb.op("act", lambda e: e.activation(out=sq.t[:, :, :n], in_=xt.t[:, :, :n], func=AF.Square),
                      reads=[xt.r], writes=sq.rs)

                def ssum(e, n=n):
                    for c in range(8):
                        ins = e.matmul(pss.t[:, :n], cst["ones_bf"].t[:], sq.t[:, c, :n], start=(c == 0), stop=(c == 7))
                    return ins
                kb.op("pe", ssum, reads=sq.rs + [cst["ones_bf"].r], writes=[pss.r])
                rstd_from_sumsq(kb, pss, rt, rs, n, cst, 1.0 / D)
                for c in range(8):
                    kb.op("dve", lambda e, c=c: e.scalar_tensor_tensor(
                        out=ht.t[:, c, :n], in0=xt.t[:, c, :n], scalar=gcol(cst, "ng", layer, 3 * j + 0 if j == 0 else 4, c),
                        op0=ALU.mult, in1=rt.t[:, :n], op1=ALU.mult), reads=[xt.r, rt.r, cst["ng"].r], writes=[ht.r])
                if DBG < 3:
                    kb.dma(STQ, Xv[:, :, t0:t0 + n], xt.t[:, :, :n], reads=[xt.r, ht.r], writes=xr)
                    continue
                for f in range(NFF):
                    b = f % 2

                    def mmg(e, f=f, b=b, n=n):
                        for c in range(8):
                            ins = e.matmul(psg[b].t[:, :n], wg.t[:, c, f * 128:(f + 1) * 128], ht.t[:, c, :n],
                                           start=(c == 0), stop=(c == 7))
                        return ins

                    def mmu(e, f=f, b=b, n=n):
                        for c in range(8):
                            ins = e.matmul(psu[b].t[:, :n], wu.t[:, c, f * 128:(f + 1) * 128], ht.t[:, c, :n],
                                           start=(c == 0), stop=(c == 7))
                        return ins
                    kb.op("pe", mmg, reads=wg.rs + [ht.r], writes=[psg[b].r])
                    kb.op("pe", mmu, reads=wu.rs + [ht.r], writes=[psu[b].r])
                    kb.op("act", lambda e, b=b, n=n: e.activation(out=sg[b].t[:, :n], in_=psg[b].t[:, :n], func=AF.Silu),
                          reads=[psg[b].r], writes=[sg[b].r])
                    kb.op("dve", lambda e, b=b, f=f, n=n: e.tensor_tensor(out=at.t[:, f, :n], in0=psu[b].t[:, :n],
                                                                          in1=sg[b].t[:, :n], op=ALU.mult),
                          reads=[psu[b].r, sg[b].r], writes=[at.rs[f]])
                if DBG < 4:
                    kb.dma(STQ, Xv[:, :, t0:t0 + n], xt.t[:, :, :n], reads=[xt.r] + at.rs, writes=xr)
                    continue
                for d in range(8):
                    b = d % 2

                    def mmd(e, d=d, b=b, n=n):
                        for f in range(NFF):
                            ins = e.matmul(pso[b].t[:, :n], wd.t[:, f, d * 128:(d + 1) * 128], at.t[:, f, :n],
                                           start=(f == 0), stop=(f == NFF - 1))
                        return ins
                    kb.op("pe", mmd, reads=wd.rs + at.rs, writes=[pso[b].r])
                    kb.op("dve", lambda e, d=d, b=b, n=n: e.tensor_copy(out=ot.t[:, d, :n], in_=pso[b].t[:, :n]),
                          reads=[pso[b].r], writes=[ot.rs[d]])
                    kb.op("act", lambda e, d=d, b=b, n=n: e.activation(out=sq.t[:, d, :n], in_=ot.t[:, d, :n], func=AF.Square),
                          reads=[ot.rs[d]], writes=[sq.rs[d]])
                if DBG < 5:
                    kb.dma(STQ, Xv[:, :, t0:t0 + n], xt.t[:, :, :n], reads=[xt.r] + ot.rs + sq.rs, writes=xr)
                    continue
                kb.op("pe", ssum, reads=sq.rs + [cst["ones_bf"].r], writes=[pss.r])
                rstd_from_sumsq(kb, pss, rt, rs, n, cst, 1.0 / D)
                if DBG < 6:
                    kb.dma(STQ, Xv[:, :, t0:t0 + n], xt.t[:, :, :n], reads=[xt.r, rt.r], writes=xr)
                    continue
                jn = 1 if j == 0 else 5
                for c in range(8):
                    b = c % 2
                    kb.op("dve", lambda e, c=c, b=b, n=n: e.scalar_tensor_tensor(
                        out=tmp[b].t[:, :n], in0=ot.t[:, c, :n], scalar=gcol(cst, "ngh", layer, jn, c),
                        op0=ALU.mult, in1=rt.t[:, :n], op1=ALU.mult), reads=[ot.rs[c], rt.r, cst["ngh"].r], writes=[tmp[b].r])
                    kb.op(ADD_ENG, lambda e, c=c, b=b, n=n: e.tensor_tensor(out=xt.t[:, c, :n], in0=xt.t[:, c, :n],
                                                                           in1=tmp[b].t[:, :n], op=ALU.add),
                          reads=[tmp[b].r, xt.r], writes=[xt.r])
                kb.dma(STQ, Xv[:, :, t0:t0 + n], xt.t[:, :, :n], reads=[xt.r], writes=xr)
        kb.barrier()


class NormCtx:
    def __init__(self, kb, st, cst, T):
        self.kb, self.cst, self.T = kb, cst, T
        self.sq = kb.sb(st, "nsq", [128, 8, T], BF16)
        self.rt = kb.sb(st, "nrt", [128, T], F32)
        self.rs = kb.sb(st, "nrs", [128, T], F32)
        self.tmp = [kb.sb(st, "ntmp%d" % i, [128, T], F32) for i in range(2)]
        self.pss = kb.ps(st, "npss", [128, 512])

    def _rstd(self, src, n):
        kb, cst = self.kb, self.cst
        kb.op("act", lambda e: e.activation(out=self.sq.t[:, :, :n], in_=src.t[:, :, :n], func=AF.Square),
              reads=src.rs, writes=[self.sq.r])

        def ssum(e):
            for c in range(8):
                ins = e.matmul(self.pss.t[:, :n], cst["ones_bf"].t[:], self.sq.t[:, c, :n], start=(c == 0), stop=(c == 7))
            return ins
        kb.op("pe", ssum, reads=[self.sq.r, cst["ones_bf"].r], writes=[self.pss.r])
        rstd_from_sumsq(kb, self.pss, self.rt, self.rs, n, cst, 1.0 / D)

    def prenorm(self, xt, ht, n, layer, jn):
        kb, cst = self.kb, self.cst
        self._rstd(xt, n)
        for c in range(8):
            kb.op("dve", lambda e, c=c: e.scalar_tensor_tensor(
                out=ht.t[:, c, :n], in0=xt.t[:, c, :n], scalar=gcol(cst, "ng", layer, jn, c),
                op0=ALU.mult, in1=self.rt.t[:, :n], op1=ALU.mult), reads=xt.rs + [self.rt.r, cst["ng"].r], writes=ht.rs)

    def postnorm_res(self, ot, xt, n, layer, jn, half):
        kb, cst = self.kb, self.cst
        self._rstd(ot, n)
        which = "ngh" if half else "ng"
        for c in range(8):
            b = c % 2
            kb.op("dve", lambda e, c=c, b=b: e.scalar_tensor_tensor(
                out=self.tmp[b].t[:, :n], in0=ot.t[:, c, :n], scalar=gcol(cst, which, layer, jn, c),
                op0=ALU.mult, in1=self.rt.t[:, :n], op1=ALU.mult), reads=ot.rs + [self.rt.r, cst[which].r], writes=[self.tmp[b].r])
            kb.op("pool", lambda e, c=c, b=b: e.tensor_tensor(out=xt.t[:, c, :n], in0=xt.t[:, c, :n],
                                                             in1=self.tmp[b].t[:, :n], op=ALU.add),
                  reads=[self.tmp[b].r] + xt.rs, writes=xt.rs)


def outproj_phase(kb, cst, Wap, bias_tile, U, uname, Kdim, X, seqs, layer):
    KC = Kdim // 128
    T = 512
    with contextlib.ExitStack() as st:
        wo = kb.sb(st, "wo", [128, KC, D], BF16, nres=KC)
        wl = WLoader(kb, st, D)
        for k in range(KC):
            wl.load(wo.t[:, k, :], Wap[k * 128:(k + 1) * 128, :], wo.rs[k], D)
        nrm = NormCtx(kb, st, cst, T)
        xt = kb.sb(st, "oxt", [128, 8, T], F32)
        ut = kb.sb(st, "out", [128, KC, T], BF16)
        ot = kb.sb(st, "oot", [128, 8, T], F32)
        pso = [kb.ps(st, "opso%d" % i, [128, 512]) for i in range(2)]
        for s, L in enumerate(seqs):
            Xv = X[s].rearrange("(c p) l -> p c l", p=128)
            Uv = U[s].rearrange("(k p) l -> p k l", p=128)
            for t0 in range(0, L, T):
                n = min(T, L - t0)
                xr = kb.dres(("X", s), t0, t0 + n)
                ur = kb.dres((uname, s), t0, t0 + n)
                kb.dma("sp", xt.t[:, :, :n], Xv[:, :, t0:t0 + n], reads=xr, writes=[xt.r])
                kb.dma("sp", ut.t[:, :, :n], Uv[:, 0:KC, t0:t0 + n], reads=ur, writes=[ut.r])
                for d in range(8):
                    b = d % 2

                    def mmo(e, d=d, b=b):
                        for k in range(KC):
                            ins = e.matmul(pso[b].t[:, :n], wo.t[:, k, d * 128:(d + 1) * 128], ut.t[:, k, :n],
                                           start=(k == 0), stop=(k == KC - 1))
                        return ins
                    kb.op("pe", mmo, reads=wo.rs + [ut.r], writes=[pso[b].r])
                    if bias_tile is None:
                        kb.op("act", lambda e, d=d, b=b: e.activation(out=ot.t[:, d, :n], in_=pso[b].t[:, :n], func=AF.Copy),
                              reads=[pso[b].r], writes=[ot.r])
                    else:
                        kb.op("act", lambda e, d=d, b=b: e.activation(out=ot.t[:, d, :n], in_=pso[b].t[:, :n], func=AF.Identity,
                                                                       bias=bias_tile.t[:, d:d + 1], scale=1.0),
                              reads=[pso[b].r, bias_tile.r], writes=[ot.r])
                nrm.postnorm_res(ot, xt, n, layer, 3, False)
                kb.dma(STQ, Xv[:, :, t0:t0 + n], xt.t[:, :, :n], reads=[xt.r], writes=xr)
        kb.barrier()


R_H, R_DK, R_DV = 4, 256, 512


def ret_stage(kb, cst, P, S, X, seqs, layer):
    nc = kb.nc
    T = 256
    W = P["b_w_in"]
    with contextlib.ExitStack() as st:
        wi = kb.sb(st, "rwi", [128, 8, 6144], BF16, nres=8)
        wl = WLoader(kb, st, 3072, nbuf=2)
        for c in range(8):
            for hh in range(2):
                wl.load(wi.t[:, c, hh * 3072:(hh + 1) * 3072], W[c * 128:(c + 1) * 128, hh * 3072:(hh + 1) * 3072], wi.rs[c], 3072)
        nrm = NormCtx(kb, st, cst, T)
        xt = kb.sb(st, "rxt", [128, 8, T], F32)
        ht = kb.sb(st, "rht", [128, 8, T], BF16)
        rope = kb.sb(st, "rrope", [128, 4, T], F32)
        qk = kb.sb(st, "rqk", [128, 16, T], BF16, nres=16)
        RSL = 2
        x1s = [kb.sb(st, "rx1%d" % i, [128, T], F32) for i in range(RSL)]
        x2s = [kb.sb(st, "rx2%d" % i, [128, T], F32) for i in range(RSL)]
        tas = [[kb.sb(st, "rta%d_%d" % (j, i), [128, T], F32) for i in range(2)] for j in range(RSL)]
        tbs = [[kb.sb(st, "rtb%d_%d" % (j, i), [128, T], F32) for i in range(2)] for j in range(RSL)]
        vt = [kb.sb(st, "rvt%d" % i, [128, 2048], BF16) for i in range(2)]
        gt = [kb.sb(st, "rgt%d" % i, [128, 2048], BF16) for i in range(2)]
        ps1s = [kb.ps(st, "rps1%d" % i, [128, 512]) for i in range(RSL)]
        ps2s = [kb.ps(st, "rps2%d" % i, [128, 512]) for i in range(RSL)]
        psv = [kb.ps(st, "rpsv%d" % i, [128, 512]) for i in range(2)]
        for s, L in enumerate(seqs):
            Xv = X[s].rearrange("(c p) l -> p c l", p=128)
            QKv = S["QK"][s].rearrange("(k p) l -> p k l", p=128)
            for t0 in range(0, L, T):
                n = min(T, L - t0)
                xr = kb.dres(("X", s), t0, t0 + n)
                kb.dma("sp", xt.t[:, :, :n], Xv[:, :, t0:t0 + n], reads=xr, writes=[xt.r])
                kb.dma("sp", rope.t[:, :, :n], P["rope"][:, :, t0:t0 + n], writes=[rope.r])
                nrm.prenorm(xt, ht, n, layer, 2)
                def rchain(pr, b):
                    isk = pr >= 4
                    ps1, ps2, x1, x2, ta, tb = ps1s[b], ps2s[b], x1s[b], x2s[b], tas[b], tbs[b]
                    for half, pst in ((0, ps1), (1, ps2)):
                        cc = 2 * pr + half

                        def mm(e):
                            for c in range(8):
                                ins = e.matmul(pst.t[:, :n], wi.t[:, c, cc * 128:(cc + 1) * 128], ht.t[:, c, :n],
                                               start=(c == 0), stop=(c == 7))
                            return ins
                        kb.op("pe", mm, reads=wi.rs + [ht.r], writes=[pst.r])
                    yield
                    kb.op("act", lambda e: e.activation(out=x1.t[:, :n], in_=ps1.t[:, :n], func=AF.Copy), reads=[ps1.r], writes=[x1.r])
                    kb.op("act", lambda e: e.activation(out=x2.t[:, :n], in_=ps2.t[:, :n], func=AF.Copy), reads=[ps2.r], writes=[x2.r])
                    yield
                    ci, si = (2, 3) if isk else (0, 1)
                    kb.op("dve", lambda e: e.tensor_tensor(out=ta[0].t[:, :n], in0=x1.t[:, :n], in1=rope.t[:, ci, :n], op=ALU.mult),
                          reads=[x1.r, rope.r], writes=[ta[0].r])
                    kb.op("pool", lambda e: e.tensor_tensor(out=tb[0].t[:, :n], in0=x2.t[:, :n], in1=rope.t[:, ci, :n], op=ALU.mult),
                          reads=[x2.r, rope.r], writes=[tb[0].r])
                    yield
                    kb.op("dve", lambda e: e.tensor_tensor(out=ta[1].t[:, :n], in0=x2.t[:, :n], in1=rope.t[:, si, :n], op=ALU.mult),
                          reads=[x2.r, rope.r], writes=[ta[1].r])
                    kb.op("pool", lambda e: e.tensor_tensor(out=tb[1].t[:, :n], in0=x1.t[:, :n], in1=rope.t[:, si, :n], op=ALU.mult),
                          reads=[x1.r, rope.r], writes=[tb[1].r])
                    yield
                    kb.op("dve", lambda e: e.tensor_tensor(out=qk.t[:, 2 * pr, :n], in0=ta[0].t[:, :n], in1=ta[1].t[:, :n], op=ALU.subtract),
                          reads=[ta[0].r, ta[1].r], writes=[qk.rs[2 * pr]])
                    kb.op("pool", lambda e: e.tensor_tensor(out=qk.t[:, 2 * pr + 1, :n], in0=tb[0].t[:, :n], in1=tb[1].t[:, :n], op=ALU.add),
                          reads=[tb[0].r, tb[1].r], writes=[qk.rs[2 * pr + 1]])
                    yield
                rolling(list(range(8)), RSL, rchain)
                kb.dma(HSTQ, QKv[:, :, t0:t0 + n], qk.t[:, :, :n], reads=qk.rs, writes=kb.dres(("QK", s), t0, t0 + n))
                for tb_i in range(n // 128):
                    bb = tb_i % 2
                    for cb in range(8):
                        pb = cb % 2

                        def mmv(e, cb=cb, pb=pb, tb_i=tb_i):
                            for c in range(8):
                                ins = e.matmul(psv[pb].t[:, :], ht.t[:, c, tb_i * 128:(tb_i + 1) * 128],
                                               wi.t[:, c, 2048 + cb * 512:2048 + (cb + 1) * 512], start=(c == 0), stop=(c == 7))
                            return ins
                        kb.op("pe", mmv, reads=wi.rs + [ht.r], writes=[psv[pb].r])
                        if cb < 4:
                            kb.op("act", lambda e, cb=cb, pb=pb, bb=bb: e.activation(out=vt[bb].t[:, cb * 512:(cb + 1) * 512], in_=psv[pb].t[:, :], func=AF.Copy),
                                  reads=[psv[pb].r], writes=[vt[bb].r])
                        else:
                            kb.op("act", lambda e, cb=cb, pb=pb, bb=bb: e.activation(out=gt[bb].t[:, (cb - 4) * 512:(cb - 3) * 512], in_=psv[pb].t[:, :], func=AF.Silu),
                                  reads=[psv[pb].r], writes=[gt[bb].r])
                    r0 = t0 + tb_i * 128
                    kb.dma(HSTQ, S["V"][s][r0:r0 + 128, :], vt[bb].t[:, :], reads=[vt[bb].r], writes=kb.dres(("V", s), r0, r0 + 128))
                    kb.dma(HSTQ, S["G"][s][r0:r0 + 128, :], gt[bb].t[:, :], reads=[gt[bb].r], writes=kb.dres(("G", s), r0, r0 + 128))
        kb.barrier()
    Lmax = max(seqs)
    NBmax = Lmax // 128
    with contextlib.ExitStack() as st:
        dl = kb.sb(st, "rdl", [128, 8], F32)
        lg = kb.sb(st, "rlg", [128, 8], F32)
        nlg = kb.sb(st, "rnlg", [128, 8], F32)
        dmat = kb.sb(st, "rdmat", [128, 128], F32)
        tri = kb.sb(st, "rtri", [128, 2, 128], F32)
        dvals = kb.sb(st, "rdvals", [128, 32], F32)
        sc = kb.sb(st, "rsc", [128, 8, 32], F32)
        Ff = kb.sb(st, "rFf", [128, 4, 128], F32)
        Fb = kb.sb(st, "rFb", [128, 4, 128], F32)
        Dg = kb.sb(st, "rDg", [128, 4, 128], F32)
        tmpm = kb.sb(st, "rtmpm", [128, 128], F32)
        gnw = kb.sb(st, "rgnw", [128, 2048], F32)
        kb.dma("sp", dl.t[:], P["b_decay_logit"][:, :], writes=[dl.r])
        kb.dma("sp", dmat.t[:], P["dmat"][:, :], writes=[dmat.r])
        kb.dma("sp", tri.t[:], P["tri"][:, :, :], writes=[tri.r])
        kb.dma("sp", dvals.t[:], P["dvals"][:, :], writes=[dvals.r])
        kb.dma("sp", gnw.t[:], P["b_gn_w"][:, :], writes=[gnw.r])
        kb.op("act", lambda e: e.activation(out=lg.t[:], in_=dl.t[:], func=AF.Exp, scale=math.log(2.0)), reads=[dl.r], writes=[lg.r])
        kb.op("act", lambda e: e.activation(out=lg.t[:], in_=lg.t[:], func=AF.Ln, scale=-1.0, bias=cst["one"].t[:, 0:1]),
              reads=[lg.r, cst["one"].r], writes=[lg.r])
        kb.op("dve", lambda e: e.tensor_scalar(out=nlg.t[:], in0=lg.t[:], scalar1=-1.0, scalar2=None, op0=ALU.mult), reads=[lg.r], writes=[nlg.r])
        for hd in range(4):
            kb.op("act", lambda e, hd=hd: e.activation(out=Ff.t[:, hd, :], in_=dmat.t[:], func=AF.Exp, scale=lg.t[:, hd:hd + 1]),
                  reads=[dmat.r, lg.r], writes=[Ff.r])
            kb.op("act", lambda e, hd=hd: e.activation(out=Fb.t[:, hd, :], in_=dmat.t[:], func=AF.Exp, scale=nlg.t[:, 4 + hd:5 + hd]),
                  reads=[dmat.r, nlg.r], writes=[Fb.r])
            kb.op("dve", lambda e, hd=hd: e.tensor_tensor(out=tmpm.t[:], in0=Ff.t[:, hd, :], in1=tri.t[:, 0, :], op=ALU.mult),
                  reads=[Ff.r, tri.r], writes=[tmpm.r])
            kb.op("dve", lambda e, hd=hd: e.tensor_tensor(out=Dg.t[:, hd, :], in0=Fb.t[:, hd, :], in1=tri.t[:, 1, :], op=ALU.mult),
                  reads=[Fb.r, tri.r], writes=[Dg.r])
            kb.op("dve", lambda e, hd=hd: e.tensor_tensor(out=Dg.t[:, hd, :], in0=Dg.t[:, hd, :], in1=tmpm.t[:], op=ALU.add),
                  reads=[Dg.r, tmpm.r], writes=[Dg.r])
        for k in range(8):
            kb.op("act", lambda e, k=k: e.activation(out=sc.t[:, k, :], in_=dvals.t[:], func=AF.Exp, scale=lg.t[:, k:k + 1]),
                  reads=[dvals.r, lg.r], writes=[sc.r])
        qT = kb.sb(st, "rqT", [128, 2, Lmax], BF16)
        kT = kb.sb(st, "rkT", [128, 2, Lmax], BF16)
        vh = kb.sb(st, "rvh", [128, NBmax, 512], BF16)
        pT = [kb.sb(st, "rpT%d" % i, [128, 128], BF16) for i in range(3)]
        sgt = [kb.sb(st, "rsg%d" % i, [128, 512], BF16) for i in range(2)]
        on = kb.sb(st, "ron", [128, 512], F32)
        on2 = kb.sb(st, "ron2", [128, 512], F32)
        utok = kb.sb(st, "rutok", [128, 512], BF16)
        uT = [kb.sb(st, "ruT%d" % i, [128, 4, 128], BF16) for i in range(2)]
        stats = kb.sb(st, "rstats", [128, 6], F32)
        mv = kb.sb(st, "rmv", [128, 2], F32)
        sd = kb.sb(st, "rsd", [128, 1], F32)
        rstd = kb.sb(st, "rrstd", [128, 1], F32)
        pss = [kb.ps(st, "rpss%d" % i, [128, 512]) for i in range(3)]
        pso = [kb.ps(st, "rpso%d" % i, [128, 512]) for i in range(2)]
        pst = kb.ps(st, "rpst", [128, 1024], BF16)
        for s, L in enumerate(seqs):
            NB = L // 128
            QKv = S["QK"][s].rearrange("(k p) l -> p k l", p=128)
            Vv = S["V"][s].rearrange("(b p) e -> p b e", p=128)
            Uv = S["U"][s].rearrange("(k p) l -> p k l", p=128)
            for hd in range(4):
                allr = kb.dres(("QK", s), 0, L)
                for l0 in range(0, L, 1024):
                    l1 = min(L, l0 + 1024)
                    kb.dma("sp", qT.t[:, :, l0:l1], QKv[:, 2 * hd:2 * hd + 2, l0:l1], reads=allr, writes=[qT.r])
                    kb.dma("sp", kT.t[:, :, l0:l1], QKv[:, 8 + 2 * hd:8 + 2 * hd + 2, l0:l1], reads=allr, writes=[kT.r])
                for b0 in range(0, NB, 4):
                    b1 = min(NB, b0 + 4)
                    kb.dma("sp", vh.t[:, b0:b1, :], Vv[:, b0:b1, hd * 512:(hd + 1) * 512], reads=kb.dres(("V", s), 0, L), writes=[vh.r])
                it = 0
                for i in range(NB):
                    ob = i % 2
                    kb.dma("sp", sgt[ob].t[:], S["G"][s][i * 128:(i + 1) * 128, hd * 512:(hd + 1) * 512],
                           reads=kb.dres(("G", s), i * 128, (i + 1) * 128), writes=[sgt[ob].r])
                    pend = []

                    def emit_mmo(j, b3, ob):
                        kb.op("pe", lambda e: e.matmul(pso[ob].t[:, :], pT[b3].t[:], vh.t[:, j, :], start=(j == 0), stop=(j == NB - 1)),
                              reads=[pT[b3].r, vh.r], writes=[pso[ob].r])
                    for j in range(NB):
                        b3 = it % 3
                        it += 1

                        def mms(e, j=j, i=i, b3=b3):
                            e.matmul(pss[b3].t[:, 0:128], kT.t[:, 0, j * 128:(j + 1) * 128], qT.t[:, 0, i * 128:(i + 1) * 128], start=True, stop=False)
                            return e.matmul(pss[b3].t[:, 0:128], kT.t[:, 1, j * 128:(j + 1) * 128], qT.t[:, 1, i * 128:(i + 1) * 128], start=False, stop=True)
                        kb.op("pe", mms, reads=[kT.r, qT.r], writes=[pss[b3].r])
                        if i == j:
                            kb.op("dve", lambda e, b3=b3: e.tensor_tensor(out=pT[b3].t[:], in0=pss[b3].t[:, 0:128], in1=Dg.t[:, hd, :], op=ALU.mult),
                                  reads=[pss[b3].r, Dg.r], writes=[pT[b3].r])
                        elif j < i:
                            kb.op("dve", lambda e, b3=b3, i=i, j=j: e.scalar_tensor_tensor(
                                out=pT[b3].t[:], in0=pss[b3].t[:, 0:128], scalar=sc.t[:, hd, (i - j):(i - j) + 1], op0=ALU.mult,
                                in1=Ff.t[:, hd, :], op1=ALU.mult), reads=[pss[b3].r, Ff.r, sc.r], writes=[pT[b3].r])
                        else:
                            kb.op("dve", lambda e, b3=b3, i=i, j=j: e.scalar_tensor_tensor(
                                out=pT[b3].t[:], in0=pss[b3].t[:, 0:128], scalar=sc.t[:, 4 + hd, (j - i):(j - i) + 1], op0=ALU.mult,
                                in1=Fb.t[:, hd, :], op1=ALU.mult), reads=[pss[b3].r, Fb.r, sc.r], writes=[pT[b3].r])
                        pend.append((j, b3))
                        if len(pend) > 2:
                            emit_mmo(*pend.pop(0), ob)
                    while pend:
                        emit_mmo(*pend.pop(0), ob)
                    kb.op("dve", lambda e, ob=ob: e.bn_stats(out=stats.t[:], in_=pso[ob].t[:, :]), reads=[pso[ob].r], writes=[stats.r])
                    kb.op("dve", lambda e: e.bn_aggr(out=mv.t[:], in_=stats.t[:]), reads=[stats.r], writes=[mv.r])
                    kb.op("act", lambda e: e.activation(out=sd.t[:], in_=mv.t[:, 1:2], func=AF.Sqrt, bias=cst["eps"].t[:, 0:1], scale=1.0),
                          reads=[mv.r, cst["eps"].r], writes=[sd.r])
                    kb.op("dve", lambda e: e.reciprocal(out=rstd.t[:], in_=sd.t[:]), reads=[sd.r], writes=[rstd.r])
                    kb.op("dve", lambda e, ob=ob: e.tensor_scalar(out=on.t[:], in0=pso[ob].t[:, :], scalar1=mv.t[:, 0:1], scalar2=rstd.t[:, 0:1],
                                                                   op0=ALU.subtract, op1=ALU.mult), reads=[pso[ob].r, mv.r, rstd.r], writes=[on.r])
                    kb.op("pool", lambda e: e.tensor_tensor(out=on2.t[:], in0=on.t[:], in1=gnw.t[:, hd * 512:(hd + 1) * 512], op=ALU.mult),
                          reads=[on.r, gnw.r], writes=[on2.r])
                    kb.op("pool", lambda e, ob=ob: e.tensor_tensor(out=utok.t[:], in0=on2.t[:], in1=sgt[ob].t[:], op=ALU.mult),
                          reads=[on2.r, sgt[ob].r], writes=[utok.r])

                    def tr(e):
                        for q in range(4):
                            ins = e.transpose(out=pst.t[:, q * 128:(q + 1) * 128], in_=utok.t[:, q * 128:(q + 1) * 128], identity=cst["ident_bf"].t[:])
                        return ins
                    kb.op("pe", tr, reads=[utok.r, cst["ident_bf"].r], writes=[pst.r])
                    kb.op("act", lambda e, ob=ob: e.activation(out=uT[ob].t[:].rearrange("p a b -> p (a b)"), in_=pst.t[:, 0:512], func=AF.Copy),
                          reads=[pst.r], writes=[uT[ob].r])
                    kb.dma(HSTQ, Uv[:, hd * 4:(hd + 1) * 4, i * 128:(i + 1) * 128], uT[ob].t[:], reads=[uT[ob].r],
                           writes=kb.dres(("U", s), i * 128, (i + 1) * 128))
        kb.barrier()
    outproj_phase(kb, cst, P["b_w_out"], None, S["U"], "U", 2048, X, seqs, layer)


G_H = 8


def gdn_stage(kb, cst, P, S, X, seqs, layer):
    nc = kb.nc
    ja = layer // 3
    T = 256
    W = P["a_w_in"][ja]
    with contextlib.ExitStack() as st:
        wi = kb.sb(st, "gwi", [128, 8, 4128], BF16, nres=8)
        wl = WLoader(kb, st, 2064, nbuf=2)
        for c in range(8):
            for hh in range(2):
                wl.load(wi.t[:, c, hh * 2064:(hh + 1) * 2064], W[c * 128:(c + 1) * 128, hh * 2064:(hh + 1) * 2064], wi.rs[c], 2064)
        N2 = T + 2
        nrm = NormCtx(kb, st, cst, N2)
        xt = kb.sb(st, "gxt", [128, 8, N2], F32)
        ht = kb.sb(st, "ght", [128, 8, N2], BF16)
        cw = kb.sb(st, "gcw", [128, 72], F32)
        kb.dma("sp", cw.t[:], P["a_conv_w"][ja], writes=[cw.r])
        nA = kb.sb(st, "gnA", [128, 16], F32)
        dtb = kb.sb(st, "gdtb", [128, 16], F32)
        kb.dma("sp", nA.t[:], P["a_a_log"][ja], writes=[nA.r])
        kb.dma("sp", dtb.t[:], P["a_dt_bias"][ja], writes=[dtb.r])
        kb.op("act", lambda e: e.activation(out=nA.t[:], in_=nA.t[:], func=AF.Exp), reads=[nA.r], writes=[nA.r])
        kb.op("dve", lambda e: e.tensor_scalar(out=nA.t[:], in0=nA.t[:], scalar1=-1.0, scalar2=None, op0=ALU.mult), reads=[nA.r], writes=[nA.r])
        NSL = 4
        pc = [kb.sb(st, "gpc%d" % i, [128, N2], F32) for i in range(NSL)]
        acc = [kb.sb(st, "gacc%d" % i, [128, T], F32) for i in range(NSL)]
        sl = [kb.sb(st, "gsl%d" % i, [128, T], F32) for i in range(NSL)]
        sqn = [kb.sb(st, "gsqn%d" % i, [128, T], BF16) for i in range(NSL)]
        rn = [kb.sb(st, "grn%d" % i, [128, T], F32) for i in range(NSL)]
        rn2 = [kb.sb(st, "grn2%d" % i, [128, T], F32) for i in range(NSL)]
        qkv = kb.sb(st, "gqkv", [128, 24, T], BF16, nres=24)
        zt = [kb.sb(st, "gzt%d" % i, [128, 1024], BF16) for i in range(2)]
        gbt = [kb.sb(st, "ggb%d" % i, [128, 32], F32) for i in range(2)]
        t16 = kb.sb(st, "gt16", [128, 2, 8], F32)
        psq = [kb.ps(st, "gpsq%d" % i, [128, 512]) for i in range(NSL)]
        psn_big = [kb.ps(st, "gpsn%d" % i, [128, 512]) for i in range(2)]
        psn = [Sub(psn_big[i // 2].t[:, (i % 2) * 256:(i % 2 + 1) * 256], psum=True) for i in range(NSL)]
        psz = [psq[0], psq[1]]
        psab = psq[2]

        def lockstep(gens):
            gens = list(gens)
            while gens:
                for g in list(gens):
                    try:
                        next(g)
                    except StopIteration:
                        gens.remove(g)
        for s, L in enumerate(seqs):
            Xv = X[s].rearrange("(c p) l -> p c l", p=128)
            QKVv = S["QKV"][s].rearrange("(k p) l -> p k l", p=128)
            for t0 in range(0, L, T):
                n = min(T, L - t0)
                n2 = n + 2
                lo, hi = max(t0 - 1, 0), min(t0 + n + 1, L)
                xr = kb.dres(("X", s), lo, hi)
                if t0 == 0:
                    kb.op("pool", lambda e: e.memset(xt.t[:, :, 0:1], 0.0), writes=[xt.r])
                if t0 + n == L:
                    kb.op("pool", lambda e: e.memset(xt.t[:, :, n + 1:n + 2], 0.0), writes=[xt.r])
                kb.dma("sp", xt.t[:, :, lo - (t0 - 1):hi - (t0 - 1)], Xv[:, :, lo:hi], reads=xr, writes=[xt.r])
                nrm.prenorm(xt, ht, n2, layer, 2)
                def chain(cc, b):
                    def mm(e):
                        for c in range(8):
                            ins = e.matmul(psq[b].t[:, :n2], wi.t[:, c, cc * 128:(cc + 1) * 128], ht.t[:, c, :n2],
                                           start=(c == 0), stop=(c == 7))
                        return ins
                    kb.op("pe", mm, reads=wi.rs + [ht.r], writes=[psq[b].r])
                    yield
                    kb.op("act", lambda e: e.activation(out=pc[b].t[:, :n2], in_=psq[b].t[:, :n2], func=AF.Copy), reads=[psq[b].r], writes=[pc[b].r])
                    yield
                    kb.op("pool", lambda e: e.tensor_scalar(out=acc[b].t[:, :n], in0=pc[b].t[:, 0:n], scalar1=cw.t[:, cc:cc + 1], scalar2=None, op0=ALU.mult),
                          reads=[pc[b].r, cw.r], writes=[acc[b].r])
                    yield
                    for tap in (1, 2):
                        kb.op("dve", lambda e: e.scalar_tensor_tensor(
                            out=acc[b].t[:, :n], in0=pc[b].t[:, tap:tap + n], scalar=cw.t[:, tap * 24 + cc:tap * 24 + cc + 1], op0=ALU.mult,
                            in1=acc[b].t[:, :n], op1=ALU.add), reads=[pc[b].r, cw.r, acc[b].r], writes=[acc[b].r])
                        yield
                    if cc >= 16:
                        kb.op("act", lambda e: e.activation(out=qkv.t[:, cc, :n], in_=acc[b].t[:, :n], func=AF.Silu), reads=[acc[b].r], writes=[qkv.rs[cc]])
                        yield
                    else:
                        kb.op("act", lambda e: e.activation(out=sl[b].t[:, :n], in_=acc[b].t[:, :n], func=AF.Silu), reads=[acc[b].r], writes=[sl[b].r])
                        yield
                        kb.op("act", lambda e: e.activation(out=sqn[b].t[:, :n], in_=sl[b].t[:, :n], func=AF.Square), reads=[sl[b].r], writes=[sqn[b].r])
                        yield
                        kb.op("pe", lambda e: e.matmul(psn[b].t[:, :n], cst["ones_bf"].t[:], sqn[b].t[:, :n], start=True, stop=True),
                              reads=[sqn[b].r, cst["ones_bf"].r], writes=[psn[b].r])
                        yield
                        kb.op("act", lambda e: e.activation(out=rn2[b].t[:, :n], in_=psn[b].t[:, :n], func=AF.Sqrt, bias=cst["eps"].t[:, 0:1], scale=1.0),
                              reads=[psn[b].r, cst["eps"].r], writes=[rn2[b].r])
                        yield
                        kb.op("dve", lambda e: e.reciprocal(out=rn[b].t[:, :n], in_=rn2[b].t[:, :n]), reads=[rn2[b].r], writes=[rn[b].r])
                        yield
                        scl = (128.0 ** -0.5) if cc < 8 else 1.0
                        kb.op("dve", lambda e: e.scalar_tensor_tensor(
                            out=qkv.t[:, cc, :n], in0=sl[b].t[:, :n], scalar=scl, op0=ALU.mult, in1=rn[b].t[:, :n], op1=ALU.mult),
                            reads=[sl[b].r, rn[b].r], writes=[qkv.rs[cc]])
                        yield
                rolling(list(range(24)), NSL, chain)
                kb.dma(HSTQ, QKVv[:, :, t0:t0 + n], qkv.t[:, :, :n], reads=qkv.rs, writes=kb.dres(("QKV", s), t0, t0 + n))
                for tb_i in range(n // 128):
                    bb = tb_i % 2
                    c0 = 1 + tb_i * 128
                    for zb in range(2):
                        def mmz(e, zb=zb, c0=c0):
                            for c in range(8):
                                ins = e.matmul(psz[zb].t[:, :], ht.t[:, c, c0:c0 + 128], wi.t[:, c, 3072 + zb * 512:3072 + (zb + 1) * 512],
                                               start=(c == 0), stop=(c == 7))
                            return ins
                        kb.op("pe", mmz, reads=wi.rs + [ht.r], writes=[psz[zb].r])
                        kb.op("act", lambda e, zb=zb, bb=bb: e.activation(out=zt[bb].t[:, zb * 512:(zb + 1) * 512], in_=psz[zb].t[:, :], func=AF.Silu),
                              reads=[psz[zb].r], writes=[zt[bb].r])

                    def mmab(e, c0=c0):
                        for c in range(8):
                            ins = e.matmul(psab.t[:, 0:32], ht.t[:, c, c0:c0 + 128], wi.t[:, c, 4096:4128], start=(c == 0), stop=(c == 7))
                        return ins
                    kb.op("pe", mmab, reads=wi.rs + [ht.r], writes=[psab.r])
                    abv = psab.t[:, 0:32].rearrange("p (d k) -> p d k", d=2)
                    kb.op("dve", lambda e: e.tensor_tensor(out=t16.t[:], in0=abv[:, :, 0:8], in1=dtb.t[:].rearrange("p (d k) -> p d k", d=2), op=ALU.add),
                          reads=[psab.r, dtb.r], writes=[t16.r])
                    kb.op("dve", lambda e, bb=bb: e.tensor_copy(out=gbt[bb].t[:, 16:32].rearrange("p (d k) -> p d k", d=2), in_=abv[:, :, 8:16]),
                          reads=[psab.r], writes=[gbt[bb].r])
                    kb.op("act", lambda e: e.activation(out=t16.t[:], in_=t16.t[:], func=AF.Exp), reads=[t16.r], writes=[t16.r])
                    kb.op("act", lambda e: e.activation(out=t16.t[:], in_=t16.t[:], func=AF.Ln, bias=cst["one"].t[:, 0:1], scale=1.0),
                          reads=[t16.r, cst["one"].r], writes=[t16.r])
                    kb.op("act", lambda e, bb=bb: e.activation(out=gbt[bb].t[:, 16:32], in_=gbt[bb].t[:, 16:32], func=AF.Sigmoid), reads=[gbt[bb].r], writes=[gbt[bb].r])
                    kb.op("dve", lambda e, bb=bb: e.tensor_tensor(out=gbt[bb].t[:, 0:16], in0=t16.t[:].rearrange("p d k -> p (d k)"), in1=nA.t[:], op=ALU.mult),
                          reads=[t16.r, nA.r, gbt[bb].r], writes=[gbt[bb].r])
                    r0 = t0 + tb_i * 128
                    kb.dma(HSTQ, S["Z"][s][r0:r0 + 128, :], zt[bb].t[:, :], reads=[zt[bb].r], writes=kb.dres(("Z", s), r0, r0 + 128))
                    kb.dma(HSTQ, S["GB"][s][r0:r0 + 128, :], gbt[bb].t[:, :], reads=[gbt[bb].r], writes=kb.dres(("GB", s), r0, r0 + 128))
        kb.barrier()
    AX = mybir.AxisListType
    HS = 4
    W4 = HS * 128
    with contextlib.ExitStack() as st:
        gm = kb.sb(st, "ggm", [128, 4, 128], F32)
        negm = kb.sb(st, "gnegm", [128, 4, W4], F32)
        ones_f = kb.sb(st, "gones", [128, 128], F32)
        nw = kb.sb(st, "gnw", [128, 1024], F32)
        bmk = kb.sb(st, "gbmk", [128, 4, W4], BF16)
        bmf = kb.sb(st, "gbmf", [128, W4], F32)
        gmb = kb.sb(st, "ggmb", [128, 4, 128], BF16)
        negb = kb.sb(st, "gnegb", [128, 4, W4], BF16)
        kb.dma("sp", gm.t[:], P["gmask"][:, :, :], writes=[gm.r])
        kb.op("dve", lambda e: e.tensor_copy(out=gmb.t[:].rearrange("p a b -> p (a b)"), in_=gm.t[:].rearrange("p a b -> p (a b)")), reads=[gm.r], writes=[gmb.r])
        for i in range(4):
            kb.dma("sp", negm.t[:, i, :], P["negm"][:, i, 0:W4], writes=[negm.r])
            kb.dma("sp", bmf.t[:], P["bmask"][:, i, 0:W4], writes=[bmf.r])
            kb.op("dve", lambda e, i=i: e.tensor_copy(out=bmk.t[:, i, :], in_=bmf.t[:]), reads=[bmf.r], writes=[bmk.r])
        kb.op("dve", lambda e: e.tensor_copy(out=negb.t[:].rearrange("p a b -> p (a b)"), in_=negm.t[:].rearrange("p a b -> p (a b)")), reads=[negm.r], writes=[negb.r])
        kb.dma("sp", nw.t[:], P["a_norm_w"][ja], writes=[nw.r])
        kb.op("pool", lambda e: e.memset(ones_f.t[:], 1.0), writes=[ones_f.r])

        def alloc_half(hh):
            t = {}
            n3 = lambda nm, dt: kb.sb(st, "g%s%d" % (nm, hh), [128, HS, 128], dt)
            t["Q"] = [kb.sb(st, "gQ%d_%d" % (hh, i), [128, 3, HS, 128], BF16) for i in range(2)]
            t["G"] = [kb.sb(st, "gG%d_%d" % (hh, i), [128, 32], F32) for i in range(2)]
            for nm in ("DAs", "DTc", "Yf", "Tf", "tav", "ot", "Sf"):
                t[nm] = n3(nm, F32)
            for nm in ("GmA", "GmT", "Yb", "Tb", "NGb", "vb", "kbg", "kdec", "nwT", "vn", "attnT", "Sb"):
                t[nm] = n3(nm, BF16)
            t["Pb"] = [n3("Pb%d_" % i, BF16) for i in range(2)]
            t["Ppb"] = [n3("Ppb%d_" % i, BF16) for i in range(2)]
            t["Bm"] = [n3("Bm%d_" % i, BF16) for i in range(3)]
            t["gc16"] = kb.sb(st, "ggc%d" % hh, [128, 2 * HS], F32)
            t["sm"] = kb.sb(st, "gsm%d" % hh, [128, 6, HS], F32)
            t["PX"] = kb.ps(st, "gPX%d" % hh, [128, 512])
            t["PY"] = kb.ps(st, "gPY%d" % hh, [128, 512])
            return t
        TL = [[alloc_half(0), alloc_half(1)], [alloc_half(2), alloc_half(3)]]

        def flat(t):
            return t.t[:].rearrange("p h m -> p (h m)")

        def hs(h):
            return slice(h * 128, (h + 1) * 128)

        def mm4(ps, lhs, rhs, lsel=None, rsel=None):
            def f(e):
                for h in range(HS):
                    l = lhs.t[:, h, :] if lsel is None else lhs.t[:, lsel, h, :]
                    r = rhs.t[:, h, :] if rsel is None else rhs.t[:, rsel, h, :]
                    ins = e.matmul(ps.t[:, hs(h)], l, r, start=True, stop=True)
                return ins
            return f

        def bfv(ps):
            return ps.t[:].bitcast(BF16)

        def tr4(ps, src, sel=None, off=0):
            def f(e):
                pv = bfv(ps)
                for h in range(HS):
                    i_ = src.t[:, h, :] if sel is None else src.t[:, sel, h, :]
                    ins = e.transpose(out=pv[:, off + h * 128:off + (h + 1) * 128], in_=i_, identity=cst["ident_bf"].t[:])
                return ins
            return f

        def half_iter(t, hh, s, dr, it, cb, QKVv4, Uv):
            h0 = hh * HS
            x0, x1 = (0, 1) if dr == 0 else (2, 3)
            c0 = cb * 128
            bq = it % 2
            Q, G = t["Q"][bq], t["G"][bq]
            PX, PY = t["PX"], t["PY"]
            sm, gc16 = t["sm"], t["gc16"]
            GmA, GmT, DAs, DTc = t["GmA"], t["GmT"], t["DAs"], t["DTc"]
            Pb, Ppb, Bm, Yf, Yb, Tf, Tb, NGb = t["Pb"], t["Ppb"], t["Bm"], t["Yf"], t["Yb"], t["Tf"], t["Tb"], t["NGb"]
            vb, kbg, kdec, nwT, vn, attnT, tav, ot, Sf, Sb = t["vb"], t["kbg"], t["kdec"], t["nwT"], t["vn"], t["attnT"], t["tav"], t["ot"], t["Sf"], t["Sb"]
            for tq in range(3):
                kb.dma("sp", Q.t[:, tq, :, :], QKVv4[:, tq, h0:h0 + HS, c0:c0 + 128], reads=kb.dres(("QKV", s), c0, c0 + 128), writes=[Q.r])
            kb.dma("sp", G.t[:], S["GB"][s][c0:c0 + 128, :], reads=kb.dres(("GB", s), c0, c0 + 128), writes=[G.r])
            yield
            gofs = dr * 8 + h0
            g8 = G.t[:, gofs:gofs + HS]
            be8 = G.t[:, 16 + gofs:16 + gofs + HS]
            for h in range(HS):
                gcolh = G.t[:, gofs + h:gofs + h + 1]
                kb.op("pool", lambda e: e.tensor_scalar(out=GmA.t[:, h, :], in0=gm.t[:, x1, :], scalar1=gcolh, scalar2=None, op0=ALU.mult),
                      reads=[gm.r, G.r], writes=[GmA.r])
                kb.op("pool", lambda e: e.tensor_scalar(out=GmT.t[:, h, :], in0=gm.t[:, x0, :], scalar1=gcolh, scalar2=None, op0=ALU.mult),
                      reads=[gm.r, G.r], writes=[GmT.r])
                yield

            def mmA(e):
                e.matmul(PX.t[:, :], gmb.t[:, x0, :], flat(GmA), start=True, stop=False)
                return e.matmul(PX.t[:, :], cst["ident_bf"].t[:], negb.t[:, x0, :], start=False, stop=True)
            kb.op("pe", mmA, reads=[gmb.r, GmA.r, negb.r, cst["ident_bf"].r], writes=[PX.r])
            kb.op("act", lambda e: e.activation(out=flat(DAs), in_=PX.t[:], func=AF.Exp), reads=[PX.r], writes=[DAs.r])
            yield

            def mmT(e):
                e.matmul(PY.t[:, :], gmb.t[:, x1, :], flat(GmT), start=True, stop=False)
                return e.matmul(PY.t[:, :], cst["ident_bf"].t[:], negb.t[:, x1, :], start=False, stop=True)
            kb.op("pe", mmT, reads=[gmb.r, GmT.r, negb.r, cst["ident_bf"].r], writes=[PY.r])
            kb.op("act", lambda e: e.activation(out=flat(DTc), in_=PY.t[:], func=AF.Exp), reads=[PY.r], writes=[DTc.r])
            yield

            def mmg(e):
                e.matmul(PX.t[:, 0:HS], gm.t[:, x0, :], g8, start=True, stop=True)
                return e.matmul(PX.t[:, HS:2 * HS], ones_f.t[:], g8, start=True, stop=True)
            kb.op("pe", mmg, reads=[gm.r, G.r, ones_f.r], writes=[PX.r])
            kb.op("dve", lambda e: e.tensor_copy(out=gc16.t[:], in_=PX.t[:, 0:2 * HS]), reads=[PX.r], writes=[gc16.r])
            yield
            kb.op("act", lambda e: e.activation(out=sm.t[:, 0:2, :].rearrange("p a k -> p (a k)"), in_=gc16.t[:], func=AF.Exp), reads=[gc16.r], writes=[sm.r])
            kb.op("dve", lambda e: e.tensor_tensor(out=sm.t[:, 5, :], in0=gc16.t[:, HS:2 * HS], in1=gc16.t[:, 0:HS], op=ALU.subtract), reads=[gc16.r, sm.r], writes=[sm.r])
            yield
            kb.op("act", lambda e: e.activation(out=sm.t[:, 2, :], in_=sm.t[:, 5, :], func=AF.Exp), reads=[sm.r], writes=[sm.r])
            kb.op("dve", lambda e: e.tensor_tensor(out=sm.t[:, 3, :], in0=sm.t[:, 0, :], in1=be8, op=ALU.mult), reads=[sm.r, G.r], writes=[sm.r])
            kb.op("dve", lambda e: e.tensor_scalar(out=sm.t[:, 4, :], in0=be8, scalar1=-1.0, scalar2=None, op0=ALU.mult), reads=[G.r, sm.r], writes=[sm.r])
            yield
            kb.op("pe", tr4(PY, Q, sel=1, off=0), reads=[Q.r, cst["ident_bf"].r], writes=[PY.r])
            kb.op("pe", tr4(PY, Q, sel=2, off=W4), reads=[Q.r, cst["ident_bf"].r], writes=[PY.r])
            yield
            for h in range(HS):
                kb.op("act", lambda e: e.activation(out=kbg.t[:, h, :], in_=bfv(PY)[:, hs(h)], func=AF.Copy, scale=sm.t[:, 3, h:h + 1]),
                      reads=[PY.r, sm.r], writes=[kbg.r])
                kb.op("act", lambda e: e.activation(out=kdec.t[:, h, :], in_=bfv(PY)[:, hs(h)], func=AF.Copy, scale=sm.t[:, 2, h:h + 1]),
                      reads=[PY.r, sm.r], writes=[kdec.r])
                kb.op("act", lambda e: e.activation(out=vb.t[:, h, :], in_=bfv(PY)[:, W4 + h * 128:W4 + (h + 1) * 128], func=AF.Copy, scale=G.t[:, 16 + gofs + h:16 + gofs + h + 1]),
                      reads=[PY.r, G.r], writes=[vb.r])
                yield
            kb.op("pe", mm4(PX, Q, Q, lsel=1, rsel=1), reads=[Q.r], writes=[PX.r])
            yield
            for h in range(HS):
                kb.op("dve", lambda e: e.scalar_tensor_tensor(out=Pb[0].t[:, h, :], in0=PX.t[:, hs(h)], scalar=sm.t[:, 4, h:h + 1], op0=ALU.mult,
                                                               in1=DAs.t[:, h, :], op1=ALU.mult), reads=[PX.r, sm.r, DAs.r], writes=[Pb[0].r])
                yield
            kb.op("pe", tr4(PY, Pb[0]), reads=[Pb[0].r, cst["ident_bf"].r], writes=[PY.r])
            kb.op("dve", lambda e: e.tensor_copy(out=flat(Ppb[0]), in_=bfv(PY)[:, 0:W4]), reads=[PY.r], writes=[Ppb[0].r])
            yield
            for lv in range(3):
                kb.op("pool", lambda e: e.tensor_tensor(out=flat(Bm[lv]), in0=flat(Ppb[0]), in1=bmk.t[:, 1 + lv, :], op=ALU.mult),
                      reads=[Ppb[0].r, bmk.r], writes=[Bm[lv].r])
                yield
            kb.op("pool", lambda e: e.tensor_tensor(out=flat(Pb[1]), in0=flat(Pb[0]), in1=bmk.t[:, 0, :], op=ALU.mult), reads=[Pb[0].r, bmk.r], writes=[Pb[1].r])
            kb.op("pool", lambda e: e.tensor_tensor(out=flat(Ppb[1]), in0=flat(Ppb[0]), in1=bmk.t[:, 0, :], op=ALU.mult), reads=[Ppb[0].r, bmk.r], writes=[Ppb[1].r])
            yield
            for h in range(HS):
                kb.op("pool", lambda e: e.tensor_tensor(out=Yf.t[:, h, :], in0=Ppb[1].t[:, h, :], in1=cst["ident_f"].t[:], op=ALU.add),
                      reads=[Ppb[1].r, cst["ident_f"].r], writes=[Yf.r])
            kb.op("act", lambda e: e.activation(out=flat(Yb), in_=flat(Yf), func=AF.Copy), reads=[Yf.r], writes=[Yb.r])
            yield
            for k in range(3):
                cur, nxt = 1 - (k % 2), k % 2
                kb.op("pe", mm4(PX, Ppb[cur], Pb[cur]), reads=[Ppb[cur].r, Pb[cur].r], writes=[PX.r])
                if k < 2:
                    kb.op("pe", mm4(PY, Pb[cur], Ppb[cur]), reads=[Ppb[cur].r, Pb[cur].r], writes=[PY.r])
                yield
                kb.op("act", lambda e: e.activation(out=flat(Pb[nxt]), in_=PX.t[:], func=AF.Copy), reads=[PX.r], writes=[Pb[nxt].r])
                if k < 2:
                    kb.op("dve", lambda e: e.tensor_copy(out=flat(Ppb[nxt]), in_=PY.t[:]), reads=[PY.r], writes=[Ppb[nxt].r])
                yield
                kb.op("pe", mm4(PX, Pb[nxt], Yb), reads=[Pb[nxt].r, Yb.r], writes=[PX.r])
                yield
                kb.op("dve", lambda e: e.tensor_tensor(out=flat(Yf), in0=PX.t[:], in1=flat(Yf), op=ALU.add), reads=[PX.r, Yf.r], writes=[Yf.r])
                yield
                kb.op("act", lambda e: e.activation(out=flat(Yb), in_=flat(Yf), func=AF.Copy), reads=[Yf.r], writes=[Yb.r])
                yield
            kb.op("pe", tr4(PY, Yb), reads=[Yb.r, cst["ident_bf"].r], writes=[PY.r])
            yield
            kb.op("dve", lambda e: e.tensor_copy(out=flat(Tf), in_=bfv(PY)[:, 0:W4]), reads=[PY.r], writes=[Tf.r])
            yield
            kb.op("act", lambda e: e.activation(out=flat(Tb), in_=flat(Tf), func=AF.Copy), reads=[Tf.r], writes=[Tb.r])
            yield
            for lv in range(3):
                kb.op("pe", mm4(PX, Bm[lv], Tb), reads=[Bm[lv].r, Tb.r], writes=[PX.r])
                yield
                kb.op("act", lambda e: e.activation(out=flat(NGb), in_=PX.t[:], func=AF.Copy), reads=[PX.r], writes=[NGb.r])
                yield
                kb.op("pe", mm4(PY, Yb, NGb), reads=[Yb.r, NGb.r], writes=[PY.r])
                yield
                kb.op("dve", lambda e: e.tensor_tensor(out=flat(Tf), in0=PY.t[:], in1=flat(Tf), op=ALU.add), reads=[PY.r, Tf.r], writes=[Tf.r])
                yield
                kb.op("act", lambda e: e.activation(out=flat(Tb), in_=flat(Tf), func=AF.Copy), reads=[Tf.r], writes=[Tb.r])
                yield
                kb.op("pe", tr4(PX, Tb), reads=[Tb.r, cst["ident_bf"].r], writes=[PX.r])
                yield
                kb.op("dve", lambda e: e.tensor_copy(out=flat(Yb), in_=bfv(PX)[:, 0:W4]), reads=[PX.r], writes=[Yb.r])
                yield
            kb.op("pe", mm4(PX, kbg, Yb), reads=[kbg.r, Yb.r], writes=[PX.r])
            yield
            kb.op("act", lambda e: e.activation(out=flat(nwT), in_=PX.t[:], func=AF.Copy, scale=-1.0), reads=[PX.r], writes=[nwT.r])
            yield

            def mmvn(e):
                for h in range(HS):
                    e.matmul(PY.t[:, hs(h)], Yb.t[:, h, :], vb.t[:, h, :], start=True, stop=False)
                    ins = e.matmul(PY.t[:, hs(h)], nwT.t[:, h, :], Sb.t[:, h, :], start=False, stop=True)
                return ins
            kb.op("pe", mmvn, reads=[Yb.r, vb.r, nwT.r, Sb.r], writes=[PY.r])
            yield
            kb.op("act", lambda e: e.activation(out=flat(vn), in_=PY.t[:], func=AF.Copy), reads=[PY.r], writes=[vn.r])
            kb.op("pe", mm4(PX, Q, Q, lsel=1, rsel=0), reads=[Q.r], writes=[PX.r])
            yield
            kb.op("dve", lambda e: e.tensor_tensor(out=flat(attnT), in0=PX.t[:], in1=flat(DTc), op=ALU.mult), reads=[PX.r, DTc.r], writes=[attnT.r])
            kb.op("pe", mm4(PY, Q, Sb, lsel=0), reads=[Q.r, Sb.r], writes=[PY.r])
            yield
            kb.op("pe", mm4(PX, attnT, vn), reads=[attnT.r, vn.r], writes=[PX.r])
            yield
            kb.op("act", lambda e: e.activation(out=flat(tav), in_=PX.t[:], func=AF.Copy), reads=[PX.r], writes=[tav.r])
            yield
            for h in range(HS):
                kb.op("dve", lambda e: e.scalar_tensor_tensor(out=ot.t[:, h, :], in0=PY.t[:, hs(h)], scalar=sm.t[:, 0, h:h + 1], op0=ALU.mult,
                                                               in1=tav.t[:, h, :], op1=ALU.add), reads=[PY.r, sm.r, tav.r], writes=[ot.r])
                yield
            kb.op("pe", mm4(PX, kdec, vn), reads=[kdec.r, vn.r], writes=[PX.r])
            yield
            for h in range(HS):
                kb.op("dve", lambda e: e.scalar_tensor_tensor(out=Sf.t[:, h, :], in0=Sf.t[:, h, :], scalar=sm.t[:, 1, h:h + 1], op0=ALU.mult,
                                                               in1=PX.t[:, hs(h)], op1=ALU.add), reads=[PX.r, sm.r, Sf.r], writes=[Sf.r])
                yield
            kb.op("act", lambda e: e.activation(out=flat(Sb), in_=flat(Sf), func=AF.Copy), reads=[Sf.r], writes=[Sb.r])
            yield
            onm = "OF" if dr == 0 else "OB"
            kb.dma(HSTQ, S[onm][s][c0:c0 + 128, h0 * 128:(h0 + HS) * 128], flat(ot), reads=[ot.r], writes=kb.dres((onm + str(hh), s), c0, c0 + 128))
            yield

        def lockstep(gens):
            gens = list(gens)
            while gens:
                for g in list(gens):
                    try:
                        next(g)
                    except StopIteration:
                        gens.remove(g)

        for s, L in enumerate(seqs):
            NB = L // 128
            QKVv4 = S["QKV"][s].rearrange("(t h p) l -> p t h l", t=3, h=8, p=128)
            for dr in range(2):
                for hh in range(2):
                    kb.op("pool", lambda e: e.memset(flat(TL[dr][hh]["Sf"]), 0.0), writes=[TL[dr][hh]["Sf"].r])
                    kb.op("pool", lambda e: e.memset(flat(TL[dr][hh]["Sb"]), 0.0), writes=[TL[dr][hh]["Sb"].r])
            for it in range(NB):
                if GDN_SKIP_B:
                    break
                lockstep([half_iter(TL[dr][hh], hh, s, dr, it, (it if dr == 0 else NB - 1 - it), QKVv4, None)
                          for dr in range(2) for hh in range(2)])
        kb.barrier()
    with contextlib.ExitStack() as st:
        nw = kb.sb(st, "gnw2", [128, 1024], F32)
        kb.dma("sp", nw.t[:], P["a_norm_w"][ja], writes=[nw.r])
        NCH = 3

        def talloc(i):
            t = {}
            t["of"] = kb.sb(st, "gtof%d" % i, [128, 8, 128], F32)
            t["ob"] = kb.sb(st, "gtob%d" % i, [128, 8, 128], F32)
            t["zl"] = kb.sb(st, "gtzl%d" % i, [128, 1024], BF16)
            t["sq"] = kb.sb(st, "gtsq%d" % i, [128, 8, 128], F32)
            t["utok"] = kb.sb(st, "gtut%d" % i, [128, 8, 128], BF16)
            t["uT"] = kb.sb(st, "gtuT%d" % i, [128, 8, 128], BF16)
            t["ss8"] = kb.sb(st, "gtss%d" % i, [128, 8], F32)
            t["rs8"] = kb.sb(st, "gtrs%d" % i, [128, 8], F32)
            t["PK"] = kb.ps(st, "gtPK%d" % i, [128, 1024], BF16)
            return t
        TT = [talloc(i) for i in range(NCH)]

        def flat(t):
            return t.t[:].rearrange("p h m -> p (h m)")

        def tail(t, s, c0, Uv):
            of, ob, zl, sq, utok, uT, ss8, rs8, PK = t["of"], t["ob"], t["zl"], t["sq"], t["utok"], t["uT"], t["ss8"], t["rs8"], t["PK"]
            kb.dma("sp", flat(of), S["OF"][s][c0:c0 + 128, :], reads=kb.dres(("OF0", s), c0, c0 + 128) + kb.dres(("OF1", s), c0, c0 + 128), writes=[of.r])
            kb.dma("sp", flat(ob), S["OB"][s][c0:c0 + 128, :], reads=kb.dres(("OB0", s), c0, c0 + 128) + kb.dres(("OB1", s), c0, c0 + 128), writes=[ob.r])
            kb.dma("sp", zl.t[:], S["Z"][s][c0:c0 + 128, :], reads=kb.dres(("Z", s), c0, c0 + 128), writes=[zl.r])
            yield
            kb.op("pool", lambda e: e.tensor_tensor(out=flat(of), in0=flat(of), in1=flat(ob), op=ALU.add), reads=[of.r, ob.r], writes=[of.r])
            yield
            kb.op("act", lambda e: e.activation(out=flat(sq), in_=flat(of), func=AF.Square), reads=[of.r], writes=[sq.r])
            yield
            kb.op("dve", lambda e: e.tensor_reduce(out=ss8.t[:], in_=sq.t[:], axis=AX.X, op=ALU.add), reads=[sq.r], writes=[ss8.r])
            yield
            kb.op("act", lambda e: e.activation(out=rs8.t[:], in_=ss8.t[:], func=AF.Sqrt, bias=cst["eps"].t[:, 0:1], scale=1.0 / 128.0),
                  reads=[ss8.r, cst["eps"].r], writes=[rs8.r])
            yield
            kb.op("dve", lambda e: e.reciprocal(out=rs8.t[:], in_=rs8.t[:]), reads=[rs8.r], writes=[rs8.r])
            yield
            for h in range(8):
                kb.op("dve", lambda e: e.scalar_tensor_tensor(out=sq.t[:, h, :], in0=of.t[:, h, :], scalar=rs8.t[:, h:h + 1], op0=ALU.mult,
                                                               in1=nw.t[:, h * 128:(h + 1) * 128], op1=ALU.mult), reads=[of.r, rs8.r, nw.r, sq.r], writes=[sq.r])
                yield
            kb.op("pool", lambda e: e.tensor_tensor(out=flat(utok), in0=flat(sq), in1=zl.t[:], op=ALU.mult), reads=[sq.r, zl.r], writes=[utok.r])
            yield

            def tr(e):
                for h in range(8):
                    ins = e.transpose(out=PK.t[:, h * 128:(h + 1) * 128], in_=utok.t[:, h, :], identity=cst["ident_bf"].t[:])
                return ins
            kb.op("pe", tr, reads=[utok.r, cst["ident_bf"].r], writes=[PK.r])
            yield
            kb.op("act", lambda e: e.activation(out=flat(uT), in_=PK.t[:], func=AF.Copy), reads=[PK.r], writes=[uT.r])
            yield
            kb.dma(HSTQ, Uv[:, 0:8, c0:c0 + 128], uT.t[:], reads=[uT.r], writes=kb.dres(("U", s), c0, c0 + 128))
            yield

        def lockstep2(gens):
            gens = list(gens)
            while gens:
                for g in list(gens):
                    try:
                        next(g)
                    except StopIteration:
                        gens.remove(g)
        for s, L in enumerate(seqs):
            Uv = S["U"][s].rearrange("(k p) l -> p k l", p=128)
            blocks = list(range(0, L, 128))
            for g0 in range(0, len(blocks), NCH):
                lockstep2([tail(TT[i], s, c0, Uv) for i, c0 in enumerate(blocks[g0:g0 + NCH])])
        kb.barrier()
    outproj_phase(kb, cst, P["a_w_out"][ja], None, S["U"], "U", 1024, X, seqs, layer)


TWO_PI = 2.0 * math.pi


def cast_load(kb, wl, dst2d, src2d, res, rows, width, chunk=1024):
    for c0 in range(0, width, chunk):
        c1 = min(width, c0 + chunk)
        wl.load(dst2d[:, c0:c1], src2d[:, c0:c1], res, c1 - c0, rows=rows)


def hy_tables(kb, st, P, L):
    N1 = 2 * L // 64
    tb = {}
    tb["f1t"] = kb.sb(st, "hf1t", [N1, 2 * N1], BF16)
    tb["tw"] = kb.sb(st, "htw", [64, N1 * 192], BF16)
    tb["twt"] = kb.sb(st, "htwt", [64, N1 * 192], BF16)
    tb["g2t"] = kb.sb(st, "hg2t", [N1, N1], BF16)
    with contextlib.ExitStack() as st_tmp:
        wl = WLoader(kb, st_tmp, 1024, nbuf=2)
        cast_load(kb, wl, tb["f1t"].t[:, :], P["hy_f1t_%d" % L][:, :], tb["f1t"].r, N1, 2 * N1)
        cast_load(kb, wl, tb["tw"].t[:, :], P["hy_tw_%d" % L][:, :], tb["tw"].r, 64, N1 * 192)
        cast_load(kb, wl, tb["twt"].t[:, :], P["hy_twt_%d" % L][:, :], tb["twt"].r, 64, N1 * 192)
        cast_load(kb, wl, tb["g2t"].t[:, :], P["hy_g2t_%d" % L][:, :], tb["g2t"].r, N1, N1)
        kb.barrier()
    return tb


def hy_F1(kb, tb, Q, ab, srcv, K, N1, C, D1, d1name):
    xt = tb["xt"]

    def chain(n2, c5, b):
        cs = slice(c5 * 512, (c5 + 1) * 512)
        kb.dma("sp", xt[b].t[:K, :], srcv[0:K, n2, cs], reads=tb["src_res"], writes=[xt[b].r])
        yield

        def mm(e):
            for ri in range(2):
                ins = e.matmul(tb["QF"][2 * b + ri].t[:N1, 0:512], tb["f1t"].t[:K, ri * N1:(ri + 1) * N1], xt[b].t[:K, :], start=True, stop=True)
            return ins
        kb.op("pe", mm, reads=[tb["f1t"].r, xt[b].r], writes=[tb["QF"][2 * b].r, tb["QF"][2 * b + 1].r])
        yield
        kb.op("act", lambda e: e.activation(out=ab[b].t[:N1, 0, :], in_=tb["QF"][2 * b].t[:N1, 0:512], func=AF.Copy), reads=[tb["QF"][2 * b].r], writes=[ab[b].r])
        kb.op("dve", lambda e: e.tensor_copy(out=ab[b].t[:N1, 1, :], in_=tb["QF"][2 * b + 1].t[:N1, 0:512]), reads=[tb["QF"][2 * b + 1].r], writes=[ab[b].r])
        yield
        kb.dma(HSTQ, D1[0:N1, n2, :, cs], ab[b].t[:N1, :, :], reads=[ab[b].r], writes=kb.dres(d1name, n2 * 128, n2 * 128 + 128))
        yield

    items = [(n2, c5) for n2 in range(64) for c5 in range(C // 512)]
    for i0 in range(0, len(items), 4):
        hy_lockstep([chain(n2, c5, b) for b, (n2, c5) in enumerate(items[i0:i0 + 4])])


def hy_lockstep(gens):
    gens = list(gens)
    while gens:
        for g in list(gens):
            try:
                next(g)
            except StopIteration:
                gens.remove(g)


def hy_F2(kb, tb, Q, N1, C, D1, d1name, mode, HS=None, hsname=None, order=0, D2=None, d2name=None):
    a2, hb, xr, xi, tt, yb, zb = tb["a2"], tb["hb"], tb["xr"], tb["xi"], tb["tt"], tb["yb"], tb["zb"]
    tw, twt = tb["tw"], tb["twt"]

    def chain(k1, c5, b):
        QR, QI = tb["QF"][2 * b], tb["QF"][2 * b + 1]
        cw = slice(c5 * 512, (c5 + 1) * 512)
        kb.dma("sp", a2[b].t[:, :, :], D1[k1, :, :, cw], reads=kb.dres(d1name, 0, 64 * 128), writes=[a2[b].r])
        if mode != "H":
            kb.dma("sp", hb[b].t[:, :, :], HS[k1, :, :, order * 1024 + c5 * 512:order * 1024 + (c5 + 1) * 512], reads=kb.dres(hsname, k1 * 128, k1 * 128 + 128), writes=[hb[b].r])
        yield

        def tws(j):
            return tw.t[:, (k1 * 3 + j) * 64:(k1 * 3 + j + 1) * 64]

        def twts(j):
            return twt.t[:, (k1 * 3 + j) * 64:(k1 * 3 + j + 1) * 64]

        def mmre(e):
            for hf in range(1):
                cs = slice(0, 512)
                e.matmul(QR.t[:64, cs], tws(0), a2[b].t[:, 0, cs], start=True, stop=False)
                ins = e.matmul(QR.t[:64, cs], tws(2), a2[b].t[:, 1, cs], start=False, stop=True)
            return ins

        def mmim(e):
            for hf in range(1):
                cs = slice(0, 512)
                e.matmul(QI.t[:64, cs], tws(1), a2[b].t[:, 0, cs], start=True, stop=False)
                ins = e.matmul(QI.t[:64, cs], tws(0), a2[b].t[:, 1, cs], start=False, stop=True)
            return ins
        kb.op("pe", mmre, reads=[tw.r, a2[b].r], writes=[QR.r])
        kb.op("pe", mmim, reads=[tw.r, a2[b].r], writes=[QI.r])
        yield
        if mode == "H":
            kb.op("act", lambda e: e.activation(out=hb[b].t[:, 0, :], in_=QR.t[:64, 0:512], func=AF.Copy), reads=[QR.r], writes=[hb[b].r])
            kb.op("dve", lambda e: e.tensor_copy(out=hb[b].t[:, 1, :], in_=QI.t[:64, 0:512]), reads=[QI.r], writes=[hb[b].r])
            yield
            kb.dma(HSTQ, HS[k1, :, :, cw], hb[b].t[:, :, :], reads=[hb[b].r], writes=kb.dres(hsname, k1 * 128, k1 * 128 + 128))
            yield
            return
        kb.op("act", lambda e: e.activation(out=xr[b].t[:, :], in_=QR.t[:64, 0:512], func=AF.Copy), reads=[QR.r], writes=[xr[b].r])
        kb.op("dve", lambda e: e.tensor_copy(out=xi[b].t[:, :], in_=QI.t[:64, 0:512]), reads=[QI.r], writes=[xi[b].r])
        yield
        t0_, t1_ = tt[b]
        kb.op("dve", lambda e: e.tensor_tensor(out=t0_.t[:, :], in0=xr[b].t[:, :], in1=hb[b].t[:, 0, :], op=ALU.mult), reads=[xr[b].r, hb[b].r], writes=[t0_.r])
        kb.op("pool", lambda e: e.tensor_tensor(out=t1_.t[:, :], in0=xi[b].t[:, :], in1=hb[b].t[:, 1, :], op=ALU.mult), reads=[xi[b].r, hb[b].r], writes=[t1_.r])
        yield
        kb.op("dve", lambda e: e.tensor_tensor(out=yb[b].t[:, 0, :], in0=t0_.t[:, :], in1=t1_.t[:, :], op=ALU.subtract), reads=[t0_.r, t1_.r], writes=[yb[b].r])
        yield
        kb.op("pool", lambda e: e.tensor_tensor(out=t1_.t[:, :], in0=xr[b].t[:, :], in1=hb[b].t[:, 1, :], op=ALU.mult), reads=[xr[b].r, hb[b].r], writes=[t1_.r])
        kb.op("dve", lambda e: e.tensor_tensor(out=t0_.t[:, :], in0=xi[b].t[:, :], in1=hb[b].t[:, 0, :], op=ALU.mult), reads=[xi[b].r, hb[b].r], writes=[t0_.r])
        yield
        kb.op("pool", lambda e: e.tensor_tensor(out=yb[b].t[:, 1, :], in0=t0_.t[:, :], in1=t1_.t[:, :], op=ALU.add), reads=[t0_.r, t1_.r], writes=[yb[b].r])
        yield

        def mzre(e):
            for hf in range(1):
                cs = slice(0, 512)
                e.matmul(QR.t[:64, cs], twts(0), yb[b].t[:, 0, cs], start=True, stop=False)
                ins = e.matmul(QR.t[:64, cs], twts(1), yb[b].t[:, 1, cs], start=False, stop=True)
            return ins

        def mzim(e):
            for hf in range(1):
                cs = slice(0, 512)
                e.matmul(QI.t[:64, cs], twts(0), yb[b].t[:, 1, cs], start=True, stop=False)
                ins = e.matmul(QI.t[:64, cs], twts(2), yb[b].t[:, 0, cs], start=False, stop=True)
            return ins
        kb.op("pe", mzre, reads=[twt.r, yb[b].r], writes=[QR.r])
        kb.op("pe", mzim, reads=[twt.r, yb[b].r], writes=[QI.r])
        yield
        kb.op("act", lambda e: e.activation(out=zb[b].t[:, 0, :], in_=QR.t[:64, 0:512], func=AF.Copy), reads=[QR.r], writes=[zb[b].r])
        kb.op("dve", lambda e: e.tensor_copy(out=zb[b].t[:, 1, :], in_=QI.t[:64, 0:512]), reads=[QI.r], writes=[zb[b].r])
        yield
        kb.dma(HSTQ, D2[k1, :, :, cw], zb[b].t[:, :, :], reads=[zb[b].r], writes=kb.dres(d2name, k1 * 128, k1 * 128 + 128))
        yield

    items = [(k1, c5) for k1 in range(N1) for c5 in range(C // 512)]
    for i0 in range(0, len(items), 4):
        hy_lockstep([chain(k1, c5, b) for b, (k1, c5) in enumerate(items[i0:i0 + 4])])


def hyena_stage(kb, cst, P, S, X, seqs, layer):
    nc = kb.nc
    for L in sorted(set(seqs)):
        if "H0" in HY_SKIP:
            break
        N = 2 * L
        N1 = N // 64
        NR = N // 128
        with contextlib.ExitStack() as st:
            w1 = kb.sb(st, "hw1", [33, 64], F32)
            w2 = kb.sb(st, "hw2", [64, 64], F32)
            w3 = kb.sb(st, "hw3", [64, 4096], F32)
            prm = kb.sb(st, "hprm", [64, 4], F32)
            bf = kb.sb(st, "hbf", [64, 2], F32)
            kb.dma("sp", w1.t[:], P["c_f_w1"][:, :], writes=[w1.r])
            kb.dma("sp", w2.t[:], P["c_f_w2"][:, :], writes=[w2.r])
            kb.dma("sp", w3.t[:], P["c_f_w3"][:, :], writes=[w3.r])
            kb.dma("sp", prm.t[:], P["c_f_prm"][:, :], writes=[prm.r])
            kb.op("dve", lambda e: e.tensor_tensor(out=bf.t[:, 0:1], in0=prm.t[:, 0:1], in1=prm.t[:, 1:2], op=ALU.mult), reads=[prm.r], writes=[bf.r])
            kb.op("dve", lambda e: e.tensor_tensor(out=bf.t[:, 1:2], in0=prm.t[:, 2:3], in1=prm.t[:, 3:4], op=ALU.mult), reads=[prm.r, bf.r], writes=[bf.r])
            zt = kb.sb(st, "hzt", [33, 512], F32)
            aa = kb.sb(st, "haa", [64, 512], F32)
            m1 = kb.sb(st, "hm1", [64, 512], F32)
            m2 = kb.sb(st, "hm2", [64, 512], F32)
            h1 = kb.sb(st, "hh1", [64, 512], F32)
            h2 = kb.sb(st, "hh2", [64, N], F32)
            ntp = kb.sb(st, "hntp", [128, NR], F32)
            dlt = kb.sb(st, "hdlt", [128, 1024], F32)
            win = kb.sb(st, "hwin", [128, 1024], F32)
            fl = [kb.sb(st, "hfl%d" % i, [128, 2048], BF16) for i in range(2)]
            kb.dma("sp", ntp.t[:], P["hy_ntp_%d" % L][:, :], writes=[ntp.r])
            kb.dma("sp", dlt.t[:], P["hy_delta"][:, :], writes=[dlt.r])
            Q = [kb.ps(st, "hQ%d" % i, [128, 1024]) for i in range(4)]

            def sin_layer(ps, col_f, col_b, dst):
                kb.op("act", lambda e: e.activation(out=aa.t[:, :], in_=ps.t[:64, 0:512], func=AF.Identity, bias=bf.t[:, col_b:col_b + 1], scale=prm.t[:, col_f:col_f + 1]),
                      reads=[ps.r, bf.r, prm.r], writes=[aa.r])
                kb.op("dve", lambda e: e.tensor_scalar(out=m1.t[:, :], in0=aa.t[:, :], scalar1=math.pi, scalar2=-TWO_PI, op0=ALU.is_gt, op1=ALU.mult), reads=[aa.r], writes=[m1.r])
                kb.op("dve", lambda e: e.tensor_scalar(out=m2.t[:, :], in0=aa.t[:, :], scalar1=-math.pi, scalar2=TWO_PI, op0=ALU.is_lt, op1=ALU.mult), reads=[aa.r], writes=[m2.r])
                kb.op("dve", lambda e: e.tensor_tensor(out=m1.t[:, :], in0=m1.t[:, :], in1=m2.t[:, :], op=ALU.add), reads=[m1.r, m2.r], writes=[m1.r])
                kb.op("dve", lambda e: e.tensor_tensor(out=aa.t[:, :], in0=aa.t[:, :], in1=m1.t[:, :], op=ALU.add), reads=[aa.r, m1.r], writes=[aa.r])
                kb.op("act", lambda e: e.activation(out=dst, in_=aa.t[:, :], func=AF.Sin), reads=[aa.r], writes=[h1.r, h2.r])

            for c0 in range(0, N, 512):
                kb.dma("sp", zt.t[:, :], P["hy_z_%d" % L][:, c0:c0 + 512], writes=[zt.r])
                kb.op("pe", lambda e: e.matmul(Q[0].t[:64, 0:512], w1.t[:, :], zt.t[:, :], start=True, stop=True), reads=[w1.r, zt.r], writes=[Q[0].r])
                sin_layer(Q[0], 1, 0, h1.t[:, :])
                kb.op("pe", lambda e: e.matmul(Q[1].t[:64, 0:512], w2.t[:, :], h1.t[:, :], start=True, stop=True), reads=[w2.r, h1.r], writes=[Q[1].r])
                sin_layer(Q[1], 3, 1, h2.t[:, c0:c0 + 512])
            FILT = S["FILT"][L]
            for r in range(NR):
                b = r % 2
                dsel = 0 if r * 128 < L else 1
                kb.op("act", lambda e, r=r: e.activation(out=win.t[:, :], in_=dlt.t[:, :], func=AF.Exp, scale=ntp.t[:, r:r + 1]), reads=[dlt.r, ntp.r], writes=[win.r])
                for q in range(4):
                    pq = Q[q % 2 + 2]
                    kb.op("pe", lambda e, q=q, pq=pq, r=r: e.matmul(pq.t[:, 0:512], h2.t[:, r * 128:(r + 1) * 128], w3.t[:, dsel * 2048 + q * 512:dsel * 2048 + (q + 1) * 512],
                                                                     start=True, stop=True), reads=[h2.r, w3.r], writes=[pq.r])
                    kb.op("dve", lambda e, q=q, pq=pq, b=b: e.tensor_tensor(out=fl[b].t[:, q * 512:(q + 1) * 512], in0=pq.t[:, 0:512], in1=win.t[:, (q % 2) * 512:(q % 2 + 1) * 512], op=ALU.mult),
                          reads=[pq.r, win.r], writes=[fl[b].r])
                kb.dma(HSTQ, FILT[r * 128:(r + 1) * 128, :], fl[b].t[:, :], reads=[fl[b].r], writes=kb.dres(("FILT", L), 0, 128))
            kb.barrier()
        with contextlib.ExitStack() as st:
            tb = hy_tables(kb, st, P, L)
            hy_work_tiles(kb, st, tb)
            Q = [kb.ps(st, "hQ%d" % i, [128, 512]) for i in range(4)]
            tb["QF"] = Q + [kb.ps(st, "hQF%d" % i, [128, 512]) for i in range(4)]
            tb["src_res"] = kb.dres(("FILT", L), 0, 128)
            hy_F1(kb, tb, Q, tb["ab"], S["FILT"][L].rearrange("(a b) c -> a b c", b=64), N1, N1, 2048, S["D1F"][L], ("D1F", L))
            hy_F2(kb, tb, Q, N1, 2048, S["D1F"][L], ("D1F", L), "H", HS=S["HS"][L], hsname=("HS", L))
            kb.barrier()
    with contextlib.ExitStack() as st:
        wi = kb.sb(st, "hwi", [128, 8, 3072], BF16, nres=8)
        wl = WLoader(kb, st, 3072, nbuf=2)
        for c in range(8):
            wl.load(wi.t[:, c, :], P["c_w_in"][c * 128:(c + 1) * 128, :], wi.rs[c], 3072)
        T = 256
        nrm = NormCtx(kb, st, cst, T)
        xt = kb.sb(st, "hxt", [128, 8, T], F32)
        ht = kb.sb(st, "hht", [128, 8, T], BF16)
        binb = kb.sb(st, "hbinb", [128, 3072], F32)
        kb.dma("sp", binb.t[:], P["c_b_in"][:, :], writes=[binb.r])
        pt = [kb.sb(st, "hpt%d" % i, [128, 3072], F32) for i in range(2)]
        zr = kb.sb(st, "hzr", [1, 3072], F32)
        kb.op("pool", lambda e: e.memset(zr.t[:], 0.0), writes=[zr.r])
        psp = [kb.ps(st, "hpsp%d" % i, [128, 512]) for i in range(2)]
        for s, L in enumerate(seqs):
            Xv = X[s].rearrange("(c p) l -> p c l", p=128)
            PP = S["PP"][s]
            kb.dma(HSTQ, PP[0:1, :], zr.t[:], reads=[zr.r], writes=kb.dres(("PP", s), 0, 1))
            kb.dma(HSTQ, PP[L + 1:L + 2, :], zr.t[:], reads=[zr.r], writes=kb.dres(("PP", s), L + 1, L + 2))
            for t0 in range(0, L, T):
                n = min(T, L - t0)
                kb.dma("sp", xt.t[:, :, :n], Xv[:, :, t0:t0 + n], reads=kb.dres(("X", s), t0, t0 + n), writes=[xt.r])
                nrm.prenorm(xt, ht, n, layer, 2)
                for tb_i in range(n // 128):
                    bb = tb_i % 2
                    for cb in range(6):
                        pb = cb % 2

                        def mmp(e, cb=cb, pb=pb, tb_i=tb_i):
                            for c in range(8):
                                ins = e.matmul(psp[pb].t[:, :], ht.t[:, c, tb_i * 128:(tb_i + 1) * 128], wi.t[:, c, cb * 512:(cb + 1) * 512],
                                               start=(c == 0), stop=(c == 7))
                            return ins
                        kb.op("pe", mmp, reads=wi.rs + [ht.r], writes=[psp[pb].r])
                        kb.op("dve", lambda e, cb=cb, pb=pb, bb=bb: e.tensor_tensor(out=pt[bb].t[:, cb * 512:(cb + 1) * 512], in0=psp[pb].t[:, :],
                                                                                     in1=binb.t[:, cb * 512:(cb + 1) * 512], op=ALU.add),
                              reads=[psp[pb].r, binb.r], writes=[pt[bb].r])
                    r0 = t0 + tb_i * 128
                    kb.dma(HSTQ, PP[1 + r0:1 + r0 + 128, :], pt[bb].t[:, :], reads=[pt[bb].r], writes=kb.dres(("PP", s), 1 + r0, 1 + r0 + 128))
        kb.barrier()
    with contextlib.ExitStack() as st:
        cwb = kb.sb(st, "hcwb", [128, 4, 3072], F32)
        for i in range(4):
            kb.dma("sp", cwb.t[:, i, :], P["c_conv"][:, i, :], writes=[cwb.r])
        pa = [[kb.sb(st, "hpa%d_%d" % (j, i), [128, 3072], F32) for i in range(3)] for j in range(2)]
        acc = [kb.sb(st, "hacc%d" % j, [128, 3072], F32) for j in range(2)]
        ob = [kb.sb(st, "hob%d" % j, [128, 3072], BF16) for j in range(2)]

        def cchain(s, r0, j):
            PP = S["PP"][s]
            p_, a_, o_ = pa[j], acc[j], ob[j]
            for i in range(3):
                kb.dma("sp", p_[i].t[:, :], PP[r0 + i:r0 + i + 128, :], reads=kb.dres(("PP", s), r0 + i, r0 + i + 128), writes=[p_[i].r])
            yield
            kb.op("dve", lambda e: e.tensor_tensor(out=a_.t[:, :], in0=p_[0].t[:, :], in1=cwb.t[:, 0, :], op=ALU.mult), reads=[p_[0].r, cwb.r], writes=[a_.r])
            kb.op("pool", lambda e: e.tensor_tensor(out=p_[1].t[:, :], in0=p_[1].t[:, :], in1=cwb.t[:, 1, :], op=ALU.mult), reads=[p_[1].r, cwb.r], writes=[p_[1].r])
            yield
            kb.op("dve", lambda e: e.tensor_tensor(out=a_.t[:, :], in0=a_.t[:, :], in1=cwb.t[:, 3, :], op=ALU.add), reads=[a_.r, cwb.r], writes=[a_.r])
            kb.op("pool", lambda e: e.tensor_tensor(out=p_[2].t[:, :], in0=p_[2].t[:, :], in1=cwb.t[:, 2, :], op=ALU.mult), reads=[p_[2].r, cwb.r], writes=[p_[2].r])
            yield
            kb.op("dve", lambda e: e.tensor_tensor(out=a_.t[:, :], in0=a_.t[:, :], in1=p_[1].t[:, :], op=ALU.add), reads=[a_.r, p_[1].r], writes=[a_.r])
            yield
            kb.op("pool", lambda e: e.tensor_tensor(out=o_.t[:, :], in0=a_.t[:, :], in1=p_[2].t[:, :], op=ALU.add), reads=[a_.r, p_[2].r], writes=[o_.r])
            yield
            kb.dma(HSTQ, S["HV"][s][0][r0:r0 + 128, :], o_.t[:, 0:1024], reads=[o_.r], writes=kb.dres(("HV0", s), r0, r0 + 128))
            kb.dma(HSTQ, S["HG"][s][r0:r0 + 128, :], o_.t[:, 1024:3072], reads=[o_.r], writes=kb.dres(("HG", s), r0, r0 + 128))
            yield
        for s, L in enumerate(seqs):
            blocks = list(range(0, L, 128))
            for i0 in range(0, len(blocks), 2):
                hy_lockstep([cchain(s, r0, j) for j, r0 in enumerate(blocks[i0:i0 + 2])])
        kb.barrier()
    for L in sorted(set(seqs)):
        if "H2" in HY_SKIP:
            break
        N1 = 2 * L // 64
        K = N1 // 2
        with contextlib.ExitStack() as st:
            tb = hy_tables(kb, st, P, L)
            hy_work_tiles(kb, st, tb)
            Q = [kb.ps(st, "hQ%d" % i, [128, 512]) for i in range(4)]
            tb["QF"] = Q + [kb.ps(st, "hQF%d" % i, [128, 512]) for i in range(4)]
            skb = kb.sb(st, "hskb", [128, 2, 1024], F32)
            kb.dma("sp", skb.t[:], P["c_bias_d"][:, :, :], writes=[skb.r])
            zt = [kb.sb(st, "hzt%d" % i, [128, 2, 512], BF16) for i in range(4)]
            ut = [kb.sb(st, "hut%d" % i, [64, 512], BF16) for i in range(4)]
            gt = [kb.sb(st, "hgt%d" % i, [64, 512], BF16) for i in range(4)]
            t1 = [kb.sb(st, "ht1%d" % i, [64, 512], BF16) for i in range(4)]
            t2 = [kb.sb(st, "ht2%d" % i, [64, 512], F32) for i in range(4)]
            zo = [kb.sb(st, "hzo%d" % i, [64, 512], BF16) for i in range(4)]
            for s, Ls in enumerate(seqs):
                if Ls != L:
                    continue
                for order in range(2):
                    Uin = S["HV"][s][order]
                    Uout = S["HV"][s][order + 1]
                    uin_name, uout_name = ("HV%d" % order, s), ("HV%d" % (order + 1), s)
                    Uiv = Uin.rearrange("(a b) c -> a b c", b=64)
                    Uov = Uout.rearrange("(a b) c -> a b c", b=64)
                    Gv = S["HG"][s].rearrange("(a b) c -> a b c", b=64)
                    tb["src_res"] = kb.dres(uin_name, 0, L)
                    hy_F1(kb, tb, Q, tb["ab"], Uiv, K, N1, 1024, S["D1"], "D1")
                    hy_F2(kb, tb, Q, N1, 1024, S["D1"], "D1", "conv", HS=S["HS"][L], hsname=("HS", L), order=order, D2=S["D2"], d2name="D2")

                    def g2chain(n2, hf, b):
                        cs = slice(hf * 512, (hf + 1) * 512)
                        kb.dma("sp", zt[b].t[:N1, :, :], S["D2"][0:N1, n2, :, cs], reads=kb.dres("D2", 0, N1 * 128), writes=[zt[b].r])
                        kb.dma("sp", ut[b].t[:K, :], Uiv[0:K, n2, cs], reads=kb.dres(uin_name, 0, L), writes=[ut[b].r])
                        kb.dma("sp", gt[b].t[:K, :], Gv[0:K, n2, order * 1024 + hf * 512:order * 1024 + (hf + 1) * 512], reads=kb.dres(("HG", s), 0, L), writes=[gt[b].r])
                        yield

                        def mmy(e):
                            e.matmul(Q[b].t[:K, 0:512], tb["g2t"].t[:N1, 0:K], zt[b].t[:N1, 0, :], start=True, stop=False)
                            return e.matmul(Q[b].t[:K, 0:512], tb["g2t"].t[:N1, K:2 * K], zt[b].t[:N1, 1, :], start=False, stop=True)
                        kb.op("pe", mmy, reads=[tb["g2t"].r, zt[b].r], writes=[Q[b].r])
                        kb.op("pool", lambda e: e.tensor_tensor(out=t1[b].t[:K, :], in0=ut[b].t[:K, :], in1=skb.t[:K, order, cs], op=ALU.mult), reads=[ut[b].r, skb.r], writes=[t1[b].r])
                        yield
                        kb.op("dve", lambda e: e.tensor_tensor(out=t2[b].t[:K, :], in0=Q[b].t[:K, 0:512], in1=t1[b].t[:K, :], op=ALU.add), reads=[Q[b].r, t1[b].r], writes=[t2[b].r])
                        yield
                        kb.op("pool", lambda e: e.tensor_tensor(out=zo[b].t[:K, :], in0=t2[b].t[:K, :], in1=gt[b].t[:K, :], op=ALU.mult), reads=[t2[b].r, gt[b].r], writes=[zo[b].r])
                        yield
                        kb.dma(HSTQ, Uov[0:K, n2, cs], zo[b].t[:K, :], reads=[zo[b].r], writes=kb.dres(uout_name, 0, L))
                        yield
                    items = [(n2, hf) for n2 in range(64) for hf in range(2)]
                    for i0 in range(0, len(items), 4):
                        hy_lockstep([g2chain(n2, hf, b) for b, (n2, hf) in enumerate(items[i0:i0 + 4])])
            kb.barrier()
    with contextlib.ExitStack() as st:
        zl = [kb.sb(st, "hzl%d" % i, [128, 8, 128], BF16) for i in range(2)]
        uT = [kb.sb(st, "huT%d" % i, [128, 8, 128], BF16) for i in range(2)]
        PK = kb.ps(st, "hPK", [128, 1024], BF16)
        for s, L in enumerate(seqs):
            Uv = S["U"][s].rearrange("(k p) l -> p k l", p=128)
            for i, r0 in enumerate(range(0, L, 128)):
                b = i % 2
                kb.dma("sp", zl[b].t[:].rearrange("p h m -> p (h m)"), S["HV"][s][2][r0:r0 + 128, :], reads=kb.dres(("HV2", s), 0, L), writes=[zl[b].r])

                def tr(e, b=b):
                    for h in range(8):
                        ins = e.transpose(out=PK.t[:, h * 128:(h + 1) * 128], in_=zl[b].t[:, h, :], identity=cst["ident_bf"].t[:])
                    return ins
                kb.op("pe", tr, reads=[zl[b].r, cst["ident_bf"].r], writes=[PK.r])
                kb.op("act", lambda e, b=b: e.activation(out=uT[b].t[:].rearrange("p h m -> p (h m)"), in_=PK.t[:], func=AF.Copy), reads=[PK.r], writes=[uT[b].r])
                kb.dma(HSTQ, Uv[:, 0:8, r0:r0 + 128], uT[b].t[:], reads=[uT[b].r], writes=kb.dres(("U", s), r0, r0 + 128))
        kb.barrier()
    with contextlib.ExitStack() as st:
        bo = kb.sb(st, "hbo", [128, 8], F32)
        kb.dma("sp", bo.t[:], P["c_b_out"][:, :], writes=[bo.r])
        outproj_phase(kb, cst, P["c_w_out"], bo, S["U"], "U", 1024, X, seqs, layer)


def hy_work_tiles(kb, st, tb):
    tb["xt"] = [kb.sb(st, "hwxt%d" % i, [128, 512], BF16) for i in range(4)]
    tb["ab"] = [kb.sb(st, "hwab%d" % i, [128, 2, 512], BF16) for i in range(4)]
    tb["a2"] = [kb.sb(st, "hwa2%d" % i, [64, 2, 512], BF16) for i in range(4)]
    tb["hb"] = [kb.sb(st, "hwhb%d" % i, [64, 2, 512], BF16) for i in range(4)]
    tb["xr"] = [kb.sb(st, "hwxr%d" % i, [64, 512], BF16) for i in range(4)]
    tb["xi"] = [kb.sb(st, "hwxi%d" % i, [64, 512], BF16) for i in range(4)]
    tb["tt"] = [[kb.sb(st, "hwtt%d_%d" % (i, j), [64, 512], BF16) for j in range(2)] for i in range(4)]
    tb["yb"] = [kb.sb(st, "hwyb%d" % i, [64, 2, 512], BF16) for i in range(4)]
    tb["zb"] = [kb.sb(st, "hwzb%d" % i, [64, 2, 512], BF16) for i in range(4)]


def copy_stage(kb, Xin, X, seqs):
    with contextlib.ExitStack() as st:
        xt = [kb.sb(st, "cxt%d" % i, [128, 8, 512], F32) for i in range(2)]
        k = 0
        for s, L in enumerate(seqs):
            Xv = X[s].rearrange("(c p) l -> p c l", p=128)
            Xiv = Xin[s].rearrange("(c p) l -> p c l", p=128)
            for t0 in range(0, L, 512):
                n = min(512, L - t0)
                b = k % 2
                k += 1
                kb.dma("sp", xt[b].t[:, :, :n], Xiv[:, :, t0:t0 + n], writes=[xt[b].r])
                kb.dma(STQ, Xv[:, :, t0:t0 + n], xt[b].t[:, :, :n], reads=[xt[b].r], writes=kb.dres(("X", s), t0, t0 + n))
        kb.barrier()


def build(seqs, stages):
    nc = bass.Bass("TRN2", target_bir_lowering=False)
    P = {}

    def din(name, shape):
        P[name] = nc.dram_tensor(name, list(shape), F32, kind="ExternalInput").ap()

    X = []
    for s, L in enumerate(seqs):
        X.append(nc.dram_tensor("xin%d" % s, [D, L], F32, kind="ExternalInput").ap())
    Y = []
    for s, L in enumerate(seqs):
        Y.append(nc.dram_tensor("yout%d" % s, [D, L], F32, kind="ExternalOutput").ap())
    din("norm_g", [128, 192])
    din("ffn_w_gate", [8, D, DFF])
    din("ffn_w_up", [8, D, DFF])
    din("ffn_w_down", [8, DFF, D])
    din("ident", [128, 128])
    din("rope", [128, 4, 4096])
    din("dmat", [128, 128])
    din("tri", [128, 2, 128])
    din("dvals", [128, 32])
    din("b_w_in", [D, 6144])
    din("b_decay_logit", [128, 8])
    din("b_gn_w", [128, 2048])
    din("b_w_out", [2048, D])
    din("a_w_in", [2, D, 4128])
    din("a_conv_w", [2, 128, 72])
    din("a_a_log", [2, 128, 16])
    din("a_dt_bias", [2, 128, 16])
    din("a_norm_w", [2, 128, 1024])
    din("a_w_out", [2, D, D])
    din("gmask", [128, 4, 128])
    din("negm", [128, 4, 1024])
    din("bmask", [128, 4, 1024])
    din("c_w_in", [D, 3072])
    din("c_b_in", [128, 3072])
    din("c_conv", [128, 4, 3072])
    din("c_f_w1", [33, 64])
    din("c_f_w2", [64, 64])
    din("c_f_w3", [64, 4096])
    din("c_f_prm", [64, 4])
    din("c_bias_d", [128, 2, 1024])
    din("c_w_out", [D, D])
    din("c_b_out", [128, 8])
    din("hy_delta", [128, 1024])
    S = {k: [] for k in ("QK", "V", "G", "U", "QKV", "Z", "GB", "OF", "OB", "PP", "HV", "HG")}
    S["FILT"], S["D1F"], S["HS"] = {}, {}, {}
    for L in sorted(set(seqs)):
        N1 = 2 * L // 64
        din("hy_f1t_%d" % L, [N1, 2 * N1])
        din("hy_tw_%d" % L, [64, N1 * 192])
        din("hy_twt_%d" % L, [64, N1 * 192])
        din("hy_g2t_%d" % L, [N1, N1])
        din("hy_z_%d" % L, [33, 2 * L])
        din("hy_ntp_%d" % L, [128, 2 * L // 128])
        S["FILT"][L] = _scr(nc, "s_filt%d" % L, [2 * L, 2048], BF16).ap()
        S["D1F"][L] = _scr(nc, "s_d1f%d" % L, [N1, 64, 2, 2048], BF16).ap()
        S["HS"][L] = _scr(nc, "s_hs%d" % L, [N1, 64, 2, 2048], BF16).ap()
    S["D1"] = _scr(nc, "s_d1", [128, 64, 2, 1024], BF16).ap()
    S["D2"] = _scr(nc, "s_d2", [128, 64, 2, 1024], BF16).ap()
    for s, L in enumerate(seqs):
        S["QK"].append(_scr(nc, "s_qk%d" % s, [2048, L], BF16).ap())
        S["V"].append(_scr(nc, "s_v%d" % s, [L, 2048], BF16).ap())
        S["G"].append(_scr(nc, "s_g%d" % s, [L, 2048], BF16).ap())
        S["U"].append(_scr(nc, "s_u%d" % s, [2048, L], BF16).ap())
        S["PP"].append(_scr(nc, "s_pp%d" % s, [L + 2, 3072], F32).ap())
        S["HV"].append([_scr(nc, "s_hv%d_%d" % (s, i), [L, 1024], BF16).ap() for i in range(3)])
        S["HG"].append(_scr(nc, "s_hg%d" % s, [L, 2048], BF16).ap())
        S["QKV"].append(_scr(nc, "s_qkv%d" % s, [3072, L], BF16).ap())
        S["Z"].append(_scr(nc, "s_z%d" % s, [L, 1024], BF16).ap())
        S["GB"].append(_scr(nc, "s_gb%d" % s, [L, 32], F32).ap())
        S["OF"].append(_scr(nc, "s_of%d" % s, [L, 1024], F32).ap())
        S["OB"].append(_scr(nc, "s_ob%d" % s, [L, 1024], F32).ap())
    kb = KB(nc)
    with kb.st:
        cst = load_consts(kb, kb.st, P)
        first = True
        for stg in stages:
            Xin = X if first else Y
            first = False
            if stg[0] == "ffn":
                ffn_stage(kb, cst, P, Xin, Y, seqs, stg[1], stg[2])
            elif stg[0] == "copy":
                copy_stage(kb, Xin, Y, seqs)
            elif stg[0] == "mix":
                kind = stg[1] % 3
                if kind == 1:
                    ret_stage(kb, cst, P, S, Y, seqs, stg[1])
                elif kind == 0:
                    gdn_stage(kb, cst, P, S, Y, seqs, stg[1])
                else:
                    hyena_stage(kb, cst, P, S, Y, seqs, stg[1])
        kb.barrier()
    return nc, kb


def hyena_consts(L):
    N = 2 * L
    N1 = N // 64
    K = N1 // 2
    o = {}
    n1 = np.arange(N1, dtype=np.float64)
    ph = 2 * np.pi * np.outer(n1, n1) / N1
    o["hy_f1t_%d" % L] = np.concatenate([np.cos(ph), -np.sin(ph)], 1).astype(np.float32)
    n2 = np.arange(64, dtype=np.float64)
    k1 = np.arange(N1, dtype=np.float64)
    k2 = np.arange(64, dtype=np.float64)
    th = 2 * np.pi * n2[:, None, None] * (k1[None, :, None] + N1 * k2[None, None, :]) / N
    C, Sm = np.cos(th), -np.sin(th)
    tw = np.stack([C, Sm, -Sm], 2)
    o["hy_tw_%d" % L] = np.ascontiguousarray(tw.reshape(64, N1 * 192).astype(np.float32))
    twt = np.stack([C, Sm, -Sm], 0).transpose(3, 2, 0, 1)
    o["hy_twt_%d" % L] = np.ascontiguousarray(twt.reshape(64, N1 * 192).astype(np.float32))
    ph2 = 2 * np.pi * np.outer(k1, np.arange(K, dtype=np.float64)) / N1
    o["hy_g2t_%d" % L] = np.concatenate([np.cos(ph2) / N, -np.sin(ph2) / N], 1).astype(np.float32)
    n = np.arange(N)
    pos = np.where(n < L, n, N - n).astype(np.float64)
    pos[L] = 0
    t = pos / (L - 1)
    om = 2 * np.pi * pos / L
    f = np.linspace(1e-4, 15.0, 16)
    z = np.concatenate([t[None, :], np.cos(f[:, None] * om[None, :]), -np.sin(f[:, None] * om[None, :])], 0)
    o["hy_z_%d" % L] = np.ascontiguousarray(z.astype(np.float32))
    nt = -t
    nt[L] = -1e4
    o["hy_ntp_%d" % L] = np.ascontiguousarray(nt.reshape(N // 128, 128).T.astype(np.float32))
    return o


def prep_params(inp, seqs=(2048, 4096)):
    out = {}
    for L in sorted(set(seqs)):
        out.update(hyena_consts(L))
    min_decay = math.log(1e-2) / 1.5
    max_decay = math.log(1e-2) / 0.3
    deltas = np.abs(np.linspace(min_decay, max_decay, D))
    out["hy_delta"] = np.ascontiguousarray(np.broadcast_to(deltas.astype(np.float32)[None, :], (128, D)))
    f32 = lambda k: np.asarray(inp[k], np.float32)
    out["c_w_in"] = np.ascontiguousarray(f32("c_w_in")[0])
    out["c_b_in"] = np.ascontiguousarray(np.broadcast_to(f32("c_b_in")[0][None, :], (128, 3072)))
    cc = np.concatenate([f32("c_conv_w")[0], f32("c_conv_b")[0][None, :]], 0)
    out["c_conv"] = np.ascontiguousarray(np.broadcast_to(cc[None], (128, 4, 3072)))
    out["c_f_w1"] = np.ascontiguousarray(f32("c_f_w1")[0])
    out["c_f_w2"] = np.ascontiguousarray(f32("c_f_w2")[0])
    out["c_f_w3"] = np.ascontiguousarray(f32("c_f_w3")[0])
    out["c_f_prm"] = np.ascontiguousarray(np.stack([f32("c_f_b1")[0], f32("c_f_freq1")[0], f32("c_f_b2")[0], f32("c_f_freq2")[0]], 1))
    out["c_bias_d"] = np.ascontiguousarray(np.broadcast_to(f32("c_bias_d")[0][None], (128, 2, 1024)))
    out["c_w_out"] = np.ascontiguousarray(f32("c_w_out")[0])
    out["c_b_out"] = np.ascontiguousarray(f32("c_b_out")[0].reshape(8, 128).T)
    ng = np.asarray(inp["norm_g"], np.float32)
    out["norm_g"] = np.ascontiguousarray(ng.reshape(4, 6, 8, 128).transpose(3, 0, 1, 2).reshape(128, 192))
    out["ffn_w_gate"] = np.ascontiguousarray(np.asarray(inp["ffn_w_gate"], np.float32).reshape(8, D, DFF))
    out["ffn_w_up"] = np.ascontiguousarray(np.asarray(inp["ffn_w_up"], np.float32).reshape(8, D, DFF))
    out["ffn_w_down"] = np.ascontiguousarray(np.asarray(inp["ffn_w_down"], np.float32).reshape(8, DFF, D))
    out["ident"] = np.eye(128, dtype=np.float32)
    inv = (10000.0 ** (-np.arange(0, 256, 2, dtype=np.float64) / 256.0))
    ang = inv[:, None] * np.arange(4096, dtype=np.float64)[None, :]
    rope = np.stack([np.cos(ang), np.sin(ang), np.cos(ang) / 16.0, np.sin(ang) / 16.0], 1)
    out["rope"] = np.ascontiguousarray(rope.astype(np.float32))
    m = np.arange(128)
    out["dmat"] = np.ascontiguousarray((m[None, :] - m[:, None]).astype(np.float32))
    out["tri"] = np.ascontiguousarray(np.stack([(m[None, :] >= m[:, None]), (m[:, None] >= m[None, :])], 1).astype(np.float32))
    out["dvals"] = np.ascontiguousarray(np.broadcast_to(128.0 * np.arange(32, dtype=np.float32)[None, :], (128, 32)))
    le = (m[:, None] <= m[None, :]).astype(np.float32)
    gt = (m[:, None] > m[None, :]).astype(np.float32)
    ge = (m[:, None] >= m[None, :]).astype(np.float32)
    lt = (m[:, None] < m[None, :]).astype(np.float32)
    out["gmask"] = np.ascontiguousarray(np.stack([le, gt, ge, lt], 1))
    out["negm"] = np.ascontiguousarray(np.stack([np.tile(-1e4 * x, (1, 8)) for x in (le, gt, ge, lt)], 1))
    blk = lambda b: (m[:, None] // b) == (m[None, :] // b)
    bms = [blk(16), blk(32) & ~blk(16), blk(64) & ~blk(32), ~blk(64)]
    out["bmask"] = np.ascontiguousarray(np.stack([np.tile(x.astype(np.float32), (1, 8)) for x in bms], 1))
    out["a_w_in"] = np.ascontiguousarray(np.asarray(inp["a_w_in"], np.float32))
    cwa = np.asarray(inp["a_conv_w"], np.float32)
    out["a_conv_w"] = np.ascontiguousarray(cwa.reshape(2, 3, 24, 128).transpose(0, 3, 1, 2).reshape(2, 128, 72))
    out["a_a_log"] = np.ascontiguousarray(np.broadcast_to(np.asarray(inp["a_a_log"], np.float32).reshape(2, 1, 16), (2, 128, 16)))
    out["a_dt_bias"] = np.ascontiguousarray(np.broadcast_to(np.asarray(inp["a_dt_bias"], np.float32).reshape(2, 1, 16), (2, 128, 16)))
    out["a_norm_w"] = np.ascontiguousarray(np.broadcast_to(np.tile(np.asarray(inp["a_norm_w"], np.float32), (1, 8)).reshape(2, 1, 1024), (2, 128, 1024)))
    out["a_w_out"] = np.ascontiguousarray(np.asarray(inp["a_w_out"], np.float32))
    out["b_w_in"] = np.ascontiguousarray(np.asarray(inp["b_w_in"], np.float32)[0])
    out["b_decay_logit"] = np.ascontiguousarray(np.broadcast_to(np.asarray(inp["b_decay_logit"], np.float32)[0].reshape(1, 8), (128, 8)))
    out["b_gn_w"] = np.ascontiguousarray(np.broadcast_to(np.asarray(inp["b_gn_w"], np.float32)[0].reshape(1, 2048), (128, 2048)))
    out["b_w_out"] = np.ascontiguousarray(np.asarray(inp["b_w_out"], np.float32)[0])
    return out


DEBUG_OUT = set()


def _scr(nc, name, shape, dt):
    kind = "ExternalOutput" if name in DEBUG_OUT else "Internal"
    return nc.dram_tensor(name, list(shape), dt, kind=kind)


ALL_STAGES = []
for _l in range(4):
    ALL_STAGES += [("ffn", _l, 0), ("mix", _l), ("ffn", _l, 1)]


def kernel(**inp):
    xp = np.asarray(inp["x_prompt"], np.float32)
    xs = np.asarray(inp["x_sample"], np.float32)
    seqs = [xp.shape[1], xs.shape[1], xs.shape[1]]
    nc, kb = build(seqs, ALL_STAGES)
    pp = prep_params(inp, seqs)
    in_maps = []
    for c in range(N_CORES):
        m = dict(pp)
        m["xin0"] = np.ascontiguousarray(xp[c].T)
        m["xin1"] = np.ascontiguousarray(xs[2 * c].T)
        m["xin2"] = np.ascontiguousarray(xs[2 * c + 1].T)
        in_maps.append(m)
    res = run_bass_kernel_spmd(nc, in_maps, core_ids=list(range(N_CORES)))
    yp = np.stack([res.results[c]["yout0"].T for c in range(N_CORES)], 0)
    ys = np.stack([res.results[c][k].T for c in range(N_CORES) for k in ("yout1", "yout2")], 0)
    return (np.ascontiguousarray(yp, dtype=np.float32), np.ascontiguousarray(ys, dtype=np.float32))
```
